# Optimizing a Trainium2 kernel written in Bass

```python
import math
import jax, jax.numpy as jnp
from jax import lax
import numpy as np

D_MODEL = 1024
BATCH = 2
SEQ = 8192
DEPTH = 1

N_META = 16
RWKV_HEAD_DIM = 64
RWKV_HEADS = D_MODEL // RWKV_HEAD_DIM
RWKV_WIDTH = RWKV_HEADS * RWKV_HEAD_DIM
DECAY_LORA = 64
AAA_LORA = 64
GATE_LORA = 160
RWKV_LN_EPS = 64e-5
DIFF_HEAD_DIM = 64
DIFF_HEADS = D_MODEL // (2 * DIFF_HEAD_DIM)
DIFF_V_DIM = 2 * DIFF_HEAD_DIM
DIFF_QK_WIDTH = 2 * DIFF_HEADS * DIFF_HEAD_DIM
DIFF_V_WIDTH = DIFF_HEADS * DIFF_V_DIM
ROPE_THETA = 500000.0
ROPE_DIM = DIFF_HEAD_DIM // 4
D_FF = 4 * D_MODEL
N_BRANCH = 2
Q_BLOCK = 128
NORM_EPS = 1e-5
SUBLN_EPS = 1e-5

RWKV_COLS = 3 * RWKV_WIDTH + DECAY_LORA + AAA_LORA + GATE_LORA
DIFF_COLS = 2 * DIFF_QK_WIDTH + DIFF_V_WIDTH
GATE_COLS = N_BRANCH * D_MODEL
N_IN = RWKV_COLS + DIFF_COLS + GATE_COLS

kernel_name = "hybrid_rwkv7_diffattn_gated_block"


def rms_norm(x, w, eps=NORM_EPS):
    xf = x.astype(jnp.float32)
    y = xf * lax.rsqrt(jnp.mean(xf * xf, axis=-1, keepdims=True) + eps)
    return (y * w.astype(jnp.float32)).astype(x.dtype)


def token_shift(p):
    return jnp.pad(p, ((0, 0), (1, 0), (0, 0)))[:, :-1]


def rwkv7_mix(p, mu, w0, w2, a0, a2, g2, k_k, k_a, r_k, ln_w, ln_b):
    f32 = jnp.float32
    bsz, seqlen, _ = p.shape
    p = p + (token_shift(p) - p) * mu
    c1 = RWKV_WIDTH
    c2 = 2 * RWKV_WIDTH
    c3 = 3 * RWKV_WIDTH
    c4 = c3 + DECAY_LORA
    c5 = c4 + AAA_LORA
    r, k, v, wd, ad, gd = jnp.split(p, [c1, c2, c3, c4, c5], axis=-1)
    w = -jax.nn.softplus(-(w0 + jnp.tanh(wd) @ w2).astype(f32)) - 0.5
    decay = jnp.exp(-jnp.exp(w))
    a = jax.nn.sigmoid((a0 + ad @ a2).astype(f32))
    g = jax.nn.sigmoid(gd) @ g2

    def heads(t):
        return t.astype(f32).reshape(bsz, seqlen, RWKV_HEADS, RWKV_HEAD_DIM)

    kk = heads(k * k_k)
    kk = kk / jnp.maximum(jnp.sqrt(jnp.sum(kk * kk, axis=-1, keepdims=True)), 1e-12)
    k = k.astype(f32) * (1.0 + (a - 1.0) * k_a.astype(f32))
    r_h, k_h, v_h, a_h, w_h = heads(r), heads(k), heads(v), heads(a), heads(decay)

    def tm(t):
        return jnp.moveaxis(t, 1, 0)

    xs = (tm(r_h), tm(w_h), tm(k_h), tm(v_h), tm(-kk), tm(kk * a_h))

    def step(S, inp):
        rt, wt, kt, vt, at, bt = inp
        sa = jnp.einsum('bhvk,bhk->bhv', S, at)
        S = S * wt[:, :, None, :] + sa[..., None] * bt[:, :, None, :] + vt[..., None] * kt[:, :, None, :]
        y = jnp.einsum('bhvk,bhk->bhv', S, rt)
        return S, y

    S0 = jnp.zeros((bsz, RWKV_HEADS, RWKV_HEAD_DIM, RWKV_HEAD_DIM), f32)
    _, ys = lax.scan(step, S0, xs)
    y = jnp.moveaxis(ys, 0, 1)
    mean = jnp.mean(y, axis=-1, keepdims=True)
    var = jnp.mean(jnp.square(y - mean), axis=-1, keepdims=True)
    y = (y - mean) * lax.rsqrt(var + RWKV_LN_EPS)
    y = y.reshape(bsz, seqlen, RWKV_WIDTH) * ln_w.astype(f32) + ln_b.astype(f32)
    bonus = jnp.sum(r_h * k_h * r_k.astype(f32), axis=-1, keepdims=True) * v_h
    y = y + bonus.reshape(bsz, seqlen, RWKV_WIDTH)
    return (y * g.astype(f32)).astype(p.dtype)


def partial_rope(x, pos):
    half = ROPE_DIM // 2
    inv = ROPE_THETA ** (-jnp.arange(0, ROPE_DIM, 2, dtype=jnp.float32) / ROPE_DIM)
    ang = pos.astype(jnp.float32)[:, None] * inv[None, :]
    cos = jnp.cos(ang)[None, :, None, :]
    sin = jnp.sin(ang)[None, :, None, :]
    x1 = x[..., :half].astype(jnp.float32)
    x2 = x[..., half:ROPE_DIM].astype(jnp.float32)
    rot = jnp.concatenate([x1 * cos - x2 * sin, x2 * cos + x1 * sin], axis=-1).astype(x.dtype)
    return jnp.concatenate([rot, x[..., ROPE_DIM:]], axis=-1)


def diff_attention(p, lq1, lk1, lq2, lk2, subln_w, lambda_init):
    f32 = jnp.float32
    bsz, seqlen, _ = p.shape
    q, k, v = jnp.split(p, [DIFF_QK_WIDTH, 2 * DIFF_QK_WIDTH], axis=-1)
    pos = jnp.arange(seqlen)
    q = partial_rope(q.reshape(bsz, seqlen, 2 * DIFF_HEADS, DIFF_HEAD_DIM), pos)
    k = partial_rope(k.reshape(bsz, seqlen, 2 * DIFF_HEADS, DIFF_HEAD_DIM), pos)
    q = jnp.transpose(q, (0, 2, 1, 3)) * (DIFF_HEAD_DIM ** -0.5)
    k = jnp.transpose(k, (0, 2, 1, 3))
    v = jnp.transpose(v.reshape(bsz, seqlen, DIFF_HEADS, DIFF_V_DIM), (0, 2, 1, 3))
    lam = (jnp.exp(jnp.sum(lq1.astype(f32) * lk1.astype(f32)))
           - jnp.exp(jnp.sum(lq2.astype(f32) * lk2.astype(f32))) + lambda_init)
    neg = jnp.finfo(f32).min

    def attend(qb, qpos):
        nq = qb.shape[2]
        s = jnp.einsum('bhqd,bhkd->bhqk', qb, k, preferred_element_type=f32)
        s = jnp.where(pos[None, :] <= qpos[:, None], s, neg)
        pr = jax.nn.softmax(s, axis=-1).reshape(bsz, DIFF_HEADS, 2, nq, seqlen)
        attn_w = pr[:, :, 0] - lam * pr[:, :, 1]
        return jnp.einsum('bhqk,bhkd->bhqd', attn_w.astype(v.dtype), v)

    meta_out = attend(q[:, :, :N_META], pos[:N_META])
    n_blk = (seqlen - N_META) // Q_BLOCK

    def blk(i):
        start = N_META + i * Q_BLOCK
        qb = lax.dynamic_slice_in_dim(q, start, Q_BLOCK, axis=2)
        return attend(qb, start + jnp.arange(Q_BLOCK))

    outs = lax.map(blk, jnp.arange(n_blk))
    outs = jnp.transpose(outs, (1, 2, 0, 3, 4)).reshape(bsz, DIFF_HEADS, n_blk * Q_BLOCK, DIFF_V_DIM)
    o = jnp.concatenate([meta_out, outs], axis=2)
    o = rms_norm(o, subln_w, SUBLN_EPS) * (1.0 - lambda_init)
    return jnp.transpose(o, (0, 2, 1, 3)).reshape(bsz, seqlen, DIFF_V_WIDTH)


def setup_inputs(seed: int = 0) -> dict:
    key = jax.random.key(seed)
    ks = jax.random.split(key, 32)
    f32 = jnp.float32

    def nrm(k, shape, scale):
        return jax.random.normal(k, shape, f32) * scale

    L = DEPTH
    return {
        "x": nrm(ks[0], (BATCH, SEQ, D_MODEL), 1.0),
        "meta_tokens": nrm(ks[1], (N_META, D_MODEL), 1.0),
        "norm_mix_w": 1.0 + nrm(ks[2], (L, D_MODEL), 0.02),
        "w_in": nrm(ks[3], (L, D_MODEL, N_IN), D_MODEL ** -0.5),
        "rwkv_mu": jax.random.uniform(ks[4], (L, RWKV_COLS), f32, 0.0, 1.0),
        "rwkv_w0": jax.random.uniform(ks[5], (L, RWKV_WIDTH), f32, -6.5, -1.5),
        "rwkv_w2": nrm(ks[6], (L, DECAY_LORA, RWKV_WIDTH), 0.1 * DECAY_LORA ** -0.5),
        "rwkv_a0": nrm(ks[7], (L, RWKV_WIDTH), 0.01),
        "rwkv_a2": nrm(ks[8], (L, AAA_LORA, RWKV_WIDTH), 0.5 * AAA_LORA ** -0.5),
        "rwkv_g2": nrm(ks[9], (L, GATE_LORA, RWKV_WIDTH), GATE_LORA ** -0.5),
        "rwkv_k_k": 0.85 + nrm(ks[10], (L, RWKV_WIDTH), 0.02),
        "rwkv_k_a": 1.0 + nrm(ks[11], (L, RWKV_WIDTH), 0.02),
        "rwkv_r_k": nrm(ks[12], (L, RWKV_HEADS, RWKV_HEAD_DIM), 0.1),
        "rwkv_ln_w": 1.0 + nrm(ks[13], (L, RWKV_WIDTH), 0.02),
        "rwkv_ln_b": nrm(ks[14], (L, RWKV_WIDTH), 0.01),
        "rwkv_w_o": nrm(ks[15], (L, RWKV_WIDTH, D_MODEL), RWKV_WIDTH ** -0.5),
        "diff_lq1": nrm(ks[16], (L, DIFF_HEAD_DIM), 0.1),
        "diff_lk1": nrm(ks[17], (L, DIFF_HEAD_DIM), 0.1),
        "diff_lq2": nrm(ks[18], (L, DIFF_HEAD_DIM), 0.1),
        "diff_lk2": nrm(ks[19], (L, DIFF_HEAD_DIM), 0.1),
        "diff_subln_w": 1.0 + nrm(ks[20], (L, DIFF_V_DIM), 0.02),
        "diff_w_o": nrm(ks[21], (L, DIFF_V_WIDTH, D_MODEL), DIFF_V_WIDTH ** -0.5),
        "w_out": nrm(ks[22], (L, D_MODEL, D_MODEL), D_MODEL ** -0.5),
        "norm_mlp_w": 1.0 + nrm(ks[23], (L, D_MODEL), 0.02),
        "mlp_w1": nrm(ks[24], (L, D_MODEL, D_FF), D_MODEL ** -0.5),
        "mlp_w2": nrm(ks[25], (L, D_FF, D_MODEL), D_FF ** -0.5),
        "final_norm_w": 1.0 + nrm(ks[26], (D_MODEL,), 0.02),
    }


def reference(x, meta_tokens, norm_mix_w, w_in, rwkv_mu, rwkv_w0, rwkv_w2, rwkv_a0, rwkv_a2, rwkv_g2,
              rwkv_k_k, rwkv_k_a, rwkv_r_k, rwkv_ln_w, rwkv_ln_b, rwkv_w_o, diff_lq1, diff_lk1, diff_lq2,
              diff_lk2, diff_subln_w, diff_w_o, w_out, norm_mlp_w, mlp_w1, mlp_w2, final_norm_w):
    bsz = x.shape[0]
    meta = jnp.broadcast_to(meta_tokens[None].astype(x.dtype), (bsz, N_META, D_MODEL))
    h_res = jnp.concatenate([meta, x], axis=1)
    seqlen = h_res.shape[1]
    for layer in range(DEPTH):
        lambda_init = 0.8 - 0.6 * math.exp(-0.3 * layer)
        h = rms_norm(h_res, norm_mix_w[layer])
        proj = h @ w_in[layer]
        p_rwkv, p_diff, p_gate = jnp.split(proj, [RWKV_COLS, RWKV_COLS + DIFF_COLS], axis=-1)
        o_rwkv = rwkv7_mix(p_rwkv, rwkv_mu[layer], rwkv_w0[layer], rwkv_w2[layer], rwkv_a0[layer],
                           rwkv_a2[layer], rwkv_g2[layer], rwkv_k_k[layer], rwkv_k_a[layer],
                           rwkv_r_k[layer], rwkv_ln_w[layer], rwkv_ln_b[layer]) @ rwkv_w_o[layer]
        o_diff = diff_attention(p_diff, diff_lq1[layer], diff_lk1[layer], diff_lq2[layer],
                                diff_lk2[layer], diff_subln_w[layer], lambda_init) @ diff_w_o[layer]
        gates = jax.nn.sigmoid(p_gate).reshape(bsz, seqlen, N_BRANCH, D_MODEL)
        merged = gates[:, :, 0] * o_rwkv + gates[:, :, 1] * o_diff
        h_res = h_res + merged @ w_out[layer]
        h = rms_norm(h_res, norm_mlp_w[layer])
        h_res = h_res + jnp.square(jax.nn.relu(h @ mlp_w1[layer])) @ mlp_w2[layer]
    out = rms_norm(h_res, final_norm_w)
    return out[:, N_META:]
```

```python
import math
import threading
from contextlib import ExitStack
import numpy as np
import ml_dtypes
import concourse.bass as bass
import concourse.mybir as mybir
from concourse.bass_utils import run_bass_kernel_spmd

F32 = mybir.dt.float32
BF16 = mybir.dt.bfloat16
AF = mybir.ActivationFunctionType
ALU = mybir.AluOpType
AX = mybir.AxisListType

D = 1024
NT = 65
LP = NT * 128
C0 = math.exp(-0.5)
LAMBDA_INIT = 0.8 - 0.6 * math.exp(0.0)


class Tk:
    __slots__ = ("lw", "rd", "name", "psum")

    def __init__(self, name=""):
        self.lw = None
        self.rd = {}
        self.name = name
        self.psum = False


class View:
    __slots__ = ("ap", "tk")

    def __init__(self, ap, tk):
        self.ap = ap
        self.tk = tk


class Buf:
    def __init__(self, t, name, tk=None, dt=None):
        self.t = t
        self.name = name
        self.tk = tk if tk is not None else Tk(name)
        self.dt = dt

    def __getitem__(self, idx):
        ap = self.t[idx] if self.dt is None else self.t.ap().bitcast(self.dt)[idx]
        return View(ap, self.tk)

    def alias(self, name, dt=None):
        return Buf(self.t, name, dt=dt if dt is not None else self.dt)

    def same(self, dt):
        return Buf(self.t, self.name, tk=self.tk, dt=dt)


class EngS:
    def __init__(self, name, eng, sem):
        self.name = name
        self.eng = eng
        self.sem = sem
        self.count = 0
        self.waited = {}


class K:
    def __init__(self, nc, es):
        self.nc = nc
        self.es = es
        self.es_sem = es
        self.engs = {}
        for n, e in (("pe", nc.tensor), ("act", nc.scalar), ("dve", nc.vector), ("pool", nc.gpsimd), ("sp", nc.sync)):
            self.engs[n] = EngS(n, e, es.enter_context(nc.semaphore("sem_" + n)))
        self.nd = 0
        self.nsb = 0
        self.nops = 0
        self.limit = 10 ** 9
        self.weave = Weave()

    def sb(self, shape, dt, name=None):
        self.nsb += 1
        name = name or f"sb{self.nsb}"
        return Buf(self.es.enter_context(self.nc.sbuf_tensor(name, list(shape), dt)), name)

    def dsem(self, name=None):
        self.nd += 1
        name = name or f"dsem{self.nd}"
        e = EngS(name, None, self.es_sem.enter_context(self.nc.semaphore(name)))
        self.engs[name] = e
        return e

    def _wait(self, E, reads, writes):
        deps = {}
        for t in reads:
            if t.lw is not None and deps.get(t.lw[0], 0) < t.lw[1]:
                deps[t.lw[0]] = t.lw[1]
            if t.psum:
                for n, c in t.rd.items():
                    if n != E.name and deps.get(n, 0) < c:
                        deps[n] = c
        for t in writes:
            if t.lw is not None and deps.get(t.lw[0], 0) < t.lw[1]:
                deps[t.lw[0]] = t.lw[1]
            for n, c in t.rd.items():
                if deps.get(n, 0) < c:
                    deps[n] = c
        for n, c in deps.items():
            if n == E.name and n == "pe":
                continue
            if E.waited.get(n, 0) >= c:
                continue
            E.eng.wait_ge(self.engs[n].sem, c)
            E.waited[n] = c

    def op(self, en, fn, reads, writes):
        self.weave.checkpoint()
        self.nops += 1
        if self.nops > self.limit:
            return
        E = self.engs[en]
        self._wait(E, reads, writes)
        ins = fn(E.eng)
        E.count += 1
        ins.then_inc(E.sem, 1)
        for t in reads:
            t.rd[en] = E.count
        for t in writes:
            t.lw = (en, E.count)
            t.rd = {}

    def dma(self, qn, out, in_, dsem, **kw):
        self.weave.checkpoint()
        self.nops += 1
        if self.nops > self.limit:
            return
        Q = self.engs[qn]
        self._wait(Q, [in_.tk], [out.tk])
        ins = Q.eng.dma_start(out=out.ap, in_=in_.ap, **kw)
        dsem.count += 16
        ins.then_inc(dsem.sem, 16)
        in_.tk.rd[dsem.name] = dsem.count
        out.tk.lw = (dsem.name, dsem.count)
        out.tk.rd = {}

    def barrier(self):
        for en in ("sp", "pool", "act", "dve", "pe"):
            E = self.engs[en]
            for n, o in self.engs.items():
                if n == en or o.count == 0:
                    continue
                if E.waited.get(n, 0) < o.count:
                    E.eng.wait_ge(o.sem, o.count)
                    E.waited[n] = o.count

    def tt(self, en, out, a, b, op):
        self.op(en, lambda e: e.tensor_tensor(out=out.ap, in0=a.ap, in1=b.ap, op=op), [a.tk, b.tk], [out.tk])

    def ts(self, en, out, a, s1, op0, s2=None, op1=None):
        rd = [a.tk]
        v1 = s1
        v2 = s2
        if isinstance(s1, View):
            rd.append(s1.tk)
            v1 = s1.ap
        if isinstance(s2, View):
            rd.append(s2.tk)
            v2 = s2.ap
        kw = {}
        if en == "pool" and op1 is None and s2 is None and op0 == ALU.mult:
            op1 = ALU.add
            v2 = 0.0
        if op1 is not None:
            kw["op1"] = op1
        self.op(en, lambda e: e.tensor_scalar(out=out.ap, in0=a.ap, scalar1=v1, scalar2=v2, op0=op0, **kw), rd, [out.tk])

    def stt(self, out, a, s, b, op0, op1):
        rd = [a.tk, b.tk]
        v = s
        if isinstance(s, View):
            rd.append(s.tk)
            v = s.ap
        self.op("dve", lambda e: e.scalar_tensor_tensor(out=out.ap, in0=a.ap, scalar=v, in1=b.ap, op0=op0, op1=op1), rd, [out.tk])

    def cp(self, en, out, a):
        if en == "act":
            self.op(en, lambda e: e.copy(out=out.ap, in_=a.ap), [a.tk], [out.tk])
        else:
            self.op(en, lambda e: e.tensor_copy(out=out.ap, in_=a.ap), [a.tk], [out.tk])

    def act(self, out, a, func, bias=None, scale=1.0, accum=None):
        rd = [a.tk]
        wr = [out.tk]
        kw = {}
        if isinstance(bias, View):
            rd.append(bias.tk)
            kw["bias"] = bias.ap
        elif bias is not None:
            kw["bias"] = bias
        if isinstance(scale, View):
            rd.append(scale.tk)
            kw["scale"] = scale.ap
        else:
            kw["scale"] = scale
        if accum is not None:
            wr.append(accum.tk)
            kw["accum_out"] = accum.ap
        self.op("act", lambda e: e.activation(out=out.ap, in_=a.ap, func=func, **kw), rd, wr)

    def memset(self, en, out, val):
        self.op(en, lambda e: e.memset(out.ap, val), [], [out.tk])

    def red(self, out, a, op=ALU.add):
        self.op("dve", lambda e: e.tensor_reduce(out=out.ap, in_=a.ap, op=op, axis=AX.X), [a.tk], [out.tk])

    def recip(self, out, a):
        self.op("dve", lambda e: e.reciprocal(out=out.ap, in_=a.ap), [a.tk], [out.tk])

    def pe(self, items):
        rd = []
        wr = []
        for it in items:
            if it[0] == "T":
                wr.append(it[1].tk)
                rd += [it[2].tk, it[3].tk]
            else:
                wr.append(it[0].tk)
                rd += [it[1].tk, it[2].tk]

        def fn(e):
            ins = None
            for it in items:
                if it[0] == "T":
                    ins = e.transpose(it[1].ap, it[2].ap, it[3].ap)
                else:
                    ins = e.matmul(it[0].ap, lhsT=it[1].ap, rhs=it[2].ap, start=it[3], stop=it[4])
            return ins

        self.op("pe", fn, rd, wr)


class Weave:
    def __init__(self):
        self.cv = threading.Condition()
        self.active = None
        self.tl = threading.local()

    def current(self):
        return getattr(self.tl, "sid", 0) if self.active is not None else None

    def _pick(self):
        best = None
        for i in range(len(self.alive)):
            if self.alive[i]:
                f = self.done[i] / self.est[i]
                if best is None or f < best[0]:
                    best = (f, i)
        self.active = best[1] if best is not None else -1

    def checkpoint(self):
        if self.active is None:
            return
        sid = self.tl.sid
        with self.cv:
            self.done[sid] += 1
            self._pick()
            self.cv.notify_all()
            while self.active != sid:
                self.cv.wait()

    def run(self, fns, ests):
        n = len(fns)
        self.done = [0] * n
        self.est = [max(1, e) for e in ests]
        self.alive = [True] * n
        self.err = []

        def body(i):
            self.tl.sid = i
            with self.cv:
                while self.active != i:
                    self.cv.wait()
            try:
                fns[i]()
            except BaseException as ex:
                self.err.append(ex)
            with self.cv:
                self.alive[i] = False
                self._pick()
                self.cv.notify_all()

        ths = [threading.Thread(target=body, args=(i,)) for i in range(n)]
        with self.cv:
            self.active = 0
        for t in ths:
            t.start()
        for t in ths:
            t.join()
        self.active = None
        if self.err:
            raise self.err[0]
        return list(self.done)


class RotSel:
    def __init__(self, weave, pools):
        self.weave = weave
        self.pools = pools

    def next(self):
        c = self.weave.current()
        return self.pools[-1 if c is None else c].next()


class Rot:
    def __init__(self, bufs):
        self.bufs = bufs
        self.i = 0

    def next(self):
        b = self.bufs[self.i % len(self.bufs)]
        self.i += 1
        return b


def build(cfg):
    NT1 = cfg.get("nt1", NT)
    do_p2 = cfg.get("p2", True)
    do_rs = cfg.get("rs", True)
    dbg = cfg.get("dbg", False)
    nc = bass.Bass("TRN2", target_bir_lowering=False)
    es = ExitStack()
    k = K(nc, es)
    k.limit = cfg.get("limit", 10 ** 9)

    def din(name, shape, dt=F32):
        return Buf(nc.dram_tensor(name, list(shape), dt, kind="ExternalInput"), name)

    def dout(name, shape, dt=F32):
        return Buf(nc.dram_tensor(name, list(shape), dt, kind="ExternalOutput"), name)

    xb = din("xb", [8192, D])
    meta = din("meta", [16, D])
    x2 = din("x2", [4, 512, D])
    w1c = din("w1c", [D, 1824])
    wg = din("wg", [D, 2048])
    pv = din("pv", [128, 32])
    nwb = din("nwb", [128, D])
    lnwb = din("lnwb", [128, 2, 256])
    sublnb = din("sublnb", [128, 128])
    lamv = din("lamv", [128, 4, 64])
    w2a2 = din("w2a2", [128, 256])
    g2 = din("g2", [160, 256])
    wor = din("wor", [256, D])
    wod = din("wod", [256, D])
    wout = din("wout", [D, D])
    mw1 = din("mw1", [D, 4096])
    mw2 = din("mw2", [4096, D])
    nm2 = din("nm2", [128, 2, D])
    rs_dbg = [din(f"rs_dbg{i}", [512, 2048], BF16) for i in range(4)] if cfg.get("p2only") else None
    rope = din("rope", [128, NT, 2, 128])
    cst = din("cst", [128, 1024])
    cst2 = din("cst2", [128, 1024])
    yout = dout("yout", [4, 512, D])
    rs_in = [Buf(nc.dram_tensor(f"rs_in{i}", [2048, 2048], BF16), f"rs_in{i}") for i in range(4)]
    rs_tk = [[Tk(f"rs{i}_{r}") for r in range(16)] for i in range(4)]
    rs_out = [Buf(nc.dram_tensor(f"rs_out{i}", [512, 2048], BF16), f"rs_out{i}") for i in range(4)]
    if dbg:
        d_mix = dout("d_mix", [NT1 * 128, 512])

    psb = [Buf(es.enter_context(nc.psum_tensor(f"ps{i}", [128, 512], F32)), f"ps{i}") for i in range(8)]
    for pb_ in psb:
        pb_.tk.psum = True
    ps_rot = RotSel(k.weave, [Rot(psb[0:3]), Rot(psb[3:6]), Rot(psb[6:7]), Rot(psb[0:7])])
    ps_o = [psb[7], psb[7]]

    identb = k.sb([128, 128], BF16, "identb")
    k.dma("pool", identb[:], cst[:, 0:128], k.dsem())
    es1 = ExitStack()
    k.es = es1
    cs = k.sb([128, 384], F32, "cs")
    sem_c = k.dsem("sem_c")
    k.dma("sp", cs[:, 0:256], cst[:, 256:512], sem_c)
    k.dma("sp", cs[:, 256:384], cst[:, 896:1024], sem_c)

    cmaskb = k.sb([128, 128], BF16, "cmaskb")
    k.dma("pool", cmaskb[:], cst[:, 128:256], k.dsem())
    mkb = k.sb([128, 1024], BF16, "mkb")
    k.dma("pool", mkb[:], cst2[:, :], k.dsem())
    bones = k.sb([128, 2], BF16, "bones")
    k.cp("dve", bones[:], cs[:, 0:128:64])
    pvt = k.sb([128, 32], F32, "pvt")
    k.dma("sp", pvt[:], pv[:, :], k.dsem())
    epsb = k.sb([128, 2], F32, "epsb")
    k.memset("dve", epsb[:, 0:1], 1e-5)
    k.memset("dve", epsb[:, 1:2], 64e-5)
    negb = k.sb([128, 4], F32, "negb")
    k.ts("dve", negb[:], pvt[:, 9:13], -1.0, ALU.mult)
    omka = k.sb([128, 2], F32, "omka")
    k.ts("dve", omka[:], pvt[:, 15:17], -1.0, ALU.mult, 1.0, ALU.add)
    nw_t = k.sb([128, D], BF16, "nw_t")
    k.dma("pool", nw_t[:], nwb[:, :], k.dsem())
    lnw_t = k.sb([128, 2, 256], F32, "lnw_t")
    k.dma("sp", lnw_t[:], lnwb[:, :, :], k.dsem())
    subln_t = k.sb([128, 128], F32, "subln_t")
    k.dma("sp", subln_t[:], sublnb[:, :], k.dsem())
    k.ts("dve", subln_t[:], subln_t[:], 1.0 - LAMBDA_INIT, ALU.mult)
    lam_in = k.sb([128, 4, 64], F32, "lam_in")
    k.dma("sp", lam_in[:], lamv[:, :, :], k.dsem())
    lam_t = k.sb([128, 4], F32, "lam_t")
    lam_j = k.sb([128, 64], F32, "lam_j")
    for i in range(2):
        k.tt("dve", lam_j[:], lam_in[:, 2 * i, :], lam_in[:, 2 * i + 1, :], ALU.mult)
        k.red(lam_t[:, i:i + 1], lam_j[:])
    k.act(lam_t[:, 0:2], lam_t[:, 0:2], AF.Exp)
    k.tt("dve", lam_t[:, 2:3], lam_t[:, 0:1], lam_t[:, 1:2], ALU.subtract)
    k.ts("dve", lam_t[:, 3:4], lam_t[:, 2:3], LAMBDA_INIT, ALU.add, -1.0, ALU.mult)
    neglam = lam_t[:, 3:4]

    mub = k.sb([128, 9, 128], F32, "mub")
    for c in range(9):
        k.ts("pool", mub[:, c, :], cs[:, 256:384], 0.0, ALU.mult, pvt[:, c:c + 1], ALU.add)

    w1b = k.sb([128, 8, 1824], BF16, "w1b")
    sem_w = k.dsem("sem_w1")
    for kc in range(8):
        k.dma("pool", w1b[:, kc, :], w1c[kc * 128:(kc + 1) * 128, :], sem_w)
    w2p = k.sb([128, 2, 256], BF16, "w2p")
    k.memset("pool", w2p[:], 0.0)
    sem_w2 = k.dsem()
    k.dma("pool", View(w2p.t[0:64, 0, :], w2p.tk), w2a2[0:64, :], sem_w2)
    k.dma("pool", View(w2p.t[64:128, 1, :], w2p.tk), w2a2[64:128, :], sem_w2)
    g2b = k.sb([128, 2, 256], BF16, "g2b")
    k.memset("pool", g2b[:], 0.0)
    sem_g2 = k.dsem()
    k.dma("pool", g2b[:, 0, :], g2[0:128, :], sem_g2)
    k.dma("pool", View(g2b.t[0:32, 1, :], g2b.tk), g2[128:160, :], sem_g2)
    wob = k.sb([128, 4, D], BF16, "wob")
    sem_wo = k.dsem()
    for i in range(2):
        k.dma("pool", wob[:, i, :], wor[i * 128:(i + 1) * 128, :], sem_wo)
        k.dma("pool", wob[:, 2 + i, :], wod[i * 128:(i + 1) * 128, :], sem_wo)

    KT = k.sb([128, 2, LP], BF16, "KT")
    KTr = [KT.alias(f"KT{j}") for j in range(NT)]
    VA = k.sb([128, NT, 2, 130], BF16, "VA")
    VAr = [VA.alias(f"VA{j}") for j in range(NT)]
    k.op("pool", lambda e: e.memset(VA.t[:, :, :, 128:130], 1.0), [], [VA.tk] + [r.tk for r in VAr])
    k.op("pool", lambda e: e.memset(VA.t[0:112, 0, :, 128:130], 0.0), [], [VA.tk, VAr[0].tk])
    raw = k.sb([128, 9, 129], F32, "raw")
    k.memset("pool", raw[:], 0.0)
    H32 = k.sb([128, 2, 64], F32, "H32")
    k.memset("dve", H32[:], 0.0)
    Hb = k.sb([128, 2, 64], BF16, "Hb")
    k.memset("dve", Hb[:], 0.0)

    def rot(n, shape, dt, name):
        return Rot([k.sb(shape, dt, f"{name}{i}") for i in range(n)])

    xt_r = Rot([(k.sb([128, D], F32, f"xt{i}"), k.dsem(f"sem_xt{i}")) for i in range(1)])
    rp_r = Rot([(k.sb([128, 2, 128], F32, f"rp{i}"), k.dsem(f"sem_rp{i}")) for i in range(2)])
    junk = k.sb([128, 128], F32, "junk")
    st_r = rot(2, [128, 8], F32, "st")
    xs_r = rot(1, [128, D], BF16, "xs")
    hT_r = rot(1, [128, 8, 128], BF16, "hT")
    sh_r = rot(2, [128, 9, 128], F32, "sh")
    qkraw_r = rot(1, [128, 4, 128], F32, "qkraw")
    QT_r = rot(2, [128, 2, 2, 128], BF16, "QT")
    for qb in QT_r.bufs:
        k.memset("pool", qb[:], 0.0)
    ropt = k.sb([128, 4, 128], F32, "ropt")
    PT_r = rot(3, [128, 4, 128], BF16, "PT")
    on_r = rot(1, [128, 4, 128], F32, "on")
    rcp_r = rot(2, [128, 4], F32, "rcp")
    dsc = k.sb([128, 256], F32, "dsc")
    mix_r = rot(2, [128, 512], BF16, "mix")
    mixT_r = rot(2, [128, 4, 128], BF16, "mixT")
    po_r = Rot([(k.sb([128, 2048], BF16, f"po{i}"), k.dsem(f"sem_po{i}")) for i in range(1)])
    if dbg:
        dmix_r = Rot([(k.sb([128, 512], F32, f"dmix{i}"), k.dsem(f"sem_dmix{i}")) for i in range(1)])

    def f32t(name, shape=(128, 2, 128)):
        return k.sb(list(shape), F32, name)

    lorab = k.sb([128, 128], BF16, "lorab")
    lact = k.sb([128, 3, 128], F32, "lact")
    k.memset("pool", lact[:], 0.0)
    sgd = k.sb([128, 2, 128], BF16, "sgd")
    k.memset("pool", sgd[:], 0.0)
    sw = f32t("sw")
    cum = f32t("cum")
    aa = f32t("aa")
    Ew = f32t("Ew")
    Ewx = f32t("Ewx")
    Einv = f32t("Einv")
    Eend = f32t("Eend")
    bC = k.sb([128, 2, 2], F32, "bC")
    WC = k.sb([128, 2, 2], F32, "WC")
    kk = f32t("kk")
    kq = f32t("kq")
    rn = f32t("rn")
    k2 = f32t("k2")
    bvec = f32t("bvec")
    rk = k.sb([128, 2, 128], BF16, "rk")
    BK = k.sb([128, 2, 2, 128], BF16, "BK")
    RT = k.sb([128, 2, 128], BF16, "RT")
    ARbd = k.sb([128, 2, 2, 2, 128], BF16, "ARbd")
    Bbd = k.sb([128, 2, 2, 128], BF16, "Bbd")
    TMbd = k.sb([128, 2, 2, 2, 128], BF16, "TMbd")
    PTbd = k.sb([128, 2, 2, 128], BF16, "PTbd")
    Hbd = k.sb([128, 2, 2, 64], BF16, "Hbd")
    for zb in (ARbd, Bbd, TMbd, PTbd, Hbd):
        k.memset("pool", zb[:], 0.0)
    FT = k.sb([128, 4, 2, 128], BF16, "FT")
    TM = k.sb([128, 4, 2, 128], BF16, "TM")
    S4 = k.sb([128, 4, 4, 128], BF16, "S4")
    ULr = rot(2, [128, 4, 2, 128], BF16, "UL")
    Yr = rot(2, [128, 4, 128], BF16, "Y")
    GTs = k.sb([128, 2, 2, 64], BF16, "GTs")
    yv = k.sb([128, 4, 64], F32, "yv")
    gst = k.sb([128, 16], F32, "gst")
    bsum = k.sb([128, 4], F32, "bsum")
    gtm = k.sb([128, 256], F32, "gtm")
    ytmp = k.sb([128, 4, 64], F32, "ytmp")

    heads = [(c2, e) for c2 in range(2) for e in range(2)]

    def p1_tile(j):
        xt, sx = xt_r.next()
        if j == 0:
            k.memset("pool", xt[:], 0.0)
            k.dma("sp", View(xt.t[112:128, :], xt.tk), meta[:, :], sx)
        else:
            k.dma("sp", xt[:], xb[(j - 1) * 128:j * 128, :], sx)
        rp, srp = rp_r.next()
        k.dma("sp", rp[:], rope[:, j, :, :], srp)
        st = st_r.next()
        xs = xs_r.next()
        k.act(xs[:], xt[:], AF.Square, accum=st[:, 0:1])
        k.act(st[:, 1:2], st[:, 0:1], AF.Ln, bias=epsb[:, 0:1], scale=1.0 / D)
        k.act(st[:, 2:3], st[:, 1:2], AF.Exp, scale=-0.5)
        k.stt(xs[:], xt[:], st[:, 2:3], nw_t[:], ALU.mult, ALU.mult)
        ps = ps_rot.next()
        psb16 = ps.same(BF16)
        k.pe([("T", psb16[:, kc * 128:(kc + 1) * 128], xs[:, kc * 128:(kc + 1) * 128], identb[:]) for kc in range(8)])
        hT = hT_r.next()
        k.cp("act", hT[:], View(psb16.t.ap().bitcast(BF16)[:, 0:1024].rearrange("p (a b) -> p a b", a=8), ps.tk))
        def proj(ps, slot, col0, width=128, rows=128):
            return [(View(ps.t[0:rows, slot * 128:slot * 128 + 128], ps.tk) if rows != 128 else ps[:, slot * 128:slot * 128 + 128],
                     w1b[:, kc, col0:col0 + width], hT[:, kc, :], kc == 0, kc == 7) for kc in range(8)]
        psA = ps_rot.next()
        items = []
        for s in range(4):
            items += proj(psA, s, s * 128)
        k.pe(items)
        k.cp("act", raw[:, 0:4, 1:129], View(psA.t[:, :].rearrange("p (a b) -> p a b", a=4), psA.tk))
        psB = ps_rot.next()
        items = []
        for s in range(4):
            items += proj(psB, s, 512 + s * 128)
        k.pe(items)
        k.cp("act", raw[:, 4:8, 1:129], View(psB.t[:, :].rearrange("p (a b) -> p a b", a=4), psB.tk))
        psC = ps_rot.next()
        items = []
        for s in range(4):
            items += proj(psC, s, 1056 + s * 128)
        k.pe(items)
        qkraw = qkraw_r.next()
        k.cp("dve", qkraw[:], View(psC.t[:, :].rearrange("p (a b) -> p a b", a=4), psC.tk))
        psD = ps_rot.next()
        items = proj(psD, 0, 1024, width=32, rows=32)
        items += [(psD[:, 128:384], hT[:, kc, :], w1b[:, kc, 1568:1824], kc == 0, kc == 7) for kc in range(8)]
        k.pe(items)
        k.cp("act", View(raw.t[0:32, 8, 1:129], raw.tk), View(psD.t[0:32, 0:128], psD.tk))
        k.cp(cfg.get("vaeng", "act"), View(VA.t[:, j, :, 0:128], VAr[j].tk), View(psD.t[:, 128:384].rearrange("p (a b) -> p a b", a=2), psD.tk))
        if j == 0:
            k.memset("pool", View(VA.t[0:112, 0, :, 0:128], VAr[0].tk), 0.0)
        sh = sh_r.next()
        k.tt("pool", sh[:], raw[:, :, 0:128], raw[:, :, 1:129], ALU.subtract)
        k.tt("pool", sh[:], sh[:], mub[:], ALU.mult)
        k.tt("pool", sh[:], sh[:], raw[:, :, 1:129], ALU.add)
        k.cp("pool", raw[:, :, 0:1], raw[:, :, 128:129])
        psR = ps_rot.next()
        k.pe([(psR[:, s * 128:(s + 1) * 128], cs[:, 128:256], qkraw[:, s, :], True, True) for s in range(4)])
        for s in range(4):
            k.tt("dve", ropt[:, s, :], psR[:, s * 128:(s + 1) * 128], rp[:, 1, :], ALU.mult)
            k.tt("pool", qkraw[:, s, :], qkraw[:, s, :], rp[:, 0, :], ALU.mult)
        QT = QT_r.next()
        for e in range(2):
            pb = 64 * e
            k.tt("pool", View(QT.t[pb:pb + 64, :, e, :], QT.tk), View(qkraw.t[pb:pb + 64, 0:2, :], qkraw.tk),
                 View(ropt.t[pb:pb + 64, 0:2, :], ropt.tk), ALU.add)
        k.tt("pool", View(KT.t[:, :, j * 128:(j + 1) * 128], KTr[j].tk), qkraw[:, 2:4, :], ropt[:, 2:4, :], ALU.add)
        return sh, QT

    def attn_tile(j, QT, mix):
        on = on_r.next()
        rcp = rcp_r.next()
        groups = [(h, i0) for h in range(4) for i0 in range(0, j + 1, 4)]
        pss = {}

        def qk(g):
            h, i0 = groups[g]
            dh, e = h // 2, h % 2
            n = min(4, j + 1 - i0)
            ps = ps_rot.next()
            k.pe([(ps[:, s * 128:(s + 1) * 128],
                   View(KT.t[:, dh, (i0 + s) * 128:(i0 + s + 1) * 128], KTr[i0 + s].tk),
                   QT[:, dh, e, :], True, True) for s in range(n)])
            pss[g] = ps

        for g in range(min(2, len(groups))):
            qk(g)
        for g, (h, i0) in enumerate(groups):
            dh, e = h // 2, h % 2
            po = ps_o[h % 2]
            n = min(4, j + 1 - i0)
            ps = pss.pop(g)
            PT = PT_r.next()
            k.act(View(PT.t[:, 0:n, :], PT.tk), View(ps.t[:, 0:n * 128].rearrange("p (a b) -> p a b", a=n), ps.tk), AF.Exp, scale=0.125)
            if i0 + n - 1 == j:
                k.tt("pool", PT[:, n - 1, :], PT[:, n - 1, :], cmaskb[:], ALU.mult)
            if g + 2 < len(groups):
                qk(g + 2)
            k.pe([(po[:, 0:130], PT[:, s, :], View(VA.t[:, i0 + s, dh, :], VAr[i0 + s].tk), (i0 + s) == 0, (i0 + s) == j)
                  for s in range(n)])
            if i0 + n - 1 == j:
                k.recip(rcp[:, h:h + 1], po[:, 128:129])
                k.ts("dve", on[:, h, :], po[:, 0:128], rcp[:, h:h + 1], ALU.mult)
        for dh in range(2):
            o = dsc[:, dh * 128:(dh + 1) * 128]
            k.stt(o, on[:, 2 * dh + 1, :], neglam, on[:, 2 * dh, :], ALU.mult, ALU.add)
            k.act(junk[:], o, AF.Square, accum=rcp[:, dh:dh + 1])
            k.act(rcp[:, dh:dh + 1], rcp[:, dh:dh + 1], AF.Ln, bias=epsb[:, 0:1], scale=1.0 / 128)
            k.act(rcp[:, 2 + dh:3 + dh], rcp[:, dh:dh + 1], AF.Exp, scale=-0.5)
            k.stt(mix[:, 256 + dh * 128:256 + (dh + 1) * 128], o, rcp[:, 2 + dh:3 + dh], subln_t[:], ALU.mult, ALU.mult)

    def rwkv_tile(j, sh, mix):
        r = sh[:, 0:2, :]
        kx = sh[:, 2:4, :]
        v = sh[:, 4:6, :]
        k.act(View(lact.t[0:64, 0, :], lact.tk), View(sh.t[0:64, 6, :], sh.tk), AF.Exp, scale=2.0)
        k.act(lact[:, 1, :], sh[:, 7, :], AF.Exp, scale=-1.0)
        k.act(View(lact.t[0:32, 2, :], lact.tk), View(sh.t[0:32, 8, :], sh.tk), AF.Exp, scale=-1.0)
        k.ts("pool", lact[:], lact[:], 1.0, ALU.add)
        k.recip(lact[:], lact[:])
        k.ts("pool", View(lorab.t[0:64, :], lorab.tk), View(lact.t[0:64, 0, :], lact.tk), -2.0, ALU.mult, 1.0, ALU.add)
        k.cp("act", View(lorab.t[64:128, :], lorab.tk), View(sh.t[64:128, 6, :], sh.tk))
        k.cp("act", sgd[:, 0, :], lact[:, 1, :])
        k.cp("act", View(sgd.t[0:32, 1, :], sgd.tk), View(lact.t[0:32, 2, :], lact.tk))
        psZ = ps_rot.next()
        items = []
        for c2 in range(2):
            items.append((psZ[:, c2 * 128:(c2 + 1) * 128], w2p[:, 0, c2 * 128:(c2 + 1) * 128], lorab[:], True, True))
            items.append((psZ[:, 256 + c2 * 128:256 + (c2 + 1) * 128], w2p[:, 1, c2 * 128:(c2 + 1) * 128], lorab[:], True, True))
        k.pe(items)
        for c2 in range(2):
            k.act(sw[:, c2, :], psZ[:, c2 * 128:(c2 + 1) * 128], AF.Exp, bias=negb[:, c2:c2 + 1], scale=-1.0)
            k.act(aa[:, c2, :], psZ[:, 256 + c2 * 128:256 + (c2 + 1) * 128], AF.Exp, bias=negb[:, 2 + c2:3 + c2], scale=-1.0)
        k.ts("pool", sw[:], sw[:], 1.0, ALU.add)
        k.recip(sw[:], sw[:])
        k.ts("pool", aa[:], aa[:], 1.0, ALU.add)
        k.recip(aa[:], aa[:])
        psG = ps_rot.next()
        k.pe([(psG[:, 0:256], sgd[:, 0, :], g2b[:, 0, :], True, False),
              (psG[:, 0:256], sgd[:, 1, :], g2b[:, 1, :], False, True)])
        k.cp("act", gtm[:], psG[:, 0:256])
        for c2 in range(2):
            k.op("dve", lambda e, c2=c2: e.tensor_tensor_scan(out=cum.t[:, c2, :], data0=cs.t[:, 256:384], data1=sw.t[:, c2, :],
                                                           initial=0.0, op0=ALU.mult, op1=ALU.add), [cs.tk, sw.tk], [cum.tk])
        k.tt("pool", Ewx[:], cum[:], sw[:], ALU.subtract)
        k.act(Ew[:], cum[:], AF.Exp, scale=-C0)
        k.act(Ewx[:], Ewx[:], AF.Exp, scale=-C0)
        k.act(Einv[:], cum[:], AF.Exp, scale=C0)
        k.ts("dve", bC[:], View(cum.t[:, :, 63:128:64], cum.tk), -C0, ALU.mult)
        k.act(WC[:], bC[:], AF.Exp)
        for c2 in range(2):
            for ch in range(2):
                k.act(Eend[:, c2, ch * 64:(ch + 1) * 64], cum[:, c2, ch * 64:(ch + 1) * 64], AF.Exp, scale=C0, bias=bC[:, c2, ch:ch + 1])
        for c2 in range(2):
            k.act(kk[:, c2, :], sh[:, 2 + c2, :], AF.Copy, scale=pvt[:, 13 + c2:14 + c2])
        k.tt("pool", kq[:], kk[:], kk[:], ALU.mult)
        psN = ps_rot.next()
        k.pe([(psN[:, c2 * 128:(c2 + 1) * 128], cs[:, 0:128], kq[:, c2, :], True, True) for c2 in range(2)])
        k.ts("dve", rn[:], View(psN.t[:, 0:256].rearrange("p (a b) -> p a b", a=2), psN.tk), 1e-24, ALU.max)
        k.act(rn[:], rn[:], AF.Ln)
        k.act(rn[:], rn[:], AF.Exp, scale=-0.5)
        k.tt("pool", kk[:], kk[:], rn[:], ALU.mult)
        for c2 in range(2):
            k.ts("dve", k2[:, c2, :], aa[:, c2, :], pvt[:, 15 + c2:16 + c2], ALU.mult, omka[:, c2:c2 + 1], ALU.add)
        k.tt("pool", k2[:], kx, k2[:], ALU.mult)
        k.tt("pool", bvec[:], kk[:], aa[:], ALU.mult)

        def c4(buf):
            return buf[:, :, :]

        k.stt(FT[:, 2, :, :], kk[:], -1.0, Ewx[:], ALU.mult, ALU.mult)
        k.tt("dve", RT[:], r, Ew[:], ALU.mult)
        k.tt("pool", BK[:, :, 0, :], bvec[:], Einv[:], ALU.mult)
        k.tt("dve", BK[:, :, 1, :], k2[:], Einv[:], ALU.mult)
        k.tt("pool", FT[:, 0, :, :], bvec[:], Eend[:], ALU.mult)
        k.tt("dve", FT[:, 1, :, :], k2[:], Eend[:], ALU.mult)
        k.cp("act", FT[:, 3, :, :], v)
        for e in range(2):
            pb = 64 * e
            k.cp("act", View(ARbd.t[pb:pb + 64, :, e, 0, :], ARbd.tk), View(FT.t[pb:pb + 64, 2, :, :], FT.tk))
            k.cp("act", View(ARbd.t[pb:pb + 64, :, e, 1, :], ARbd.tk), View(RT.t[pb:pb + 64, :, :], RT.tk))
            k.cp("act", View(Bbd.t[pb:pb + 64, :, e, :], Bbd.tk), View(BK.t[pb:pb + 64, :, 0, :], BK.tk))
        k.tt("pool", kq[:], r, k2[:], ALU.mult)
        for c2 in range(2):
            k.act(rk[:, c2, :], kq[:, c2, :], AF.Copy, scale=pvt[:, 17 + c2:18 + c2])
        psT = ps_rot.next()
        psT16 = psT.same(BF16)
        k.pe([("T", psT16[:, (kind * 2 + c2) * 128:(kind * 2 + c2 + 1) * 128], FT[:, kind, c2, :], identb[:])
              for kind in range(4) for c2 in range(2)])
        psTv = psT.t.ap().bitcast(BF16)[:, 0:1024].rearrange("p (a b c) -> p a b c", a=4, b=2)
        k.cp("act", TM[:], View(psTv, psT.tk))
        for ch in range(2):
            pt = 64 * ch
            k.cp("act", View(TMbd.t[pt:pt + 64, :, :, ch, :], TMbd.tk), View(psTv[pt:pt + 64, 0:2, :, :], psT.tk))
        UL = ULr.next()
        mk1 = View(mkb.t[:, 0:512].rearrange("p (a b c) -> p a b c", a=2, b=2), mkb.tk)
        for c2 in range(2):
            arv = View(ARbd.t[:, c2, :, :, :].rearrange("p a b c -> p (a b c)"), ARbd.tk)
            for kind in range(2):
                psg = ps_rot.next()
                k.pe([(psg[:, :], BK[:, c2, kind, :], arv, True, True)])
                k.tt("dve", S4[:, 2 * c2:2 * c2 + 2, 2 * kind:2 * kind + 2, :],
                     View(psg.t[:, :].rearrange("p (a b c) -> p a b c", a=2, b=2), psg.tk), mk1, ALU.mult)
        psLo = ps_rot.next()
        k.pe([(psLo[:, c2 * 256:(c2 + 1) * 256], FT[:, 2, c2, :], View(Bbd.t[:, c2, :, :].rearrange("p a b -> p (a b)"), Bbd.tk), True, True)
              for c2 in range(2)])
        k.tt("dve", UL[:, :, 1, :], View(psLo.t[:, :].rearrange("p (h q) -> p h q", h=4), psLo.tk),
             View(mkb.t[:, 512:1024].rearrange("p (h q) -> p h q", h=4), mkb.tk), ALU.mult)
        k.cp("act", UL[:, :, 0, :], S4[:, :, 0, :])
        Y = Yr.next()
        psY = ps_rot.next()
        k.pe([(psY[:, hh * 64:(hh + 1) * 64], S4[:, hh, 2, :], TM[:, 3, c2, 64 * e:64 * e + 64], True, True)
              for hh, (c2, e) in enumerate(heads)])
        k.cp("act", Y[:, :, 0:64], View(TM.t[:, 2, :, :].rearrange("p a (e q) -> p (a e) q", e=2), TM.tk))
        k.cp("dve", Y[:, :, 64:128], View(psY.t[:, 0:256].rearrange("p (h q) -> p h q", h=4), psY.tk))
        for lvl in range(6):
            psY = ps_rot.next()
            k.pe([(psY[:, hh * 128:(hh + 1) * 128], UL[:, hh, 0, :], Y[:, hh, :], True, True) for hh in range(4)])
            if lvl < 5:
                psU = [ps_rot.next(), ps_rot.next()]
                items = []
                for hh in range(4):
                    pu = psU[hh // 2]
                    o0 = (hh % 2) * 256
                    items.append((pu[:, o0:o0 + 128], UL[:, hh, 1, :], UL[:, hh, 0, :], True, True))
                    if lvl < 4:
                        items.append((pu[:, o0 + 128:o0 + 256], UL[:, hh, 0, :], UL[:, hh, 1, :], True, True))
                k.pe(items)
            Yn = Yr.next()
            k.tt("dve", Yn[:], View(psY.t[:, :].rearrange("p (h q) -> p h q", h=4), psY.tk), Y[:], ALU.add)
            Y = Yn
            if lvl < 5:
                UL = ULr.next()
                for half in range(2):
                    k.cp("act", UL[:, 2 * half:2 * half + 2, :, :], View(psU[half].t[:, :].rearrange("p (h a q) -> p h a q", h=2, a=2), psU[half].tk))
        psP = ps_rot.next()
        items = []
        for hh, (c2, e) in enumerate(heads):
            pbk = 64 * e
            m1 = Y[:, hh, 0:64]
            items.append((View(psP.t[pbk:pbk + 64, c2 * 256:c2 * 256 + 128], psP.tk), m1,
                          View(TMbd.t[:, 0, c2, :, pbk:pbk + 64], TMbd.tk), True, True))
            items.append((View(psP.t[pbk:pbk + 64, c2 * 256 + 128:c2 * 256 + 256], psP.tk), m1, S4[:, hh, 1, :], True, True))
        k.pe(items)
        ppv = psP.t[:, :].rearrange("p (a x) -> p a x", a=2)
        for e in range(2):
            pbk = 64 * e
            k.cp("dve", View(PTbd.t[pbk:pbk + 64, :, :, pbk:pbk + 64], PTbd.tk),
                 View(ppv[pbk:pbk + 64, :, 0:128].rearrange("p a (c q) -> p a c q", c=2), psP.tk))
        k.tt("dve", GTs[:], View(ppv[:, :, 128:256].rearrange("p a (c q) -> p a c q", c=2), psP.tk),
             View(RT.t[:, :, :].rearrange("p a (c q) -> p a c q", c=2), RT.tk), ALU.add)
        psBn = ps_rot.next()
        k.pe([(psBn[:, 2 * c2:2 * c2 + 2], rk[:, c2, :], bones[:], True, True) for c2 in range(2)])
        k.cp("act", bsum[:], psBn[:, 0:4])
        psYo = ps_rot.next()
        for ch in range(2):
            pt = 64 * ch
            psH = ps_rot.next()
            items = []
            for c2 in range(2):
                items.append((View(psYo.t[pt:pt + 64, c2 * 128:(c2 + 1) * 128], psYo.tk), GTs[:, c2, ch, :],
                              View(Hbd.t[:, c2, :, :].rearrange("p a b -> p (a b)"), Hbd.tk), True, False))
                for e in range(2):
                    hh = 2 * c2 + e
                    yo = View(psYo.t[pt:pt + 64, hh * 64:(hh + 1) * 64], psYo.tk)
                    items.append((yo, S4[:, hh, 1, pt:pt + 64], Y[:, hh, 64:128], False, False))
                    items.append((yo, S4[:, hh, 3, pt:pt + 64], TM[:, 3, c2, 64 * e:64 * e + 64], False, e == 1))
            for c2 in range(2):
                items.append((psH[:, c2 * 64:(c2 + 1) * 64], PTbd[:, c2, ch, :], Hb[:, c2, :], True, False))
                for e in range(2):
                    hh = 2 * c2 + e
                    pbk = 64 * e
                    ho = View(psH.t[pbk:pbk + 64, c2 * 64:(c2 + 1) * 64], psH.tk)
                    items.append((ho, TMbd[:, 0, c2, ch, pbk:pbk + 64], Y[:, hh, 64:128], False, False))
                    items.append((ho, TMbd[:, 1, c2, ch, pbk:pbk + 64], TM[:, 3, c2, pbk:pbk + 64], False, True))
            k.pe(items)
            for c2 in range(2):
                k.stt(H32[:, c2, :], H32[:, c2, :], WC[:, c2, ch:ch + 1], psH[:, c2 * 64:(c2 + 1) * 64], ALU.mult, ALU.add)
            k.cp("act", Hb[:], H32[:])
            for e in range(2):
                pbk = 64 * e
                k.cp("act", View(Hbd.t[pbk:pbk + 64, :, e, :], Hbd.tk), View(H32.t[pbk:pbk + 64, :, :], H32.tk))
        if j == 0:
            return
        k.cp("act", yv[:], View(psYo.t[:, 0:256].rearrange("p (h q) -> p h q", h=4), psYo.tk))
        k.red(gst[:, 0:4], yv[:])
        k.tt("pool", ytmp[:], yv[:], yv[:], ALU.mult)
        k.red(gst[:, 4:8], ytmp[:])
        k.ts("dve", gst[:, 0:8], gst[:, 0:8], 1.0 / 64, ALU.mult)
        k.tt("dve", gst[:, 8:12], gst[:, 0:4], gst[:, 0:4], ALU.mult)
        k.tt("dve", gst[:, 8:12], gst[:, 4:8], gst[:, 8:12], ALU.subtract)
        k.act(gst[:, 8:12], gst[:, 8:12], AF.Ln, bias=epsb[:, 1:2])
        k.act(gst[:, 12:16], gst[:, 8:12], AF.Exp, scale=-0.5)
        for hh in range(4):
            k.ts("dve", ytmp[:, hh, :], yv[:, hh, :], gst[:, hh:hh + 1], ALU.subtract, gst[:, 12 + hh:13 + hh], ALU.mult)
        yt2 = View(ytmp.t[:, :, :].rearrange("p h q -> p (h q)"), ytmp.tk)
        k.tt("pool", yt2, yt2, lnw_t[:, 0, :], ALU.mult)
        k.tt("pool", yt2, yt2, lnw_t[:, 1, :], ALU.add)
        for hh, (c2, e) in enumerate(heads):
            k.stt(ytmp[:, hh, :], TM[:, 3, c2, 64 * e:64 * e + 64], bsum[:, hh:hh + 1], ytmp[:, hh, :], ALU.mult, ALU.add)
        k.tt("dve", mix[:, 0:256], yt2, gtm[:], ALU.mult)

    def outproj_tile(j, mix):
        psT = ps_rot.next()
        psT16 = psT.same(BF16)
        k.pe([("T", psT16[:, c * 128:(c + 1) * 128], mix[:, c * 128:(c + 1) * 128], identb[:]) for c in range(4)])
        mixT = mixT_r.next()
        k.cp("act", mixT[:], View(psT.t.ap().bitcast(BF16)[:, 0:512].rearrange("p (a b) -> p a b", a=4), psT.tk))
        po, spo = po_r.next()
        for br in range(2):
            for half in range(2):
                ps = ps_rot.next()
                k.pe([(ps[:, :], mixT[:, 2 * br + kc, :], wob[:, 2 * br + kc, half * 512:(half + 1) * 512], kc == 0, kc == 1) for kc in range(2)])
                k.cp("act" if half == 0 else "dve", po[:, br * 1024 + half * 512:br * 1024 + (half + 1) * 512], ps[:, :])
        i = j - 1
        k.dma("sp", View(rs_in[i // 16].t[(i % 16) * 128:(i % 16 + 1) * 128, :], rs_tk[i // 16][i % 16]), po[:], spo)
        if dbg:
            dm, sdm = dmix_r.next()
            k.cp("pool", dm[:], mix[:])
            k.dma("sp", d_mix[j * 128:(j + 1) * 128, :], dm[:], sdm)

    cc_sem = es.enter_context(nc.semaphore("cc_sem"))
    k.engs["cc"] = EngS("cc", None, cc_sem)
    n_cc = 0
    def issue_rs(j):
        nonlocal n_cc
        if do_rs and j >= 1 and j % 16 == 0:
            q = j // 16 - 1
            E = k.engs["pool"]
            k._wait(E, rs_tk[q], [rs_out[q].tk])
            nc.gpsimd.collective_compute("ReduceScatter", ALU.add, replica_groups=[[0, 1, 2, 3], [4, 5, 6, 7]],
                                         ins=[rs_in[q].t.ap().opt()], outs=[rs_out[q].t.ap().opt()]).then_inc(cc_sem)
            n_cc += 1
            for t_ in rs_tk[q]:
                t_.rd["cc"] = n_cc
            rs_out[q].tk.lw = ("cc", n_cc)
            rs_out[q].tk.rd = {}

    tiles = [{"j": j} for j in range(NT1)]

    def P(t):
        t["sh"], t["QT"] = p1_tile(t["j"])

    def R(t):
        t["mix"] = mix_r.next()
        rwkv_tile(t["j"], t["sh"], t["mix"])

    def Y(t):
        attn_tile(t["j"], t["QT"], t["mix"])
        outproj_tile(t["j"], t["mix"])

    lenP, lenR = 40, 230
    if NT1 > 0:
        n0 = k.nops
        P(tiles[0])
        lenP = k.nops - n0
    for j in range(NT1):
        fns, ests = [], []
        fns.append(lambda t=tiles[j]: R(t))
        ests.append(lenR)
        if j >= 2:
            jj = j - 1
            fns.append(lambda t=tiles[jj]: Y(t))
            ests.append(4 * (3 * ((jj + 4) // 4) + 2) + 40)
        else:
            fns.append(lambda: None)
            ests.append(1)
        if j + 1 < NT1:
            fns.append(lambda t=tiles[j + 1]: P(t))
            ests.append(lenP)
        done = k.weave.run(fns, ests)
        lenR = max(1, done[0])
        if j >= 2:
            issue_rs(j - 1)
    if NT1 >= 2:
        Y(tiles[NT1 - 1])
        issue_rs(NT1 - 1)

    k.barrier()
    es1.close()
    if do_p2:
        src = rs_dbg if rs_dbg is not None else rs_out
        esA = ExitStack()
        k.es = es
        h2 = k.sb([128, 16, D], F32, "h2")
        k.es = esA
        wgb = k.sb([128, 8, 2048], BF16, "wgb")
        sem_wg = k.dsem()
        for kc in range(8):
            k.dma("pool", wgb[:, kc, :], wg[kc * 128:(kc + 1) * 128, :], sem_wg)
        woutb = k.sb([128, 8, D], BF16, "woutb")
        sem_wo2 = k.dsem()
        for kc in range(8):
            k.dma("pool", woutb[:, kc, :], wout[kc * 128:(kc + 1) * 128, :], sem_wo2)
        nw2 = k.sb([128, D], F32, "nw2")
        k.dma("sp", nw2[:], nwb[:, :], k.dsem())
        x2_r = Rot([(k.sb([128, 4, D], F32, f"x2t{i}"), k.dsem()) for i in range(2)])
        rs_r = Rot([(k.sb([128, 4, 2048], BF16, f"rst{i}"), k.dsem()) for i in range(1)])
        st2 = k.sb([128, 8], F32, "st2")
        xs2 = k.sb([128, D], BF16, "xs2")
        hT2 = k.sb([128, 8, 512], BF16, "hT2")
        gate = k.sb([128, 2048], BF16, "gate")
        mrg = k.sb([128, D], BF16, "mrg")
        mrg2 = k.sb([128, D], BF16, "mrg2")
        mT = k.sb([128, 8, 128], BF16, "mT")
        gateB = k.sb([128, 2048], BF16, "gateB")
        mrgB = k.sb([128, D], BF16, "mrgB")
        mrg2B = k.sb([128, D], BF16, "mrg2B")
        mTB = k.sb([128, 8, 128], BF16, "mTB")

        def norm_T(xin, nwt, xs_, st_, dstT, col0):
            k.act(xs_[:], xin, AF.Square, accum=st_[:, 0:1])
            k.act(st_[:, 1:2], st_[:, 0:1], AF.Sqrt, bias=1e-5, scale=1.0 / D)
            k.recip(st_[:, 2:3], st_[:, 1:2])
            k.stt(xs_[:], xin, st_[:, 2:3], nwt, ALU.mult, ALU.mult)
            ps = ps_rot.next()
            ps16 = ps.same(BF16)
            k.pe([("T", ps16[:, kc * 128:(kc + 1) * 128], xs_[:, kc * 128:(kc + 1) * 128], identb[:]) for kc in range(8)])
            k.cp("act", View(dstT.t[:, :, col0:col0 + 128], dstT.tk),
                 View(ps.t.ap().bitcast(BF16)[:, 0:1024].rearrange("p (a b) -> p a b", a=8), ps.tk))

        for kq in range(4):
            x2t, sx2 = x2_r.next()
            k.dma("sp", x2t[:], View(x2.t[kq].rearrange("(s p) d -> p s d", p=128), x2.tk), sx2)
            rst, srs = rs_r.next()
            k.dma("sp", rst[:], View(src[kq].t.ap().rearrange("(s p) d -> p s d", p=128), src[kq].tk), srs)
            for s_ in range(4):
                norm_T(x2t[:, s_, :], nw2[:], xs2, st2, hT2, s_ * 128)
            def sub2a(s_, gate, mrg, mrg2, mT, kq=kq, x2t=x2t, rst=rst):
                for cb in range(4):
                    ps = ps_rot.next()
                    k.pe([(ps[:, :], hT2[:, kc, s_ * 128:(s_ + 1) * 128], wgb[:, kc, cb * 512:(cb + 1) * 512], kc == 0, kc == 7) for kc in range(8)])
                    k.act(gate[:, cb * 512:(cb + 1) * 512], ps[:, :], AF.Sigmoid)
                k.tt("pool", mrg[:], gate[:, 0:D], rst[:, s_, 0:D], ALU.mult)
                k.tt("dve", mrg2[:], gate[:, D:2 * D], rst[:, s_, D:2 * D], ALU.mult)
                k.tt("pool", mrg[:], mrg[:], mrg2[:], ALU.add)
                ps = ps_rot.next()
                ps16 = ps.same(BF16)
                k.pe([("T", ps16[:, kc * 128:(kc + 1) * 128], mrg[:, kc * 128:(kc + 1) * 128], identb[:]) for kc in range(8)])
                k.cp("act", mT[:], View(ps.t.ap().bitcast(BF16)[:, 0:1024].rearrange("p (a b) -> p a b", a=8), ps.tk))
                for cb in range(2):
                    ps = ps_rot.next()
                    k.pe([(ps[:, :], mT[:, kc, :], woutb[:, kc, cb * 512:(cb + 1) * 512], kc == 0, kc == 7) for kc in range(8)])
                    k.tt("dve", h2[:, kq * 4 + s_, cb * 512:(cb + 1) * 512], ps[:, :], x2t[:, s_, cb * 512:(cb + 1) * 512], ALU.add)

            for s0 in (0, 2):
                k.weave.run([lambda: sub2a(s0, gate, mrg, mrg2, mT), lambda: sub2a(s0 + 1, gateB, mrgB, mrg2B, mTB)], [1, 1])
        k.barrier()
        esA.close()
        esB = ExitStack()
        k.es = esB
        nmt = k.sb([128, 2, D], F32, "nmt")
        k.dma("sp", nmt[:], nm2[:, :, :], k.dsem())
        hnT = k.sb([128, 8, 2048], BF16, "hnT")
        xs3 = k.sb([128, D], BF16, "xs3")
        st3 = k.sb([128, 8], F32, "st3")
        wq_r = Rot([(k.sb([128, 8, D], BF16, f"w1q{i}"), k.sb([128, 8, D], BF16, f"w2q{i}"), k.dsem(), k.dsem()) for i in range(2)])
        aT_r = rot(2, [128, 8, 512], BF16, "aT")
        rl_r = rot(2, [128, 512], BF16, "rl")

        def load_q(p):
            w1q, w2q, s1, s2 = wq_r.next()
            for kc in range(8):
                k.dma("pool", w1q[:, kc, :], mw1[kc * 128:(kc + 1) * 128, p * D:(p + 1) * D], s1)
            for fc in range(8):
                k.dma("pool", w2q[:, fc, :], mw2[p * D + fc * 128:p * D + (fc + 1) * 128, :], s2)
            return w1q, w2q

        nxt = load_q(0)
        for t16 in range(16):
            norm_T(h2[:, t16, :], nmt[:, 0, :], xs3, st3, hnT, t16 * 128)
        for p in range(4):
            w1q, w2q = nxt
            if p < 3:
                nxt = load_q(p + 1)
            def up(kq, aT, w1q=w1q):
                for fc in range(8):
                    ps = ps_rot.next()
                    k.pe([(ps[:, :], w1q[:, kc, fc * 128:(fc + 1) * 128], hnT[:, kc, kq * 512:(kq + 1) * 512], kc == 0, kc == 7) for kc in range(8)])
                    rl = rl_r.next()
                    k.act(rl[:], ps[:, :], AF.Relu)
                    k.tt("pool", aT[:, fc, :], rl[:], rl[:], ALU.mult)

            def down(kq, aT, w2q=w2q):
                for s_ in range(4):
                    for cb in range(2):
                        ps = ps_rot.next()
                        k.pe([(ps[:, :], aT[:, fc, s_ * 128:(s_ + 1) * 128], w2q[:, fc, cb * 512:(cb + 1) * 512], fc == 0, fc == 7) for fc in range(8)])
                        hv = h2[:, kq * 4 + s_, cb * 512:(cb + 1) * 512]
                        k.tt("dve", hv, ps[:, :], hv, ALU.add)

            aTs = [aT_r.next() for _ in range(4)]
            up(0, aTs[0])
            for kq in range(4):
                if kq < 3:
                    k.weave.run([lambda: down(kq, aTs[kq]), lambda: up(kq + 1, aTs[kq + 1])], [1, 1])
                else:
                    down(kq, aTs[kq])
        yo_r = Rot([(k.sb([128, D], F32, f"yo{i}"), k.dsem()) for i in range(2)])
        for t16 in range(16):
            yo, syo = yo_r.next()
            hv = h2[:, t16, :]
            k.act(yo[:], hv, AF.Square, accum=st3[:, 0:1])
            k.act(st3[:, 1:2], st3[:, 0:1], AF.Sqrt, bias=1e-5, scale=1.0 / D)
            k.recip(st3[:, 2:3], st3[:, 1:2])
            k.stt(yo[:], hv, st3[:, 2:3], nmt[:, 1, :], ALU.mult, ALU.mult)
            k.dma("sp", yout[t16 // 4, (t16 % 4) * 128:(t16 % 4 + 1) * 128, :], yo[:], syo)
        k.barrier()
        esB.close()
    k.barrier()
    print("total ops", k.nops, {n: e.count for n, e in k.engs.items() if e.count})
    return nc


def make_consts():
    c = np.zeros((128, 1024), np.float32)
    p = np.arange(128)[:, None]
    q = np.arange(128)[None, :]
    c[:, 0:128] = (p == q)
    c[:, 128:256] = (p <= q)
    c[:, 256:384] = (p // 64 == q // 64)
    pm = np.zeros((128, 128), np.float32)
    for hb in (0, 64):
        for d in range(8):
            pm[hb + d + 8, hb + d] = -1.0
            pm[hb + d, hb + d + 8] = 1.0
    c[:, 384:512] = pm
    c[:, 512:576] = (p % 64 == np.arange(64)[None, :])
    s = (np.arange(128) % 64)[:, None]
    t = np.arange(64)[None, :]
    c[:, 576:640] = (s < t)
    c[:, 640:704] = (s <= t)
    c[:, 704:768] = (s < t)
    c[:, 768:832] = (s <= t)
    c[:, 832:896] = (s > t)
    c[:, 896:1024] = 1.0
    c[:, 896] = 0.0
    c[:, 960] = 0.0
    return c


def make_consts2():
    c = np.zeros((128, 1024), np.float32)
    p = np.arange(128)[:, None]
    q = np.arange(128)[None, :]
    same = (p // 64 == q // 64)
    strict = same & ((p % 64) < (q % 64))
    incl = same & ((p % 64) <= (q % 64))
    lo = same & ((p % 64) > (q % 64))
    for e in range(2):
        c[:, e * 256:e * 256 + 128] = strict
        c[:, e * 256 + 128:e * 256 + 256] = incl
    for h in range(4):
        c[:, 512 + h * 128:512 + (h + 1) * 128] = lo
    return c


def make_rope():
    half = 8
    inv = (np.float32(500000.0) ** (-(np.arange(0, 16, 2, dtype=np.float32)) / np.float32(16))).astype(np.float32)
    pos = np.maximum(np.arange(LP) - 112, 0).astype(np.float32)
    ang = pos[:, None] * inv[None, :]
    cos = np.cos(ang).astype(np.float32)
    sin = np.sin(ang).astype(np.float32)
    ct = np.ones((128, LP), np.float32)
    stb = np.zeros((128, LP), np.float32)
    for hb in (0, 64):
        for d in range(16):
            ct[hb + d] = cos[:, d % 8]
            stb[hb + d] = sin[:, d % 8]
    r = np.stack([ct.reshape(128, NT, 128), stb.reshape(128, NT, 128)], axis=2)
    return np.ascontiguousarray(r)


def core_inputs(inp, c, consts, rope):
    b, g = c // 4, c % 4
    f = lambda a: np.ascontiguousarray(np.asarray(a, dtype=np.float32))
    w_in = inp["w_in"][0]
    sl = slice(256 * g, 256 * g + 256)
    cols = np.concatenate([np.arange(256 * g, 256 * g + 256), 1024 + np.arange(256 * g, 256 * g + 256),
                           2048 + np.arange(256 * g, 256 * g + 256), np.arange(3072, 3360),
                           3360 + np.arange(256 * g, 256 * g + 256), 4384 + np.arange(256 * g, 256 * g + 256),
                           5408 + np.arange(256 * g, 256 * g + 256)])
    mu = inp["rwkv_mu"][0]
    pv = np.zeros((128, 32), np.float32)
    mucols = [mu[0 + 256 * g:0 + 256 * g + 128], mu[256 * g + 128:256 * g + 256],
              mu[1024 + 256 * g:1024 + 256 * g + 128], mu[1024 + 256 * g + 128:1024 + 256 * g + 256],
              mu[2048 + 256 * g:2048 + 256 * g + 128], mu[2048 + 256 * g + 128:2048 + 256 * g + 256],
              mu[3072:3200], mu[3200:3328]]
    for i, m in enumerate(mucols):
        pv[:, i] = m
    pv[0:32, 8] = mu[3328:3360]
    for i, nm in enumerate(["rwkv_w0", "rwkv_a0", "rwkv_k_k", "rwkv_k_a"]):
        vv = inp[nm][0][sl]
        pv[:, 9 + 2 * i] = vv[0:128]
        pv[:, 10 + 2 * i] = vv[128:256]
    rkv = inp["rwkv_r_k"][0].reshape(-1)[sl]
    pv[:, 17] = rkv[0:128]
    pv[:, 18] = rkv[128:256]
    q = g
    x2 = np.stack([inp["x"][b, 2048 * kk + 512 * q:2048 * kk + 512 * q + 512] for kk in range(4)], 0)
    d = {
        "xb": f(inp["x"][b]),
        "meta": f(inp["meta_tokens"]),
        "x2": f(x2),
        "w1c": f(w_in[:, cols]),
        "wg": f(w_in[:, 6432:8480]),
        "pv": pv,
        "nwb": f(np.broadcast_to(inp["norm_mix_w"][0][None, :], (128, D))),
        "lnwb": f(np.broadcast_to(np.stack([inp["rwkv_ln_w"][0][sl], inp["rwkv_ln_b"][0][sl]], 0)[None], (128, 2, 256))),
        "sublnb": f(np.broadcast_to(inp["diff_subln_w"][0][None, :], (128, 128))),
        "lamv": f(np.broadcast_to(np.stack([inp["diff_lq1"][0], inp["diff_lk1"][0], inp["diff_lq2"][0], inp["diff_lk2"][0]], 0)[None], (128, 4, 64))),
        "w2a2": f(np.concatenate([inp["rwkv_w2"][0][:, sl], inp["rwkv_a2"][0][:, sl]], 0)),
        "g2": f(inp["rwkv_g2"][0][:, sl]),
        "wor": f(inp["rwkv_w_o"][0][sl, :]),
        "wod": f(inp["diff_w_o"][0][sl, :]),
        "wout": f(inp["w_out"][0]),
        "mw1": f(inp["mlp_w1"][0]),
        "mw2": f(inp["mlp_w2"][0]),
        "nm2": f(np.broadcast_to(np.stack([inp["norm_mlp_w"][0], inp["final_norm_w"]], 0)[None], (128, 2, D))),
        "rope": rope,
        "cst": consts,
        "cst2": make_consts2(),
    }
    return d


_CACHE = {}


def kernel(**inputs):
    inp = {kk: np.asarray(v) for kk, v in inputs.items()}
    if "nc" not in _CACHE:
        _CACHE["nc"] = build({})
    nc = _CACHE["nc"]
    consts = make_consts()
    rope = make_rope()
    in_maps = [core_inputs(inp, c, consts, rope) for c in range(8)]
    res = run_bass_kernel_spmd(nc, in_maps, core_ids=list(range(8)))
    out = np.zeros((2, 8192, D), np.float32)
    for c in range(8):
        b, q = c // 4, c % 4
        y = np.asarray(res.results[c]["yout"])
        for kk in range(4):
            out[b, 2048 * kk + 512 * q:2048 * kk + 512 * q + 512] = y[kk]
    return out
```

```python
import math
import threading
from contextlib import ExitStack
import numpy as np
import ml_dtypes
import concourse.bass as bass
import concourse.mybir as mybir
from concourse.bass_utils import run_bass_kernel_spmd

F32 = mybir.dt.float32
BF16 = mybir.dt.bfloat16
AF = mybir.ActivationFunctionType
ALU = mybir.AluOpType
AX = mybir.AxisListType

D = 1024
NT = 65
LP = NT * 128
C0 = math.exp(-0.5)
LAMBDA_INIT = 0.8 - 0.6 * math.exp(0.0)


class Tk:
    __slots__ = ("lw", "rd", "name", "psum")

    def __init__(self, name=""):
        self.lw = None
        self.rd = {}
        self.name = name
        self.psum = False


class View:
    __slots__ = ("ap", "tk")

    def __init__(self, ap, tk):
        self.ap = ap
        self.tk = tk


class Buf:
    def __init__(self, t, name, tk=None, dt=None):
        self.t = t
        self.name = name
        self.tk = tk if tk is not None else Tk(name)
        self.dt = dt

    def __getitem__(self, idx):
        ap = self.t[idx] if self.dt is None else self.t.ap().bitcast(self.dt)[idx]
        return View(ap, self.tk)

    def alias(self, name, dt=None):
        return Buf(self.t, name, dt=dt if dt is not None else self.dt)

    def same(self, dt):
        return Buf(self.t, self.name, tk=self.tk, dt=dt)


class EngS:
    def __init__(self, name, eng, sem):
        self.name = name
        self.eng = eng
        self.sem = sem
        self.count = 0
        self.waited = {}


class K:
    def __init__(self, nc, es):
        self.nc = nc
        self.es = es
        self.es_sem = es
        self.engs = {}
        for n, e in (("pe", nc.tensor), ("act", nc.scalar), ("dve", nc.vector), ("pool", nc.gpsimd), ("sp", nc.sync)):
            self.engs[n] = EngS(n, e, es.enter_context(nc.semaphore("sem_" + n)))
        self.nd = 0
        self.nsb = 0
        self.nops = 0
        self.limit = 10 ** 9
        self.weave = Weave()

    def sb(self, shape, dt, name=None):
        self.nsb += 1
        name = name or f"sb{self.nsb}"
        return Buf(self.es.enter_context(self.nc.sbuf_tensor(name, list(shape), dt)), name)

    def dsem(self, name=None):
        self.nd += 1
        name = name or f"dsem{self.nd}"
        e = EngS(name, None, self.es_sem.enter_context(self.nc.semaphore(name)))
        self.engs[name] = e
        return e

    def _wait(self, E, reads, writes):
        deps = {}
        for t in reads:
            if t.lw is not None and deps.get(t.lw[0], 0) < t.lw[1]:
                deps[t.lw[0]] = t.lw[1]
            if t.psum:
                for n, c in t.rd.items():
                    if n != E.name and deps.get(n, 0) < c:
                        deps[n] = c
        for t in writes:
            if t.lw is not None and deps.get(t.lw[0], 0) < t.lw[1]:
                deps[t.lw[0]] = t.lw[1]
            for n, c in t.rd.items():
                if deps.get(n, 0) < c:
                    deps[n] = c
        for n, c in deps.items():
            if n == E.name and n == "pe":
                continue
            if E.waited.get(n, 0) >= c:
                continue
            E.eng.wait_ge(self.engs[n].sem, c)
            E.waited[n] = c

    def op(self, en, fn, reads, writes):
        self.weave.checkpoint()
        self.nops += 1
        if self.nops > self.limit:
            return
        E = self.engs[en]
        self._wait(E, reads, writes)
        ins = fn(E.eng)
        E.count += 1
        ins.then_inc(E.sem, 1)
        for t in reads:
            t.rd[en] = E.count
        for t in writes:
            t.lw = (en, E.count)
            t.rd = {}

    def dma(self, qn, out, in_, dsem, **kw):
        self.weave.checkpoint()
        self.nops += 1
        if self.nops > self.limit:
            return
        Q = self.engs[qn]
        self._wait(Q, [in_.tk], [out.tk])
        ins = Q.eng.dma_start(out=out.ap, in_=in_.ap, **kw)
        dsem.count += 16
        ins.then_inc(dsem.sem, 16)
        in_.tk.rd[dsem.name] = dsem.count
        out.tk.lw = (dsem.name, dsem.count)
        out.tk.rd = {}

    def barrier(self):
        for en in ("sp", "pool", "act", "dve", "pe"):
            E = self.engs[en]
            for n, o in self.engs.items():
                if n == en or o.count == 0:
                    continue
                if E.waited.get(n, 0) < o.count:
                    E.eng.wait_ge(o.sem, o.count)
                    E.waited[n] = o.count

    def ldcast(self, dst, src, stg_r, eng="pool"):
        stg, sem = stg_r.next()
        sv = View(stg.t[:, 0:dst.ap.shape[-1]], stg.tk)
        self.dma("sp", sv, src, sem)
        self.cp(eng, dst, sv)

    def tt(self, en, out, a, b, op):
        self.op(en, lambda e: e.tensor_tensor(out=out.ap, in0=a.ap, in1=b.ap, op=op), [a.tk, b.tk], [out.tk])

    def ts(self, en, out, a, s1, op0, s2=None, op1=None):
        rd = [a.tk]
        v1 = s1
        v2 = s2
        if isinstance(s1, View):
            rd.append(s1.tk)
            v1 = s1.ap
        if isinstance(s2, View):
            rd.append(s2.tk)
            v2 = s2.ap
        kw = {}
        if en == "pool" and op1 is None and s2 is None and op0 == ALU.mult:
            op1 = ALU.add
            v2 = 0.0
        if op1 is not None:
            kw["op1"] = op1
        self.op(en, lambda e: e.tensor_scalar(out=out.ap, in0=a.ap, scalar1=v1, scalar2=v2, op0=op0, **kw), rd, [out.tk])

    def stt(self, out, a, s, b, op0, op1):
        rd = [a.tk, b.tk]
        v = s
        if isinstance(s, View):
            rd.append(s.tk)
            v = s.ap
        self.op("dve", lambda e: e.scalar_tensor_tensor(out=out.ap, in0=a.ap, scalar=v, in1=b.ap, op0=op0, op1=op1), rd, [out.tk])

    def cp(self, en, out, a):
        if en == "act":
            self.op(en, lambda e: e.copy(out=out.ap, in_=a.ap), [a.tk], [out.tk])
        else:
            self.op(en, lambda e: e.tensor_copy(out=out.ap, in_=a.ap), [a.tk], [out.tk])

    def act(self, out, a, func, bias=None, scale=1.0, accum=None):
        rd = [a.tk]
        wr = [out.tk]
        kw = {}
        if isinstance(bias, View):
            rd.append(bias.tk)
            kw["bias"] = bias.ap
        elif bias is not None:
            kw["bias"] = bias
        if isinstance(scale, View):
            rd.append(scale.tk)
            kw["scale"] = scale.ap
        else:
            kw["scale"] = scale
        if accum is not None:
            wr.append(accum.tk)
            kw["accum_out"] = accum.ap
        self.op("act", lambda e: e.activation(out=out.ap, in_=a.ap, func=func, **kw), rd, wr)

    def memset(self, en, out, val):
        self.op(en, lambda e: e.memset(out.ap, val), [], [out.tk])

    def red(self, out, a, op=ALU.add):
        self.op("dve", lambda e: e.tensor_reduce(out=out.ap, in_=a.ap, op=op, axis=AX.X), [a.tk], [out.tk])

    def recip(self, out, a):
        self.op("dve", lambda e: e.reciprocal(out=out.ap, in_=a.ap), [a.tk], [out.tk])

    def pe(self, items):
        rd = []
        wr = []
        for it in items:
            if it[0] == "T":
                wr.append(it[1].tk)
                rd += [it[2].tk, it[3].tk]
            else:
                wr.append(it[0].tk)
                rd += [it[1].tk, it[2].tk]

        def fn(e):
            ins = None
            for it in items:
                if it[0] == "T":
                    ins = e.transpose(it[1].ap, it[2].ap, it[3].ap)
                else:
                    ins = e.matmul(it[0].ap, lhsT=it[1].ap, rhs=it[2].ap, start=it[3], stop=it[4])
            return ins

        self.op("pe", fn, rd, wr)


class Weave:
    def __init__(self):
        self.cv = threading.Condition()
        self.active = None
        self.tl = threading.local()

    def current(self):
        return getattr(self.tl, "sid", 0) if self.active is not None else None

    def _pick(self):
        best = None
        for i in range(len(self.alive)):
            if self.alive[i]:
                f = self.done[i] / self.est[i]
                if best is None or f < best[0]:
                    best = (f, i)
        self.active = best[1] if best is not None else -1

    def checkpoint(self):
        if self.active is None:
            return
        sid = self.tl.sid
        with self.cv:
            self.done[sid] += 1
            self._pick()
            self.cv.notify_all()
            while self.active != sid:
                self.cv.wait()

    def run(self, fns, ests):
        n = len(fns)
        self.done = [0] * n
        self.est = [max(1, e) for e in ests]
        self.alive = [True] * n
        self.err = []

        def body(i):
            self.tl.sid = i
            with self.cv:
                while self.active != i:
                    self.cv.wait()
            try:
                fns[i]()
            except BaseException as ex:
                self.err.append(ex)
            with self.cv:
                self.alive[i] = False
                self._pick()
                self.cv.notify_all()

        ths = [threading.Thread(target=body, args=(i,)) for i in range(n)]
        with self.cv:
            self.active = 0
        for t in ths:
            t.start()
        for t in ths:
            t.join()
        self.active = None
        if self.err:
            raise self.err[0]
        return list(self.done)


class RotSel:
    def __init__(self, weave, pools):
        self.weave = weave
        self.pools = pools

    def next(self):
        c = self.weave.current()
        return self.pools[-1 if c is None else c].next()


class Rot:
    def __init__(self, bufs):
        self.bufs = bufs
        self.i = 0

    def next(self):
        b = self.bufs[self.i % len(self.bufs)]
        self.i += 1
        return b


def build(cfg):
    NT1 = cfg.get("nt1", NT)
    do_p2 = cfg.get("p2", True)
    do_rs = cfg.get("rs", True)
    dbg = cfg.get("dbg", False)
    nc = bass.Bass("TRN2", target_bir_lowering=False)
    es = ExitStack()
    k = K(nc, es)
    k.limit = cfg.get("limit", 10 ** 9)

    def din(name, shape, dt=F32):
        return Buf(nc.dram_tensor(name, list(shape), dt, kind="ExternalInput"), name)

    def dout(name, shape, dt=F32):
        return Buf(nc.dram_tensor(name, list(shape), dt, kind="ExternalOutput"), name)

    xb = din("xb", [8192, D])
    meta = din("meta", [16, D])
    x2 = din("x2", [4, 512, D])
    w1c = din("w1c", [D, 1824])
    wg = din("wg", [D, 2048])
    pv = din("pv", [128, 32])
    nwb = din("nwb", [128, D])
    lnwb = din("lnwb", [128, 2, 256])
    sublnb = din("sublnb", [128, 128])
    lamv = din("lamv", [128, 4, 64])
    w2a2 = din("w2a2", [128, 256])
    g2 = din("g2", [160, 256])
    wor = din("wor", [256, D])
    wod = din("wod", [256, D])
    wout = din("wout", [D, D])
    mw1 = din("mw1", [D, 4096])
    mw2 = din("mw2", [4096, D])
    nm2 = din("nm2", [128, 2, D])
    rs_dbg = [din(f"rs_dbg{i}", [512, 2048], BF16) for i in range(4)] if cfg.get("p2only") else None
    rope = din("rope", [128, NT, 2, 128])
    cst = din("cst", [128, 1024])
    cst2 = din("cst2", [128, 1024])
    yout = dout("yout", [4, 512, D])
    rs_in = [Buf(nc.dram_tensor(f"rs_in{i}", [2048, 2048], BF16), f"rs_in{i}") for i in range(4)]
    rs_tk = [[Tk(f"rs{i}_{r}") for r in range(16)] for i in range(4)]
    rs_out = [Buf(nc.dram_tensor(f"rs_out{i}", [512, 2048], BF16), f"rs_out{i}") for i in range(4)]
    if dbg:
        d_mix = dout("d_mix", [NT1 * 128, 512])

    psb = [Buf(es.enter_context(nc.psum_tensor(f"ps{i}", [128, 512], F32)), f"ps{i}") for i in range(8)]
    for pb_ in psb:
        pb_.tk.psum = True
    ps_rot = RotSel(k.weave, [Rot(psb[0:3]), Rot(psb[3:6]), Rot(psb[6:7]), Rot(psb[0:7])])
    ps_o = [psb[7], psb[7]]

    identb = k.sb([128, 128], BF16, "identb")
    k.dma("pool", identb[:], cst[:, 0:128], k.dsem())
    es1 = ExitStack()
    k.es = es1
    cs = k.sb([128, 384], F32, "cs")
    sem_c = k.dsem("sem_c")
    k.dma("sp", cs[:, 0:256], cst[:, 256:512], sem_c)
    k.dma("sp", cs[:, 256:384], cst[:, 896:1024], sem_c)

    cmaskb = k.sb([128, 128], BF16, "cmaskb")
    k.dma("pool", cmaskb[:], cst[:, 128:256], k.dsem())
    mkb = k.sb([128, 1024], BF16, "mkb")
    k.dma("pool", mkb[:], cst2[:, :], k.dsem())
    bones = k.sb([128, 2], BF16, "bones")
    k.cp("dve", bones[:], cs[:, 0:128:64])
    pvt = k.sb([128, 32], F32, "pvt")
    k.dma("sp", pvt[:], pv[:, :], k.dsem())
    omka = k.sb([128, 2], F32, "omka")
    k.ts("dve", omka[:], pvt[:, 15:17], -1.0, ALU.mult, 1.0, ALU.add)
    nw_t = k.sb([128, D], BF16, "nw_t")
    k.dma("pool", nw_t[:], nwb[:, :], k.dsem())
    lnw_t = k.sb([128, 2, 256], F32, "lnw_t")
    k.dma("sp", lnw_t[:], lnwb[:, :, :], k.dsem())
    subln_t = k.sb([128, 128], F32, "subln_t")
    k.dma("sp", subln_t[:], sublnb[:, :], k.dsem())
    k.ts("dve", subln_t[:], subln_t[:], 1.0 - LAMBDA_INIT, ALU.mult)
    lam_in = k.sb([128, 4, 64], F32, "lam_in")
    k.dma("sp", lam_in[:], lamv[:, :, :], k.dsem())
    lam_t = k.sb([128, 4], F32, "lam_t")
    lam_j = k.sb([128, 64], F32, "lam_j")
    for i in range(2):
        k.tt("dve", lam_j[:], lam_in[:, 2 * i, :], lam_in[:, 2 * i + 1, :], ALU.mult)
        k.red(lam_t[:, i:i + 1], lam_j[:])
    k.act(lam_t[:, 0:2], lam_t[:, 0:2], AF.Exp)
    k.tt("dve", lam_t[:, 2:3], lam_t[:, 0:1], lam_t[:, 1:2], ALU.subtract)
    k.ts("dve", lam_t[:, 3:4], lam_t[:, 2:3], LAMBDA_INIT, ALU.add, -1.0, ALU.mult)
    neglam = lam_t[:, 3:4]

    mub = k.sb([128, 9, 128], F32, "mub")
    for c in range(9):
        k.ts("pool", mub[:, c, :], cs[:, 256:384], 0.0, ALU.mult, pvt[:, c:c + 1], ALU.add)

    w1b = k.sb([128, 8, 1824], BF16, "w1b")
    esS = ExitStack()
    k.es = esS
    stg1 = Rot([(k.sb([128, 1824], F32, f"stg1_{i}"), k.dsem()) for i in range(2)])
    for kc in range(8):
        k.ldcast(w1b[:, kc, :], w1c[kc * 128:(kc + 1) * 128, :], stg1, "pool" if kc % 2 == 0 else "dve")
    k.barrier()
    esS.close()
    k.es = es1
    w2p = k.sb([128, 2, 256], BF16, "w2p")
    k.memset("pool", w2p[:], 0.0)
    sem_w2 = k.dsem()
    k.dma("pool", View(w2p.t[0:64, 0, :], w2p.tk), w2a2[0:64, :], sem_w2)
    k.dma("pool", View(w2p.t[64:128, 1, :], w2p.tk), w2a2[64:128, :], sem_w2)
    g2b = k.sb([128, 2, 256], BF16, "g2b")
    k.memset("pool", g2b[:], 0.0)
    sem_g2 = k.dsem()
    k.dma("pool", g2b[:, 0, :], g2[0:128, :], sem_g2)
    k.dma("pool", View(g2b.t[0:32, 1, :], g2b.tk), g2[128:160, :], sem_g2)
    wob = k.sb([128, 4, D], BF16, "wob")
    sem_wo = k.dsem()
    for i in range(2):
        k.dma("pool", wob[:, i, :], wor[i * 128:(i + 1) * 128, :], sem_wo)
        k.dma("pool", wob[:, 2 + i, :], wod[i * 128:(i + 1) * 128, :], sem_wo)

    KT = k.sb([128, 2, LP], BF16, "KT")
    KTr = [KT.alias(f"KT{j}") for j in range(NT)]
    VA = k.sb([128, NT, 2, 130], BF16, "VA")
    VAr = [VA.alias(f"VA{j}") for j in range(NT)]
    k.op("pool", lambda e: e.memset(VA.t[:, :, :, 128:130], 1.0), [], [VA.tk] + [r.tk for r in VAr])
    k.op("pool", lambda e: e.memset(VA.t[0:112, 0, :, 128:130], 0.0), [], [VA.tk, VAr[0].tk])
    raw = k.sb([128, 9, 129], F32, "raw")
    k.memset("pool", raw[:], 0.0)
    H32 = k.sb([128, 2, 64], F32, "H32")
    k.memset("dve", H32[:], 0.0)
    Hb = k.sb([128, 2, 64], BF16, "Hb")
    k.memset("dve", Hb[:], 0.0)

    def rot(n, shape, dt, name):
        return Rot([k.sb(shape, dt, f"{name}{i}") for i in range(n)])

    xt_r = Rot([(k.sb([128, D], F32, f"xt{i}"), k.dsem(f"sem_xt{i}")) for i in range(1)])
    rp_r = Rot([(k.sb([128, 2, 128], F32, f"rp{i}"), k.dsem(f"sem_rp{i}")) for i in range(2)])
    junk = k.sb([128, 128], F32, "junk")
    st_r = rot(2, [128, 8], F32, "st")
    xs_r = rot(1, [128, D], BF16, "xs")
    hT_r = rot(1, [128, 8, 128], BF16, "hT")
    sh_r = rot(2, [128, 9, 128], F32, "sh")
    qkraw_r = rot(1, [128, 4, 128], F32, "qkraw")
    QT_r = rot(2, [128, 2, 2, 128], BF16, "QT")
    for qb in QT_r.bufs:
        k.memset("pool", qb[:], 0.0)
    ropt = k.sb([128, 4, 128], F32, "ropt")
    PT_r = rot(3, [128, 4, 128], BF16, "PT")
    on_r = rot(1, [128, 4, 128], F32, "on")
    rcp_r = rot(2, [128, 4], F32, "rcp")
    dsc = k.sb([128, 256], F32, "dsc")
    mix_r = rot(2, [128, 512], BF16, "mix")
    mixT_r = rot(2, [128, 4, 128], BF16, "mixT")
    po_r = Rot([(k.sb([128, 2048], BF16, f"po{i}"), k.dsem(f"sem_po{i}")) for i in range(1)])
    if dbg:
        dmix_r = Rot([(k.sb([128, 512], F32, f"dmix{i}"), k.dsem(f"sem_dmix{i}")) for i in range(1)])

    def f32t(name, shape=(128, 2, 128)):
        return k.sb(list(shape), F32, name)

    lorab = k.sb([128, 128], BF16, "lorab")
    sgd = k.sb([128, 2, 128], BF16, "sgd")
    k.memset("pool", sgd[:], 0.0)
    sw = f32t("sw")
    cum = f32t("cum")
    aa = f32t("aa")
    Ew = f32t("Ew")
    Ewx = f32t("Ewx")
    Einv = f32t("Einv")
    Eend = f32t("Eend")
    bC = k.sb([128, 2, 2], F32, "bC")
    WC = k.sb([128, 2, 2], F32, "WC")
    kk = f32t("kk")
    kq = f32t("kq")
    rn = f32t("rn")
    k2 = f32t("k2")
    bvec = f32t("bvec")
    rk = k.sb([128, 2, 128], BF16, "rk")
    BK = k.sb([128, 2, 2, 128], BF16, "BK")
    RT = k.sb([128, 2, 128], BF16, "RT")
    ARbd = k.sb([128, 2, 2, 2, 128], BF16, "ARbd")
    Bbd = k.sb([128, 2, 2, 128], BF16, "Bbd")
    TMbd = k.sb([128, 2, 2, 2, 128], BF16, "TMbd")
    PTbd = k.sb([128, 2, 2, 128], BF16, "PTbd")
    Hbd = k.sb([128, 2, 2, 64], BF16, "Hbd")
    for zb in (ARbd, Bbd, TMbd, PTbd, Hbd):
        k.memset("pool", zb[:], 0.0)
    FT = k.sb([128, 4, 2, 128], BF16, "FT")
    TM = k.sb([128, 4, 2, 128], BF16, "TM")
    S4 = k.sb([128, 4, 4, 128], BF16, "S4")
    ULr = rot(2, [128, 4, 2, 128], BF16, "UL")
    Yr = rot(2, [128, 4, 128], BF16, "Y")
    GTs = k.sb([128, 2, 2, 64], BF16, "GTs")
    yv = k.sb([128, 4, 64], F32, "yv")
    gst = k.sb([128, 16], F32, "gst")
    bsum = k.sb([128, 4], F32, "bsum")
    gtm = k.sb([128, 256], F32, "gtm")
    ytmp = k.sb([128, 4, 64], F32, "ytmp")

    heads = [(c2, e) for c2 in range(2) for e in range(2)]

    def p1_tile(j):
        xt, sx = xt_r.next()
        if j == 0:
            k.memset("pool", xt[:], 0.0)
            k.dma("sp", View(xt.t[112:128, :], xt.tk), meta[:, :], sx)
        else:
            k.dma("sp", xt[:], xb[(j - 1) * 128:j * 128, :], sx)
        rp, srp = rp_r.next()
        k.dma("sp", rp[:], rope[:, j, :, :], srp)
        st = st_r.next()
        xs = xs_r.next()
        k.act(xs[:], xt[:], AF.Square, accum=st[:, 0:1])
        k.act(st[:, 1:2], st[:, 0:1], AF.Sqrt, bias=1e-5, scale=1.0 / D)
        k.recip(st[:, 2:3], st[:, 1:2])
        k.stt(xs[:], xt[:], st[:, 2:3], nw_t[:], ALU.mult, ALU.mult)
        ps = ps_rot.next()
        psb16 = ps.same(BF16)
        k.pe([("T", psb16[:, kc * 128:(kc + 1) * 128], xs[:, kc * 128:(kc + 1) * 128], identb[:]) for kc in range(8)])
        hT = hT_r.next()
        k.cp("act", hT[:], View(psb16.t.ap().bitcast(BF16)[:, 0:1024].rearrange("p (a b) -> p a b", a=8), ps.tk))
        def proj(ps, slot, col0, width=128, rows=128):
            return [(View(ps.t[0:rows, slot * 128:slot * 128 + 128], ps.tk) if rows != 128 else ps[:, slot * 128:slot * 128 + 128],
                     w1b[:, kc, col0:col0 + width], hT[:, kc, :], kc == 0, kc == 7) for kc in range(8)]
        psA = ps_rot.next()
        items = []
        for s in range(4):
            items += proj(psA, s, s * 128)
        k.pe(items)
        k.cp("act", raw[:, 0:4, 1:129], View(psA.t[:, :].rearrange("p (a b) -> p a b", a=4), psA.tk))
        psB = ps_rot.next()
        items = []
        for s in range(4):
            items += proj(psB, s, 512 + s * 128)
        k.pe(items)
        k.cp("act", raw[:, 4:8, 1:129], View(psB.t[:, :].rearrange("p (a b) -> p a b", a=4), psB.tk))
        psC = ps_rot.next()
        items = []
        for s in range(4):
            items += proj(psC, s, 1056 + s * 128)
        k.pe(items)
        qkraw = qkraw_r.next()
        k.cp("dve", qkraw[:], View(psC.t[:, :].rearrange("p (a b) -> p a b", a=4), psC.tk))
        psD = ps_rot.next()
        items = proj(psD, 0, 1024, width=32, rows=32)
        items += [(psD[:, 128:384], hT[:, kc, :], w1b[:, kc, 1568:1824], kc == 0, kc == 7) for kc in range(8)]
        k.pe(items)
        k.cp("act", View(raw.t[0:32, 8, 1:129], raw.tk), View(psD.t[0:32, 0:128], psD.tk))
        k.cp(cfg.get("vaeng", "act"), View(VA.t[:, j, :, 0:128], VAr[j].tk), View(psD.t[:, 128:384].rearrange("p (a b) -> p a b", a=2), psD.tk))
        if j == 0:
            k.memset("pool", View(VA.t[0:112, 0, :, 0:128], VAr[0].tk), 0.0)
        sh = sh_r.next()
        k.tt("pool", sh[:], raw[:, :, 0:128], raw[:, :, 1:129], ALU.subtract)
        k.tt("pool", sh[:], sh[:], mub[:], ALU.mult)
        k.tt("pool", sh[:], sh[:], raw[:, :, 1:129], ALU.add)
        k.cp("pool", raw[:, :, 0:1], raw[:, :, 128:129])
        psR = ps_rot.next()
        k.pe([(psR[:, s * 128:(s + 1) * 128], cs[:, 128:256], qkraw[:, s, :], True, True) for s in range(4)])
        for s in range(4):
            k.tt("dve", ropt[:, s, :], psR[:, s * 128:(s + 1) * 128], rp[:, 1, :], ALU.mult)
            k.tt("pool", qkraw[:, s, :], qkraw[:, s, :], rp[:, 0, :], ALU.mult)
        QT = QT_r.next()
        for e in range(2):
            pb = 64 * e
            k.tt("pool", View(QT.t[pb:pb + 64, :, e, :], QT.tk), View(qkraw.t[pb:pb + 64, 0:2, :], qkraw.tk),
                 View(ropt.t[pb:pb + 64, 0:2, :], ropt.tk), ALU.add)
        k.tt("pool", View(KT.t[:, :, j * 128:(j + 1) * 128], KTr[j].tk), qkraw[:, 2:4, :], ropt[:, 2:4, :], ALU.add)
        return sh, QT

    def attn_tile(j, QT, mix):
        on = on_r.next()
        rcp = rcp_r.next()
        groups = [(h, i0) for h in range(4) for i0 in range(0, j + 1, 4)]
        pss = {}

        def qk(g):
            h, i0 = groups[g]
            dh, e = h // 2, h % 2
            n = min(4, j + 1 - i0)
            ps = ps_rot.next()
            k.pe([(ps[:, s * 128:(s + 1) * 128],
                   View(KT.t[:, dh, (i0 + s) * 128:(i0 + s + 1) * 128], KTr[i0 + s].tk),
                   QT[:, dh, e, :], True, True) for s in range(n)])
            pss[g] = ps

        for g in range(min(2, len(groups))):
            qk(g)
        for g, (h, i0) in enumerate(groups):
            dh, e = h // 2, h % 2
            po = ps_o[h % 2]
            n = min(4, j + 1 - i0)
            ps = pss.pop(g)
            PT = PT_r.next()
            k.act(View(PT.t[:, 0:n, :], PT.tk), View(ps.t[:, 0:n * 128].rearrange("p (a b) -> p a b", a=n), ps.tk), AF.Exp, scale=0.125)
            if i0 + n - 1 == j:
                k.tt("pool", PT[:, n - 1, :], PT[:, n - 1, :], cmaskb[:], ALU.mult)
            if g + 2 < len(groups):
                qk(g + 2)
            k.pe([(po[:, 0:130], PT[:, s, :], View(VA.t[:, i0 + s, dh, :], VAr[i0 + s].tk), (i0 + s) == 0, (i0 + s) == j)
                  for s in range(n)])
            if i0 + n - 1 == j:
                k.recip(rcp[:, h:h + 1], po[:, 128:129])
                k.ts("dve", on[:, h, :], po[:, 0:128], rcp[:, h:h + 1], ALU.mult)
        for dh in range(2):
            o = dsc[:, dh * 128:(dh + 1) * 128]
            k.stt(o, on[:, 2 * dh + 1, :], neglam, on[:, 2 * dh, :], ALU.mult, ALU.add)
            k.act(junk[:], o, AF.Square, accum=rcp[:, dh:dh + 1])
            k.act(rcp[:, dh:dh + 1], rcp[:, dh:dh + 1], AF.Sqrt, bias=1e-5, scale=1.0 / 128)
            k.recip(rcp[:, 2 + dh:3 + dh], rcp[:, dh:dh + 1])
            k.stt(mix[:, 256 + dh * 128:256 + (dh + 1) * 128], o, rcp[:, 2 + dh:3 + dh], subln_t[:], ALU.mult, ALU.mult)

    def rwkv_tile(j, sh, mix):
        r = sh[:, 0:2, :]
        kx = sh[:, 2:4, :]
        v = sh[:, 4:6, :]
        k.act(View(lorab.t[0:64, :], lorab.tk), View(sh.t[0:64, 6, :], sh.tk), AF.Tanh)
        k.cp("act", View(lorab.t[64:128, :], lorab.tk), View(sh.t[64:128, 6, :], sh.tk))
        k.act(sgd[:, 0, :], sh[:, 7, :], AF.Sigmoid)
        k.act(View(sgd.t[0:32, 1, :], sgd.tk), View(sh.t[0:32, 8, :], sh.tk), AF.Sigmoid)
        psZ = ps_rot.next()
        items = []
        for c2 in range(2):
            items.append((psZ[:, c2 * 128:(c2 + 1) * 128], w2p[:, 0, c2 * 128:(c2 + 1) * 128], lorab[:], True, True))
            items.append((psZ[:, 256 + c2 * 128:256 + (c2 + 1) * 128], w2p[:, 1, c2 * 128:(c2 + 1) * 128], lorab[:], True, True))
        k.pe(items)
        for c2 in range(2):
            k.act(sw[:, c2, :], psZ[:, c2 * 128:(c2 + 1) * 128], AF.Sigmoid, bias=pvt[:, 9 + c2:10 + c2])
            k.act(aa[:, c2, :], psZ[:, 256 + c2 * 128:256 + (c2 + 1) * 128], AF.Sigmoid, bias=pvt[:, 11 + c2:12 + c2])
        psG = ps_rot.next()
        k.pe([(psG[:, 0:256], sgd[:, 0, :], g2b[:, 0, :], True, False),
              (psG[:, 0:256], sgd[:, 1, :], g2b[:, 1, :], False, True)])
        k.cp("act", gtm[:], psG[:, 0:256])
        for c2 in range(2):
            k.op("dve", lambda e, c2=c2: e.tensor_tensor_scan(out=cum.t[:, c2, :], data0=cs.t[:, 256:384], data1=sw.t[:, c2, :],
                                                           initial=0.0, op0=ALU.mult, op1=ALU.add), [cs.tk, sw.tk], [cum.tk])
        k.tt("pool", Ewx[:], cum[:], sw[:], ALU.subtract)
        k.act(Ew[:], cum[:], AF.Exp, scale=-C0)
        k.act(Ewx[:], Ewx[:], AF.Exp, scale=-C0)
        k.act(Einv[:], cum[:], AF.Exp, scale=C0)
        k.ts("dve", bC[:], View(cum.t[:, :, 63:128:64], cum.tk), -C0, ALU.mult)
        k.act(WC[:], bC[:], AF.Exp)
        for c2 in range(2):
            for ch in range(2):
                k.act(Eend[:, c2, ch * 64:(ch + 1) * 64], cum[:, c2, ch * 64:(ch + 1) * 64], AF.Exp, scale=C0, bias=bC[:, c2, ch:ch + 1])
        for c2 in range(2):
            k.act(kk[:, c2, :], sh[:, 2 + c2, :], AF.Copy, scale=pvt[:, 13 + c2:14 + c2])
        k.tt("pool", kq[:], kk[:], kk[:], ALU.mult)
        psN = ps_rot.next()
        k.pe([(psN[:, c2 * 128:(c2 + 1) * 128], cs[:, 0:128], kq[:, c2, :], True, True) for c2 in range(2)])
        k.act(rn[:], View(psN.t[:, 0:256].rearrange("p (a b) -> p a b", a=2), psN.tk), AF.Sqrt)
        k.ts("dve", rn[:], rn[:], 1e-12, ALU.max)
        k.recip(rn[:], rn[:])
        k.tt("pool", kk[:], kk[:], rn[:], ALU.mult)
        for c2 in range(2):
            k.ts("dve", k2[:, c2, :], aa[:, c2, :], pvt[:, 15 + c2:16 + c2], ALU.mult, omka[:, c2:c2 + 1], ALU.add)
        k.tt("pool", k2[:], kx, k2[:], ALU.mult)
        k.tt("pool", bvec[:], kk[:], aa[:], ALU.mult)

        def c4(buf):
            return buf[:, :, :]

        k.stt(FT[:, 2, :, :], kk[:], -1.0, Ewx[:], ALU.mult, ALU.mult)
        k.tt("dve", RT[:], r, Ew[:], ALU.mult)
        k.tt("pool", BK[:, :, 0, :], bvec[:], Einv[:], ALU.mult)
        k.tt("dve", BK[:, :, 1, :], k2[:], Einv[:], ALU.mult)
        k.tt("pool", FT[:, 0, :, :], bvec[:], Eend[:], ALU.mult)
        k.tt("dve", FT[:, 1, :, :], k2[:], Eend[:], ALU.mult)
        k.cp("act", FT[:, 3, :, :], v)
        for e in range(2):
            pb = 64 * e
            k.cp("act", View(ARbd.t[pb:pb + 64, :, e, 0, :], ARbd.tk), View(FT.t[pb:pb + 64, 2, :, :], FT.tk))
            k.cp("act", View(ARbd.t[pb:pb + 64, :, e, 1, :], ARbd.tk), View(RT.t[pb:pb + 64, :, :], RT.tk))
            k.cp("act", View(Bbd.t[pb:pb + 64, :, e, :], Bbd.tk), View(BK.t[pb:pb + 64, :, 0, :], BK.tk))
        k.tt("pool", kq[:], r, k2[:], ALU.mult)
        for c2 in range(2):
            k.act(rk[:, c2, :], kq[:, c2, :], AF.Copy, scale=pvt[:, 17 + c2:18 + c2])
        psT = ps_rot.next()
        psT16 = psT.same(BF16)
        k.pe([("T", psT16[:, (kind * 2 + c2) * 128:(kind * 2 + c2 + 1) * 128], FT[:, kind, c2, :], identb[:])
              for kind in range(4) for c2 in range(2)])
        psTv = psT.t.ap().bitcast(BF16)[:, 0:1024].rearrange("p (a b c) -> p a b c", a=4, b=2)
        k.cp("act", TM[:], View(psTv, psT.tk))
        for ch in range(2):
            pt = 64 * ch
            k.cp("act", View(TMbd.t[pt:pt + 64, :, :, ch, :], TMbd.tk), View(psTv[pt:pt + 64, 0:2, :, :], psT.tk))
        UL = ULr.next()
        mk1 = View(mkb.t[:, 0:512].rearrange("p (a b c) -> p a b c", a=2, b=2), mkb.tk)
        for c2 in range(2):
            arv = View(ARbd.t[:, c2, :, :, :].rearrange("p a b c -> p (a b c)"), ARbd.tk)
            for kind in range(2):
                psg = ps_rot.next()
                k.pe([(psg[:, :], BK[:, c2, kind, :], arv, True, True)])
                k.tt("dve", S4[:, 2 * c2:2 * c2 + 2, 2 * kind:2 * kind + 2, :],
                     View(psg.t[:, :].rearrange("p (a b c) -> p a b c", a=2, b=2), psg.tk), mk1, ALU.mult)
        psLo = ps_rot.next()
        k.pe([(psLo[:, c2 * 256:(c2 + 1) * 256], FT[:, 2, c2, :], View(Bbd.t[:, c2, :, :].rearrange("p a b -> p (a b)"), Bbd.tk), True, True)
              for c2 in range(2)])
        k.tt("dve", UL[:, :, 1, :], View(psLo.t[:, :].rearrange("p (h q) -> p h q", h=4), psLo.tk),
             View(mkb.t[:, 512:1024].rearrange("p (h q) -> p h q", h=4), mkb.tk), ALU.mult)
        k.cp("act", UL[:, :, 0, :], S4[:, :, 0, :])
        Y = Yr.next()
        psY = ps_rot.next()
        k.pe([(psY[:, hh * 64:(hh + 1) * 64], S4[:, hh, 2, :], TM[:, 3, c2, 64 * e:64 * e + 64], True, True)
              for hh, (c2, e) in enumerate(heads)])
        k.cp("act", Y[:, :, 0:64], View(TM.t[:, 2, :, :].rearrange("p a (e q) -> p (a e) q", e=2), TM.tk))
        k.cp("dve", Y[:, :, 64:128], View(psY.t[:, 0:256].rearrange("p (h q) -> p h q", h=4), psY.tk))
        for lvl in range(6):
            psY = ps_rot.next()
            k.pe([(psY[:, hh * 128:(hh + 1) * 128], UL[:, hh, 0, :], Y[:, hh, :], True, True) for hh in range(4)])
            if lvl < 5:
                psU = [ps_rot.next(), ps_rot.next()]
                items = []
                for hh in range(4):
                    pu = psU[hh // 2]
                    o0 = (hh % 2) * 256
                    items.append((pu[:, o0:o0 + 128], UL[:, hh, 1, :], UL[:, hh, 0, :], True, True))
                    if lvl < 4:
                        items.append((pu[:, o0 + 128:o0 + 256], UL[:, hh, 0, :], UL[:, hh, 1, :], True, True))
                k.pe(items)
            Yn = Yr.next()
            k.tt("dve", Yn[:], View(psY.t[:, :].rearrange("p (h q) -> p h q", h=4), psY.tk), Y[:], ALU.add)
            Y = Yn
            if lvl < 5:
                UL = ULr.next()
                for half in range(2):
                    k.cp("act", UL[:, 2 * half:2 * half + 2, :, :], View(psU[half].t[:, :].rearrange("p (h a q) -> p h a q", h=2, a=2), psU[half].tk))
        psP = ps_rot.next()
        items = []
        for hh, (c2, e) in enumerate(heads):
            pbk = 64 * e
            m1 = Y[:, hh, 0:64]
            items.append((View(psP.t[pbk:pbk + 64, c2 * 256:c2 * 256 + 128], psP.tk), m1,
                          View(TMbd.t[:, 0, c2, :, pbk:pbk + 64], TMbd.tk), True, True))
            items.append((View(psP.t[pbk:pbk + 64, c2 * 256 + 128:c2 * 256 + 256], psP.tk), m1, S4[:, hh, 1, :], True, True))
        k.pe(items)
        ppv = psP.t[:, :].rearrange("p (a x) -> p a x", a=2)
        for e in range(2):
            pbk = 64 * e
            k.cp("dve", View(PTbd.t[pbk:pbk + 64, :, :, pbk:pbk + 64], PTbd.tk),
                 View(ppv[pbk:pbk + 64, :, 0:128].rearrange("p a (c q) -> p a c q", c=2), psP.tk))
        k.tt("dve", GTs[:], View(ppv[:, :, 128:256].rearrange("p a (c q) -> p a c q", c=2), psP.tk),
             View(RT.t[:, :, :].rearrange("p a (c q) -> p a c q", c=2), RT.tk), ALU.add)
        psBn = ps_rot.next()
        k.pe([(psBn[:, 2 * c2:2 * c2 + 2], rk[:, c2, :], bones[:], True, True) for c2 in range(2)])
        k.cp("act", bsum[:], psBn[:, 0:4])
        psYo = ps_rot.next()
        for ch in range(2):
            pt = 64 * ch
            psH = ps_rot.next()
            items = []
            for c2 in range(2):
                items.append((View(psYo.t[pt:pt + 64, c2 * 128:(c2 + 1) * 128], psYo.tk), GTs[:, c2, ch, :],
                              View(Hbd.t[:, c2, :, :].rearrange("p a b -> p (a b)"), Hbd.tk), True, False))
                for e in range(2):
                    hh = 2 * c2 + e
                    yo = View(psYo.t[pt:pt + 64, hh * 64:(hh + 1) * 64], psYo.tk)
                    items.append((yo, S4[:, hh, 1, pt:pt + 64], Y[:, hh, 64:128], False, False))
                    items.append((yo, S4[:, hh, 3, pt:pt + 64], TM[:, 3, c2, 64 * e:64 * e + 64], False, e == 1))
            for c2 in range(2):
                items.append((psH[:, c2 * 64:(c2 + 1) * 64], PTbd[:, c2, ch, :], Hb[:, c2, :], True, False))
                for e in range(2):
                    hh = 2 * c2 + e
                    pbk = 64 * e
                    ho = View(psH.t[pbk:pbk + 64, c2 * 64:(c2 + 1) * 64], psH.tk)
                    items.append((ho, TMbd[:, 0, c2, ch, pbk:pbk + 64], Y[:, hh, 64:128], False, False))
                    items.append((ho, TMbd[:, 1, c2, ch, pbk:pbk + 64], TM[:, 3, c2, pbk:pbk + 64], False, True))
            k.pe(items)
            for c2 in range(2):
                k.stt(H32[:, c2, :], H32[:, c2, :], WC[:, c2, ch:ch + 1], psH[:, c2 * 64:(c2 + 1) * 64], ALU.mult, ALU.add)
            k.cp("act", Hb[:], H32[:])
            for e in range(2):
                pbk = 64 * e
                k.cp("act", View(Hbd.t[pbk:pbk + 64, :, e, :], Hbd.tk), View(H32.t[pbk:pbk + 64, :, :], H32.tk))
        if j == 0:
            return
        k.cp("act", yv[:], View(psYo.t[:, 0:256].rearrange("p (h q) -> p h q", h=4), psYo.tk))
        k.red(gst[:, 0:4], yv[:])
        k.tt("pool", ytmp[:], yv[:], yv[:], ALU.mult)
        k.red(gst[:, 4:8], ytmp[:])
        k.ts("dve", gst[:, 0:8], gst[:, 0:8], 1.0 / 64, ALU.mult)
        k.tt("dve", gst[:, 8:12], gst[:, 0:4], gst[:, 0:4], ALU.mult)
        k.tt("dve", gst[:, 8:12], gst[:, 4:8], gst[:, 8:12], ALU.subtract)
        k.act(gst[:, 8:12], gst[:, 8:12], AF.Sqrt, bias=64e-5)
        k.recip(gst[:, 12:16], gst[:, 8:12])
        for hh in range(4):
            k.ts("dve", ytmp[:, hh, :], yv[:, hh, :], gst[:, hh:hh + 1], ALU.subtract, gst[:, 12 + hh:13 + hh], ALU.mult)
        yt2 = View(ytmp.t[:, :, :].rearrange("p h q -> p (h q)"), ytmp.tk)
        k.tt("pool", yt2, yt2, lnw_t[:, 0, :], ALU.mult)
        k.tt("pool", yt2, yt2, lnw_t[:, 1, :], ALU.add)
        for hh, (c2, e) in enumerate(heads):
            k.stt(ytmp[:, hh, :], TM[:, 3, c2, 64 * e:64 * e + 64], bsum[:, hh:hh + 1], ytmp[:, hh, :], ALU.mult, ALU.add)
        k.tt("dve", mix[:, 0:256], yt2, gtm[:], ALU.mult)

    def outproj_tile(j, mix):
        psT = ps_rot.next()
        psT16 = psT.same(BF16)
        k.pe([("T", psT16[:, c * 128:(c + 1) * 128], mix[:, c * 128:(c + 1) * 128], identb[:]) for c in range(4)])
        mixT = mixT_r.next()
        k.cp("act", mixT[:], View(psT.t.ap().bitcast(BF16)[:, 0:512].rearrange("p (a b) -> p a b", a=4), psT.tk))
        po, spo = po_r.next()
        for br in range(2):
            for half in range(2):
                ps = ps_rot.next()
                k.pe([(ps[:, :], mixT[:, 2 * br + kc, :], wob[:, 2 * br + kc, half * 512:(half + 1) * 512], kc == 0, kc == 1) for kc in range(2)])
                k.cp("act" if half == 0 else "dve", po[:, br * 1024 + half * 512:br * 1024 + (half + 1) * 512], ps[:, :])
        i = j - 1
        k.dma("sp", View(rs_in[i // 16].t[(i % 16) * 128:(i % 16 + 1) * 128, :], rs_tk[i // 16][i % 16]), po[:], spo)
        if dbg:
            dm, sdm = dmix_r.next()
            k.cp("pool", dm[:], mix[:])
            k.dma("sp", d_mix[j * 128:(j + 1) * 128, :], dm[:], sdm)

    cc_sem = es.enter_context(nc.semaphore("cc_sem"))
    k.engs["cc"] = EngS("cc", None, cc_sem)
    n_cc = 0
    def issue_rs(j):
        nonlocal n_cc
        if do_rs and j >= 1 and j % 16 == 0:
            q = j // 16 - 1
            E = k.engs["pool"]
            k._wait(E, rs_tk[q], [rs_out[q].tk])
            nc.gpsimd.collective_compute("ReduceScatter", ALU.add, replica_groups=[[0, 1, 2, 3], [4, 5, 6, 7]],
                                         ins=[rs_in[q].t.ap().opt()], outs=[rs_out[q].t.ap().opt()]).then_inc(cc_sem)
            n_cc += 1
            for t_ in rs_tk[q]:
                t_.rd["cc"] = n_cc
            rs_out[q].tk.lw = ("cc", n_cc)
            rs_out[q].tk.rd = {}

    tiles = [{"j": j} for j in range(NT1)]

    def P(t):
        t["sh"], t["QT"] = p1_tile(t["j"])

    def R(t):
        t["mix"] = mix_r.next()
        rwkv_tile(t["j"], t["sh"], t["mix"])

    def Y(t):
        attn_tile(t["j"], t["QT"], t["mix"])
        outproj_tile(t["j"], t["mix"])

    lenP, lenR = 40, 230
    if NT1 > 0:
        n0 = k.nops
        P(tiles[0])
        lenP = k.nops - n0
    for j in range(NT1):
        fns, ests = [], []
        fns.append(lambda t=tiles[j]: R(t))
        ests.append(lenR)
        if j >= 2:
            jj = j - 1
            fns.append(lambda t=tiles[jj]: Y(t))
            ests.append(4 * (3 * ((jj + 4) // 4) + 2) + 40)
        else:
            fns.append(lambda: None)
            ests.append(1)
        if j + 1 < NT1:
            fns.append(lambda t=tiles[j + 1]: P(t))
            ests.append(lenP)
        done = k.weave.run(fns, ests)
        lenR = max(1, done[0])
        if j >= 2:
            issue_rs(j - 1)
    if NT1 >= 2:
        Y(tiles[NT1 - 1])
        issue_rs(NT1 - 1)

    k.barrier()
    es1.close()
    if do_p2:
        src = rs_dbg if rs_dbg is not None else rs_out
        esA = ExitStack()
        k.es = es
        h2 = k.sb([128, 16, D], F32, "h2")
        k.es = esA
        wgb = k.sb([128, 8, 2048], BF16, "wgb")
        woutb = k.sb([128, 8, D], BF16, "woutb")
        stgA = Rot([(k.sb([128, D], F32, f"stgA{i}"), k.dsem()) for i in range(4)])
        for kc in range(8):
            for hf in range(2):
                k.ldcast(wgb[:, kc, hf * D:(hf + 1) * D], wg[kc * 128:(kc + 1) * 128, hf * D:(hf + 1) * D], stgA, "pool" if hf == 0 else "dve")
        for kc in range(8):
            k.ldcast(woutb[:, kc, :], wout[kc * 128:(kc + 1) * 128, :], stgA, "pool" if kc % 2 == 0 else "dve")
        nw2 = k.sb([128, D], F32, "nw2")
        k.dma("sp", nw2[:], nwb[:, :], k.dsem())
        x2_r = Rot([(k.sb([128, 4, D], F32, f"x2t{i}"), k.dsem()) for i in range(1)])
        rs_r = Rot([(k.sb([128, 4, 2048], BF16, f"rst{i}"), k.dsem()) for i in range(1)])
        st2 = k.sb([128, 8], F32, "st2")
        xs2 = k.sb([128, D], BF16, "xs2")
        hT2 = k.sb([128, 8, 512], BF16, "hT2")
        gate = k.sb([128, 2048], BF16, "gate")
        mrg = k.sb([128, D], BF16, "mrg")
        mrg2 = k.sb([128, D], BF16, "mrg2")
        mT = k.sb([128, 8, 128], BF16, "mT")
        gateB = k.sb([128, 2048], BF16, "gateB")
        mrgB = k.sb([128, D], BF16, "mrgB")
        mrg2B = k.sb([128, D], BF16, "mrg2B")
        mTB = k.sb([128, 8, 128], BF16, "mTB")

        def norm_T(xin, nwt, xs_, st_, dstT, col0):
            k.act(xs_[:], xin, AF.Square, accum=st_[:, 0:1])
            k.act(st_[:, 1:2], st_[:, 0:1], AF.Sqrt, bias=1e-5, scale=1.0 / D)
            k.recip(st_[:, 2:3], st_[:, 1:2])
            k.stt(xs_[:], xin, st_[:, 2:3], nwt, ALU.mult, ALU.mult)
            ps = ps_rot.next()
            ps16 = ps.same(BF16)
            k.pe([("T", ps16[:, kc * 128:(kc + 1) * 128], xs_[:, kc * 128:(kc + 1) * 128], identb[:]) for kc in range(8)])
            k.cp("act", View(dstT.t[:, :, col0:col0 + 128], dstT.tk),
                 View(ps.t.ap().bitcast(BF16)[:, 0:1024].rearrange("p (a b) -> p a b", a=8), ps.tk))

        for kq in range(4):
            x2t, sx2 = x2_r.next()
            k.dma("sp", x2t[:], View(x2.t[kq].rearrange("(s p) d -> p s d", p=128), x2.tk), sx2)
            rst, srs = rs_r.next()
            k.dma("sp", rst[:], View(src[kq].t.ap().rearrange("(s p) d -> p s d", p=128), src[kq].tk), srs)
            for s_ in range(4):
                norm_T(x2t[:, s_, :], nw2[:], xs2, st2, hT2, s_ * 128)
            def sub2a(s_, gate, mrg, mrg2, mT, kq=kq, x2t=x2t, rst=rst):
                for cb in range(4):
                    ps = ps_rot.next()
                    k.pe([(ps[:, :], hT2[:, kc, s_ * 128:(s_ + 1) * 128], wgb[:, kc, cb * 512:(cb + 1) * 512], kc == 0, kc == 7) for kc in range(8)])
                    k.act(gate[:, cb * 512:(cb + 1) * 512], ps[:, :], AF.Sigmoid)
                k.tt("pool", mrg[:], gate[:, 0:D], rst[:, s_, 0:D], ALU.mult)
                k.tt("dve", mrg2[:], gate[:, D:2 * D], rst[:, s_, D:2 * D], ALU.mult)
                k.tt("pool", mrg[:], mrg[:], mrg2[:], ALU.add)
                ps = ps_rot.next()
                ps16 = ps.same(BF16)
                k.pe([("T", ps16[:, kc * 128:(kc + 1) * 128], mrg[:, kc * 128:(kc + 1) * 128], identb[:]) for kc in range(8)])
                k.cp("act", mT[:], View(ps.t.ap().bitcast(BF16)[:, 0:1024].rearrange("p (a b) -> p a b", a=8), ps.tk))
                for cb in range(2):
                    ps = ps_rot.next()
                    k.pe([(ps[:, :], mT[:, kc, :], woutb[:, kc, cb * 512:(cb + 1) * 512], kc == 0, kc == 7) for kc in range(8)])
                    k.tt("dve", h2[:, kq * 4 + s_, cb * 512:(cb + 1) * 512], ps[:, :], x2t[:, s_, cb * 512:(cb + 1) * 512], ALU.add)

            for s0 in (0, 2):
                k.weave.run([lambda: sub2a(s0, gate, mrg, mrg2, mT), lambda: sub2a(s0 + 1, gateB, mrgB, mrg2B, mTB)], [1, 1])
        k.barrier()
        esA.close()
        esB = ExitStack()
        k.es = esB
        nmt = k.sb([128, 2, D], F32, "nmt")
        k.dma("sp", nmt[:], nm2[:, :, :], k.dsem())
        hnT = k.sb([128, 8, 2048], BF16, "hnT")
        xs3 = k.sb([128, D], BF16, "xs3")
        st3 = k.sb([128, 8], F32, "st3")
        esC = ExitStack()
        k.es = esC
        wq_r = Rot([(k.sb([128, 8, D], BF16, f"w1q{i}"), k.sb([128, 8, D], BF16, f"w2q{i}")) for i in range(2)])
        aT_r = rot(2, [128, 8, 512], BF16, "aT")
        rl_r = rot(2, [128, 512], BF16, "rl")
        stgB = Rot([(k.sb([128, D], F32, f"stgB{i}"), k.dsem()) for i in range(4)])

        def load_q(p):
            w1q, w2q = wq_r.next()
            for kc in range(8):
                k.ldcast(w1q[:, kc, :], mw1[kc * 128:(kc + 1) * 128, p * D:(p + 1) * D], stgB, "pool")
            for fc in range(8):
                k.ldcast(w2q[:, fc, :], mw2[p * D + fc * 128:p * D + (fc + 1) * 128, :], stgB, "pool")
            return w1q, w2q

        nxt = load_q(0)
        for t16 in range(16):
            norm_T(h2[:, t16, :], nmt[:, 0, :], xs3, st3, hnT, t16 * 128)
        for p in range(4):
            w1q, w2q = nxt
            if p < 3:
                nxt = load_q(p + 1)
            def up(kq, aT, w1q=w1q):
                for fc in range(8):
                    ps = ps_rot.next()
                    k.pe([(ps[:, :], w1q[:, kc, fc * 128:(fc + 1) * 128], hnT[:, kc, kq * 512:(kq + 1) * 512], kc == 0, kc == 7) for kc in range(8)])
                    rl = rl_r.next()
                    k.act(rl[:], ps[:, :], AF.Relu)
                    k.tt("pool", aT[:, fc, :], rl[:], rl[:], ALU.mult)

            def down(kq, aT, w2q=w2q):
                for s_ in range(4):
                    for cb in range(2):
                        ps = ps_rot.next()
                        k.pe([(ps[:, :], aT[:, fc, s_ * 128:(s_ + 1) * 128], w2q[:, fc, cb * 512:(cb + 1) * 512], fc == 0, fc == 7) for fc in range(8)])
                        hv = h2[:, kq * 4 + s_, cb * 512:(cb + 1) * 512]
                        k.tt("dve", hv, ps[:, :], hv, ALU.add)

            aTs = [aT_r.next() for _ in range(4)]
            up(0, aTs[0])
            for kq in range(4):
                if kq < 3:
                    k.weave.run([lambda: down(kq, aTs[kq]), lambda: up(kq + 1, aTs[kq + 1])], [1, 1])
                else:
                    down(kq, aTs[kq])
        k.barrier()
        esC.close()
        k.es = esB
        yo_r = Rot([(k.sb([128, D], F32, f"yo{i}"), k.dsem()) for i in range(2)])
        for t16 in range(16):
            yo, syo = yo_r.next()
            hv = h2[:, t16, :]
            k.act(yo[:], hv, AF.Square, accum=st3[:, 0:1])
            k.act(st3[:, 1:2], st3[:, 0:1], AF.Sqrt, bias=1e-5, scale=1.0 / D)
            k.recip(st3[:, 2:3], st3[:, 1:2])
            k.stt(yo[:], hv, st3[:, 2:3], nmt[:, 1, :], ALU.mult, ALU.mult)
            k.dma("sp", yout[t16 // 4, (t16 % 4) * 128:(t16 % 4 + 1) * 128, :], yo[:], syo)
        k.barrier()
        esB.close()
    k.barrier()
    print("total ops", k.nops, {n: e.count for n, e in k.engs.items() if e.count})
    return nc


def make_consts():
    c = np.zeros((128, 1024), np.float32)
    p = np.arange(128)[:, None]
    q = np.arange(128)[None, :]
    c[:, 0:128] = (p == q)
    c[:, 128:256] = (p <= q)
    c[:, 256:384] = (p // 64 == q // 64)
    pm = np.zeros((128, 128), np.float32)
    for hb in (0, 64):
        for d in range(8):
            pm[hb + d + 8, hb + d] = -1.0
            pm[hb + d, hb + d + 8] = 1.0
    c[:, 384:512] = pm
    c[:, 512:576] = (p % 64 == np.arange(64)[None, :])
    s = (np.arange(128) % 64)[:, None]
    t = np.arange(64)[None, :]
    c[:, 576:640] = (s < t)
    c[:, 640:704] = (s <= t)
    c[:, 704:768] = (s < t)
    c[:, 768:832] = (s <= t)
    c[:, 832:896] = (s > t)
    c[:, 896:1024] = 1.0
    c[:, 896] = 0.0
    c[:, 960] = 0.0
    return c


def make_consts2():
    c = np.zeros((128, 1024), np.float32)
    p = np.arange(128)[:, None]
    q = np.arange(128)[None, :]
    same = (p // 64 == q // 64)
    strict = same & ((p % 64) < (q % 64))
    incl = same & ((p % 64) <= (q % 64))
    lo = same & ((p % 64) > (q % 64))
    for e in range(2):
        c[:, e * 256:e * 256 + 128] = strict
        c[:, e * 256 + 128:e * 256 + 256] = incl
    for h in range(4):
        c[:, 512 + h * 128:512 + (h + 1) * 128] = lo
    return c


def make_rope():
    half = 8
    inv = (np.float32(500000.0) ** (-(np.arange(0, 16, 2, dtype=np.float32)) / np.float32(16))).astype(np.float32)
    pos = np.maximum(np.arange(LP) - 112, 0).astype(np.float32)
    ang = pos[:, None] * inv[None, :]
    cos = np.cos(ang).astype(np.float32)
    sin = np.sin(ang).astype(np.float32)
    ct = np.ones((128, LP), np.float32)
    stb = np.zeros((128, LP), np.float32)
    for hb in (0, 64):
        for d in range(16):
            ct[hb + d] = cos[:, d % 8]
            stb[hb + d] = sin[:, d % 8]
    r = np.stack([ct.reshape(128, NT, 128), stb.reshape(128, NT, 128)], axis=2)
    return np.ascontiguousarray(r)


def core_inputs(inp, c, consts, rope):
    b, g = c // 4, c % 4
    f = lambda a: np.ascontiguousarray(np.asarray(a, dtype=np.float32))
    w_in = inp["w_in"][0]
    sl = slice(256 * g, 256 * g + 256)
    cols = np.concatenate([np.arange(256 * g, 256 * g + 256), 1024 + np.arange(256 * g, 256 * g + 256),
                           2048 + np.arange(256 * g, 256 * g + 256), np.arange(3072, 3360),
                           3360 + np.arange(256 * g, 256 * g + 256), 4384 + np.arange(256 * g, 256 * g + 256),
                           5408 + np.arange(256 * g, 256 * g + 256)])
    mu = inp["rwkv_mu"][0]
    pv = np.zeros((128, 32), np.float32)
    mucols = [mu[0 + 256 * g:0 + 256 * g + 128], mu[256 * g + 128:256 * g + 256],
              mu[1024 + 256 * g:1024 + 256 * g + 128], mu[1024 + 256 * g + 128:1024 + 256 * g + 256],
              mu[2048 + 256 * g:2048 + 256 * g + 128], mu[2048 + 256 * g + 128:2048 + 256 * g + 256],
              mu[3072:3200], mu[3200:3328]]
    for i, m in enumerate(mucols):
        pv[:, i] = m
    pv[0:32, 8] = mu[3328:3360]
    for i, nm in enumerate(["rwkv_w0", "rwkv_a0", "rwkv_k_k", "rwkv_k_a"]):
        vv = inp[nm][0][sl]
        pv[:, 9 + 2 * i] = vv[0:128]
        pv[:, 10 + 2 * i] = vv[128:256]
    rkv = inp["rwkv_r_k"][0].reshape(-1)[sl]
    pv[:, 17] = rkv[0:128]
    pv[:, 18] = rkv[128:256]
    q = g
    x2 = np.stack([inp["x"][b, 2048 * kk + 512 * q:2048 * kk + 512 * q + 512] for kk in range(4)], 0)
    d = {
        "xb": f(inp["x"][b]),
        "meta": f(inp["meta_tokens"]),
        "x2": f(x2),
        "w1c": f(w_in[:, cols]),
        "wg": f(w_in[:, 6432:8480]),
        "pv": pv,
        "nwb": f(np.broadcast_to(inp["norm_mix_w"][0][None, :], (128, D))),
        "lnwb": f(np.broadcast_to(np.stack([inp["rwkv_ln_w"][0][sl], inp["rwkv_ln_b"][0][sl]], 0)[None], (128, 2, 256))),
        "sublnb": f(np.broadcast_to(inp["diff_subln_w"][0][None, :], (128, 128))),
        "lamv": f(np.broadcast_to(np.stack([inp["diff_lq1"][0], inp["diff_lk1"][0], inp["diff_lq2"][0], inp["diff_lk2"][0]], 0)[None], (128, 4, 64))),
        "w2a2": f(np.concatenate([inp["rwkv_w2"][0][:, sl], inp["rwkv_a2"][0][:, sl]], 0)),
        "g2": f(inp["rwkv_g2"][0][:, sl]),
        "wor": f(inp["rwkv_w_o"][0][sl, :]),
        "wod": f(inp["diff_w_o"][0][sl, :]),
        "wout": f(inp["w_out"][0]),
        "mw1": f(inp["mlp_w1"][0]),
        "mw2": f(inp["mlp_w2"][0]),
        "nm2": f(np.broadcast_to(np.stack([inp["norm_mlp_w"][0], inp["final_norm_w"]], 0)[None], (128, 2, D))),
        "rope": rope,
        "cst": consts,
        "cst2": make_consts2(),
    }
    return d


_CACHE = {}


def kernel(**inputs):
    inp = {kk: np.asarray(v) for kk, v in inputs.items()}
    if "nc" not in _CACHE:
        _CACHE["nc"] = build({})
    nc = _CACHE["nc"]
    consts = make_consts()
    rope = make_rope()
    in_maps = [core_inputs(inp, c, consts, rope) for c in range(8)]
    res = run_bass_kernel_spmd(nc, in_maps, core_ids=list(range(8)))
    out = np.zeros((2, 8192, D), np.float32)
    for c in range(8):
        b, q = c // 4, c % 4
        y = np.asarray(res.results[c]["yout"])
        for kk in range(4):
            out[b, 2048 * kk + 512 * q:2048 * kk + 512 * q + 512] = y[kk]
    return out
```

```python
import math
import threading
from contextlib import ExitStack
import numpy as np
import ml_dtypes
import concourse.bass as bass
import concourse.mybir as mybir
from concourse.bass_utils import run_bass_kernel_spmd

F32 = mybir.dt.float32
BF16 = mybir.dt.bfloat16
AF = mybir.ActivationFunctionType
ALU = mybir.AluOpType
AX = mybir.AxisListType

D = 1024
NT = 65
LP = NT * 128
C0 = math.exp(-0.5)
LAMBDA_INIT = 0.8 - 0.6 * math.exp(0.0)


class Tk:
    __slots__ = ("lw", "rd", "name", "psum")

    def __init__(self, name=""):
        self.lw = None
        self.rd = {}
        self.name = name
        self.psum = False


class View:
    __slots__ = ("ap", "tk")

    def __init__(self, ap, tk):
        self.ap = ap
        self.tk = tk


class Buf:
    def __init__(self, t, name, tk=None, dt=None):
        self.t = t
        self.name = name
        self.tk = tk if tk is not None else Tk(name)
        self.dt = dt

    def __getitem__(self, idx):
        ap = self.t[idx] if self.dt is None else self.t.ap().bitcast(self.dt)[idx]
        return View(ap, self.tk)

    def alias(self, name, dt=None):
        return Buf(self.t, name, dt=dt if dt is not None else self.dt)

    def same(self, dt):
        return Buf(self.t, self.name, tk=self.tk, dt=dt)


class EngS:
    def __init__(self, name, eng, sem):
        self.name = name
        self.eng = eng
        self.sem = sem
        self.count = 0
        self.waited = {}


class K:
    def __init__(self, nc, es):
        self.nc = nc
        self.es = es
        self.es_sem = es
        self.engs = {}
        for n, e in (("pe", nc.tensor), ("act", nc.scalar), ("dve", nc.vector), ("pool", nc.gpsimd), ("sp", nc.sync)):
            self.engs[n] = EngS(n, e, es.enter_context(nc.semaphore("sem_" + n)))
        self.nd = 0
        self.nsb = 0
        self.nops = 0
        self.limit = 10 ** 9
        self.weave = Weave()

    def sb(self, shape, dt, name=None):
        self.nsb += 1
        name = name or f"sb{self.nsb}"
        return Buf(self.es.enter_context(self.nc.sbuf_tensor(name, list(shape), dt)), name)

    def dsem(self, name=None):
        self.nd += 1
        name = name or f"dsem{self.nd}"
        e = EngS(name, None, self.es_sem.enter_context(self.nc.semaphore(name)))
        self.engs[name] = e
        return e

    def _wait(self, E, reads, writes):
        deps = {}
        for t in reads:
            if t.lw is not None and deps.get(t.lw[0], 0) < t.lw[1]:
                deps[t.lw[0]] = t.lw[1]
            if t.psum:
                for n, c in t.rd.items():
                    if n != E.name and deps.get(n, 0) < c:
                        deps[n] = c
        for t in writes:
            if t.lw is not None and deps.get(t.lw[0], 0) < t.lw[1]:
                deps[t.lw[0]] = t.lw[1]
            for n, c in t.rd.items():
                if deps.get(n, 0) < c:
                    deps[n] = c
        for n, c in deps.items():
            if n == E.name and n == "pe":
                continue
            if E.waited.get(n, 0) >= c:
                continue
            E.eng.wait_ge(self.engs[n].sem, c)
            E.waited[n] = c

    def op(self, en, fn, reads, writes):
        self.weave.checkpoint()
        self.nops += 1
        if self.nops > self.limit:
            return
        E = self.engs[en]
        self._wait(E, reads, writes)
        ins = fn(E.eng)
        E.count += 1
        ins.then_inc(E.sem, 1)
        for t in reads:
            t.rd[en] = E.count
        for t in writes:
            t.lw = (en, E.count)
            t.rd = {}

    def dma(self, qn, out, in_, dsem, **kw):
        self.weave.checkpoint()
        self.nops += 1
        if self.nops > self.limit:
            return
        Q = self.engs[qn]
        self._wait(Q, [in_.tk], [out.tk])
        ins = Q.eng.dma_start(out=out.ap, in_=in_.ap, **kw)
        dsem.count += 16
        ins.then_inc(dsem.sem, 16)
        in_.tk.rd[dsem.name] = dsem.count
        out.tk.lw = (dsem.name, dsem.count)
        out.tk.rd = {}

    def barrier(self):
        for en in ("sp", "pool", "act", "dve", "pe"):
            E = self.engs[en]
            for n, o in self.engs.items():
                if n == en or o.count == 0:
                    continue
                if E.waited.get(n, 0) < o.count:
                    E.eng.wait_ge(o.sem, o.count)
                    E.waited[n] = o.count

    def ldcast(self, dst, src, stg_r, eng="pool"):
        stg, sem = stg_r.next()
        sv = View(stg.t[:, 0:dst.ap.shape[-1]], stg.tk)
        self.dma("sp", sv, src, sem)
        self.cp(eng, dst, sv)

    def tt(self, en, out, a, b, op):
        self.op(en, lambda e: e.tensor_tensor(out=out.ap, in0=a.ap, in1=b.ap, op=op), [a.tk, b.tk], [out.tk])

    def ts(self, en, out, a, s1, op0, s2=None, op1=None):
        rd = [a.tk]
        v1 = s1
        v2 = s2
        if isinstance(s1, View):
            rd.append(s1.tk)
            v1 = s1.ap
        if isinstance(s2, View):
            rd.append(s2.tk)
            v2 = s2.ap
        kw = {}
        if en == "pool" and op1 is None and s2 is None and op0 == ALU.mult:
            op1 = ALU.add
            v2 = 0.0
        if op1 is not None:
            kw["op1"] = op1
        self.op(en, lambda e: e.tensor_scalar(out=out.ap, in0=a.ap, scalar1=v1, scalar2=v2, op0=op0, **kw), rd, [out.tk])

    def stt(self, out, a, s, b, op0, op1):
        rd = [a.tk, b.tk]
        v = s
        if isinstance(s, View):
            rd.append(s.tk)
            v = s.ap
        self.op("dve", lambda e: e.scalar_tensor_tensor(out=out.ap, in0=a.ap, scalar=v, in1=b.ap, op0=op0, op1=op1), rd, [out.tk])

    def cp(self, en, out, a):
        if en == "act":
            self.op(en, lambda e: e.copy(out=out.ap, in_=a.ap), [a.tk], [out.tk])
        else:
            self.op(en, lambda e: e.tensor_copy(out=out.ap, in_=a.ap), [a.tk], [out.tk])

    def act(self, out, a, func, bias=None, scale=1.0, accum=None):
        rd = [a.tk]
        wr = [out.tk]
        kw = {}
        if isinstance(bias, View):
            rd.append(bias.tk)
            kw["bias"] = bias.ap
        elif bias is not None:
            kw["bias"] = bias
        if isinstance(scale, View):
            rd.append(scale.tk)
            kw["scale"] = scale.ap
        else:
            kw["scale"] = scale
        if accum is not None:
            wr.append(accum.tk)
            kw["accum_out"] = accum.ap
        self.op("act", lambda e: e.activation(out=out.ap, in_=a.ap, func=func, **kw), rd, wr)

    def memset(self, en, out, val):
        self.op(en, lambda e: e.memset(out.ap, val), [], [out.tk])

    def red(self, out, a, op=ALU.add):
        self.op("dve", lambda e: e.tensor_reduce(out=out.ap, in_=a.ap, op=op, axis=AX.X), [a.tk], [out.tk])

    def recip(self, out, a):
        self.op("dve", lambda e: e.reciprocal(out=out.ap, in_=a.ap), [a.tk], [out.tk])

    def pe(self, items):
        rd = []
        wr = []
        for it in items:
            if it[0] == "T":
                wr.append(it[1].tk)
                rd += [it[2].tk, it[3].tk]
            else:
                wr.append(it[0].tk)
                rd += [it[1].tk, it[2].tk]

        def fn(e):
            ins = None
            for it in items:
                if it[0] == "T":
                    ins = e.transpose(it[1].ap, it[2].ap, it[3].ap)
                else:
                    ins = e.matmul(it[0].ap, lhsT=it[1].ap, rhs=it[2].ap, start=it[3], stop=it[4])
            return ins

        self.op("pe", fn, rd, wr)


class Weave:
    def __init__(self):
        self.cv = threading.Condition()
        self.active = None
        self.tl = threading.local()

    def current(self):
        return getattr(self.tl, "sid", 0) if self.active is not None else None

    def _pick(self):
        best = None
        for i in range(len(self.alive)):
            if self.alive[i]:
                f = self.done[i] / self.est[i]
                if best is None or f < best[0]:
                    best = (f, i)
        self.active = best[1] if best is not None else -1

    def checkpoint(self):
        if self.active is None:
            return
        sid = self.tl.sid
        with self.cv:
            self.done[sid] += 1
            self._pick()
            self.cv.notify_all()
            while self.active != sid:
                self.cv.wait()

    def run(self, fns, ests):
        n = len(fns)
        self.done = [0] * n
        self.est = [max(1, e) for e in ests]
        self.alive = [True] * n
        self.err = []

        def body(i):
            self.tl.sid = i
            with self.cv:
                while self.active != i:
                    self.cv.wait()
            try:
                fns[i]()
            except BaseException as ex:
                self.err.append(ex)
            with self.cv:
                self.alive[i] = False
                self._pick()
                self.cv.notify_all()

        ths = [threading.Thread(target=body, args=(i,)) for i in range(n)]
        with self.cv:
            self.active = 0
        for t in ths:
            t.start()
        for t in ths:
            t.join()
        self.active = None
        if self.err:
            raise self.err[0]
        return list(self.done)


class RotSel:
    def __init__(self, weave, pools):
        self.weave = weave
        self.pools = pools

    def next(self):
        c = self.weave.current()
        return self.pools[-1 if c is None else c].next()


class Rot:
    def __init__(self, bufs):
        self.bufs = bufs
        self.i = 0

    def next(self):
        b = self.bufs[self.i % len(self.bufs)]
        self.i += 1
        return b


def build(cfg):
    NT1 = cfg.get("nt1", NT)
    do_p2 = cfg.get("p2", True)
    do_rs = cfg.get("rs", True)
    dbg = cfg.get("dbg", False)
    nc = bass.Bass("TRN2", target_bir_lowering=False)
    es = ExitStack()
    k = K(nc, es)
    k.limit = cfg.get("limit", 10 ** 9)

    def din(name, shape, dt=F32):
        return Buf(nc.dram_tensor(name, list(shape), dt, kind="ExternalInput"), name)

    def dout(name, shape, dt=F32):
        return Buf(nc.dram_tensor(name, list(shape), dt, kind="ExternalOutput"), name)

    xb = din("xb", [8192, D])
    meta = din("meta", [16, D])
    x2 = din("x2", [4, 512, D])
    w1c = din("w1c", [D, 1824])
    wg = din("wg", [D, 2048])
    pv = din("pv", [128, 32])
    nwb = din("nwb", [128, D])
    lnwb = din("lnwb", [128, 2, 256])
    sublnb = din("sublnb", [128, 128])
    lamv = din("lamv", [128, 4, 64])
    w2a2 = din("w2a2", [128, 256])
    g2 = din("g2", [160, 256])
    wor = din("wor", [256, D])
    wod = din("wod", [256, D])
    wout = din("wout", [D, D])
    mw1 = din("mw1", [D, 4096])
    mw2 = din("mw2", [4096, D])
    nm2 = din("nm2", [128, 2, D])
    rs_dbg = [din(f"rs_dbg{i}", [512, 2048], BF16) for i in range(4)] if cfg.get("p2only") else None
    rope = din("rope", [128, NT, 2, 128])
    cst = din("cst", [128, 1024])
    cst2 = din("cst2", [128, 1024])
    yout = dout("yout", [4, 512, D])
    rs_in = [Buf(nc.dram_tensor(f"rs_in{i}", [2048, 2048], BF16), f"rs_in{i}") for i in range(4)]
    rs_tk = [[Tk(f"rs{i}_{r}") for r in range(16)] for i in range(4)]
    rs_out = [Buf(nc.dram_tensor(f"rs_out{i}", [512, 2048], BF16), f"rs_out{i}") for i in range(4)]
    if dbg:
        d_mix = dout("d_mix", [NT1 * 128, 512])

    psb = [Buf(es.enter_context(nc.psum_tensor(f"ps{i}", [128, 512], F32)), f"ps{i}") for i in range(8)]
    for pb_ in psb:
        pb_.tk.psum = True
    ps_rot = RotSel(k.weave, [Rot(psb[0:3]), Rot(psb[3:6]), Rot(psb[6:7]), Rot(psb[0:7])])
    ps_o = [psb[7], psb[7]]

    mhalf = k.sb([128, 2], F32, "mhalf")
    k.memset("dve", mhalf[:], -0.5)
    identb = k.sb([128, 128], BF16, "identb")
    k.dma("pool", identb[:], cst[:, 0:128], k.dsem())
    es1 = ExitStack()
    k.es = es1
    cs = k.sb([128, 384], F32, "cs")
    sem_c = k.dsem("sem_c")
    k.dma("sp", cs[:, 0:256], cst[:, 256:512], sem_c)
    k.dma("sp", cs[:, 256:384], cst[:, 896:1024], sem_c)

    cmaskb = k.sb([128, 128], BF16, "cmaskb")
    k.dma("pool", cmaskb[:], cst[:, 128:256], k.dsem())
    mkb = k.sb([128, 1024], BF16, "mkb")
    k.dma("pool", mkb[:], cst2[:, :], k.dsem())
    bones = k.sb([128, 2], BF16, "bones")
    k.cp("dve", bones[:], cs[:, 0:128:64])
    pvt = k.sb([128, 32], F32, "pvt")
    k.dma("sp", pvt[:], pv[:, :], k.dsem())
    hb = k.sb([128, 4], F32, "hb")
    k.ts("dve", hb[:], pvt[:, 9:13], 0.5, ALU.mult)
    omka = k.sb([128, 2], F32, "omka")
    k.ts("dve", omka[:], pvt[:, 15:17], -1.0, ALU.mult, 1.0, ALU.add)
    nw_t = k.sb([128, D], BF16, "nw_t")
    k.dma("pool", nw_t[:], nwb[:, :], k.dsem())
    lnw_t = k.sb([128, 2, 256], F32, "lnw_t")
    k.dma("sp", lnw_t[:], lnwb[:, :, :], k.dsem())
    subln_t = k.sb([128, 128], F32, "subln_t")
    k.dma("sp", subln_t[:], sublnb[:, :], k.dsem())
    k.ts("dve", subln_t[:], subln_t[:], 1.0 - LAMBDA_INIT, ALU.mult)
    lam_in = k.sb([128, 4, 64], F32, "lam_in")
    k.dma("sp", lam_in[:], lamv[:, :, :], k.dsem())
    lam_t = k.sb([128, 4], F32, "lam_t")
    lam_j = k.sb([128, 64], F32, "lam_j")
    for i in range(2):
        k.tt("dve", lam_j[:], lam_in[:, 2 * i, :], lam_in[:, 2 * i + 1, :], ALU.mult)
        k.red(lam_t[:, i:i + 1], lam_j[:])
    k.act(lam_t[:, 0:2], lam_t[:, 0:2], AF.Exp)
    k.tt("dve", lam_t[:, 2:3], lam_t[:, 0:1], lam_t[:, 1:2], ALU.subtract)
    k.ts("dve", lam_t[:, 3:4], lam_t[:, 2:3], LAMBDA_INIT, ALU.add, -1.0, ALU.mult)
    neglam = lam_t[:, 3:4]

    mub = k.sb([128, 9, 128], F32, "mub")
    for c in range(9):
        k.ts("pool", mub[:, c, :], cs[:, 256:384], 0.0, ALU.mult, pvt[:, c:c + 1], ALU.add)

    w1b = k.sb([128, 8, 1824], BF16, "w1b")
    esS = ExitStack()
    k.es = esS
    stg1 = Rot([(k.sb([128, 1824], F32, f"stg1_{i}"), k.dsem()) for i in range(2)])
    for kc in range(8):
        k.ldcast(w1b[:, kc, :], w1c[kc * 128:(kc + 1) * 128, :], stg1, "pool" if kc % 2 == 0 else "dve")
    k.barrier()
    esS.close()
    k.es = es1
    w2p = k.sb([128, 2, 256], BF16, "w2p")
    k.memset("pool", w2p[:], 0.0)
    sem_w2 = k.dsem()
    k.dma("pool", View(w2p.t[0:64, 0, :], w2p.tk), w2a2[0:64, :], sem_w2)
    k.dma("pool", View(w2p.t[64:128, 1, :], w2p.tk), w2a2[64:128, :], sem_w2)
    g2b = k.sb([128, 2, 256], BF16, "g2b")
    k.memset("pool", g2b[:], 0.0)
    sem_g2 = k.dsem()
    k.dma("pool", g2b[:, 0, :], g2[0:128, :], sem_g2)
    k.dma("pool", View(g2b.t[0:32, 1, :], g2b.tk), g2[128:160, :], sem_g2)
    wob = k.sb([128, 4, D], BF16, "wob")
    sem_wo = k.dsem()
    for i in range(2):
        k.dma("pool", wob[:, i, :], wor[i * 128:(i + 1) * 128, :], sem_wo)
        k.dma("pool", wob[:, 2 + i, :], wod[i * 128:(i + 1) * 128, :], sem_wo)

    KT = k.sb([128, 2, LP], BF16, "KT")
    KTr = [KT.alias(f"KT{j}") for j in range(NT)]
    VA = k.sb([128, NT, 2, 130], BF16, "VA")
    VAr = [VA.alias(f"VA{j}") for j in range(NT)]
    k.op("pool", lambda e: e.memset(VA.t[:, :, :, 128:130], 1.0), [], [VA.tk] + [r.tk for r in VAr])
    k.op("pool", lambda e: e.memset(VA.t[0:112, 0, :, 128:130], 0.0), [], [VA.tk, VAr[0].tk])
    raw = k.sb([128, 9, 129], F32, "raw")
    k.memset("pool", raw[:], 0.0)
    H32 = k.sb([128, 2, 64], F32, "H32")
    k.memset("dve", H32[:], 0.0)
    Hb = k.sb([128, 2, 64], BF16, "Hb")
    k.memset("dve", Hb[:], 0.0)

    def rot(n, shape, dt, name):
        return Rot([k.sb(shape, dt, f"{name}{i}") for i in range(n)])

    xt_r = Rot([(k.sb([128, D], F32, f"xt{i}"), k.dsem(f"sem_xt{i}")) for i in range(1)])
    rp_r = Rot([(k.sb([128, 2, 128], F32, f"rp{i}"), k.dsem(f"sem_rp{i}")) for i in range(2)])
    junk = k.sb([128, 128], F32, "junk")
    st_r = rot(2, [128, 8], F32, "st")
    xs_r = rot(1, [128, D], BF16, "xs")
    hT_r = rot(1, [128, 8, 128], BF16, "hT")
    sh_r = rot(2, [128, 9, 128], F32, "sh")
    qkraw_r = rot(1, [128, 4, 128], F32, "qkraw")
    QT_r = rot(2, [128, 2, 2, 128], BF16, "QT")
    for qb in QT_r.bufs:
        k.memset("pool", qb[:], 0.0)
    ropt = k.sb([128, 4, 128], F32, "ropt")
    PT_r = rot(3, [128, 4, 128], BF16, "PT")
    on_r = rot(1, [128, 4, 128], F32, "on")
    rcp_r = rot(2, [128, 4], F32, "rcp")
    dsc = k.sb([128, 256], F32, "dsc")
    mix_r = rot(2, [128, 512], BF16, "mix")
    mixT_r = rot(2, [128, 4, 128], BF16, "mixT")
    po_r = Rot([(k.sb([128, 2048], BF16, f"po{i}"), k.dsem(f"sem_po{i}")) for i in range(1)])
    if dbg:
        dmix_r = Rot([(k.sb([128, 512], F32, f"dmix{i}"), k.dsem(f"sem_dmix{i}")) for i in range(1)])

    def f32t(name, shape=(128, 2, 128)):
        return k.sb(list(shape), F32, name)

    lorab = k.sb([128, 128], BF16, "lorab")
    sgd = k.sb([128, 2, 128], BF16, "sgd")
    k.memset("pool", sgd[:], 0.0)
    sw = f32t("sw")
    cum = f32t("cum")
    aa = f32t("aa")
    Ew = f32t("Ew")
    Ewx = f32t("Ewx")
    Einv = f32t("Einv")
    Eend = f32t("Eend")
    bC = k.sb([128, 2, 2], F32, "bC")
    WC = k.sb([128, 2, 2], F32, "WC")
    kk = f32t("kk")
    kq = f32t("kq")
    rn = f32t("rn")
    k2 = f32t("k2")
    bvec = f32t("bvec")
    rk = k.sb([128, 2, 128], BF16, "rk")
    BK = k.sb([128, 2, 2, 128], BF16, "BK")
    RT = k.sb([128, 2, 128], BF16, "RT")
    ARbd = k.sb([128, 2, 2, 2, 128], BF16, "ARbd")
    Bbd = k.sb([128, 2, 2, 128], BF16, "Bbd")
    TMbd = k.sb([128, 2, 2, 2, 128], BF16, "TMbd")
    PTbd = k.sb([128, 2, 2, 128], BF16, "PTbd")
    Hbd = k.sb([128, 2, 2, 64], BF16, "Hbd")
    for zb in (ARbd, Bbd, TMbd, PTbd, Hbd):
        k.memset("pool", zb[:], 0.0)
    FT = k.sb([128, 4, 2, 128], BF16, "FT")
    TM = k.sb([128, 4, 2, 128], BF16, "TM")
    S4 = k.sb([128, 4, 4, 128], BF16, "S4")
    ULr = rot(2, [128, 4, 2, 128], BF16, "UL")
    Yr = rot(2, [128, 4, 128], BF16, "Y")
    GTs = k.sb([128, 2, 2, 64], BF16, "GTs")
    yv = k.sb([128, 4, 64], F32, "yv")
    gst = k.sb([128, 16], F32, "gst")
    bsum = k.sb([128, 4], F32, "bsum")
    gtm = k.sb([128, 256], F32, "gtm")
    ytmp = k.sb([128, 4, 64], F32, "ytmp")

    heads = [(c2, e) for c2 in range(2) for e in range(2)]

    def p1_tile(j):
        xt, sx = xt_r.next()
        if j == 0:
            k.memset("pool", xt[:], 0.0)
            k.dma("sp", View(xt.t[112:128, :], xt.tk), meta[:, :], sx)
        else:
            k.dma("sp", xt[:], xb[(j - 1) * 128:j * 128, :], sx)
        rp, srp = rp_r.next()
        k.dma("sp", rp[:], rope[:, j, :, :], srp)
        st = st_r.next()
        xs = xs_r.next()
        k.act(xs[:], xt[:], AF.Square, accum=st[:, 0:1])
        k.ts("pool", st[:, 1:2], st[:, 0:1], 1.0 / D, ALU.mult, 1e-5, ALU.add)
        k.tt("pool", st[:, 2:3], st[:, 1:2], mhalf[:, 0:1], ALU.pow)
        k.stt(xs[:], xt[:], st[:, 2:3], nw_t[:], ALU.mult, ALU.mult)
        ps = ps_rot.next()
        psb16 = ps.same(BF16)
        k.pe([("T", psb16[:, kc * 128:(kc + 1) * 128], xs[:, kc * 128:(kc + 1) * 128], identb[:]) for kc in range(8)])
        hT = hT_r.next()
        k.cp("act", hT[:], View(psb16.t.ap().bitcast(BF16)[:, 0:1024].rearrange("p (a b) -> p a b", a=8), ps.tk))
        def proj(ps, slot, col0, width=128, rows=128):
            return [(View(ps.t[0:rows, slot * 128:slot * 128 + 128], ps.tk) if rows != 128 else ps[:, slot * 128:slot * 128 + 128],
                     w1b[:, kc, col0:col0 + width], hT[:, kc, :], kc == 0, kc == 7) for kc in range(8)]
        psA = ps_rot.next()
        items = []
        for s in range(4):
            items += proj(psA, s, s * 128)
        k.pe(items)
        k.cp("act", raw[:, 0:4, 1:129], View(psA.t[:, :].rearrange("p (a b) -> p a b", a=4), psA.tk))
        psB = ps_rot.next()
        items = []
        for s in range(4):
            items += proj(psB, s, 512 + s * 128)
        k.pe(items)
        k.cp("act", raw[:, 4:8, 1:129], View(psB.t[:, :].rearrange("p (a b) -> p a b", a=4), psB.tk))
        psC = ps_rot.next()
        items = []
        for s in range(4):
            items += proj(psC, s, 1056 + s * 128)
        k.pe(items)
        qkraw = qkraw_r.next()
        k.cp("dve", qkraw[:], View(psC.t[:, :].rearrange("p (a b) -> p a b", a=4), psC.tk))
        psD = ps_rot.next()
        items = proj(psD, 0, 1024, width=32, rows=32)
        items += [(psD[:, 128:384], hT[:, kc, :], w1b[:, kc, 1568:1824], kc == 0, kc == 7) for kc in range(8)]
        k.pe(items)
        k.cp("act", View(raw.t[0:32, 8, 1:129], raw.tk), View(psD.t[0:32, 0:128], psD.tk))
        k.cp(cfg.get("vaeng", "act"), View(VA.t[:, j, :, 0:128], VAr[j].tk), View(psD.t[:, 128:384].rearrange("p (a b) -> p a b", a=2), psD.tk))
        if j == 0:
            k.memset("pool", View(VA.t[0:112, 0, :, 0:128], VAr[0].tk), 0.0)
        sh = sh_r.next()
        k.tt("pool", sh[:], raw[:, :, 0:128], raw[:, :, 1:129], ALU.subtract)
        k.tt("pool", sh[:], sh[:], mub[:], ALU.mult)
        k.tt("pool", sh[:], sh[:], raw[:, :, 1:129], ALU.add)
        k.cp("pool", raw[:, :, 0:1], raw[:, :, 128:129])
        psR = ps_rot.next()
        k.pe([(psR[:, s * 128:(s + 1) * 128], cs[:, 128:256], qkraw[:, s, :], True, True) for s in range(4)])
        for s in range(4):
            k.tt("dve", ropt[:, s, :], psR[:, s * 128:(s + 1) * 128], rp[:, 1, :], ALU.mult)
            k.tt("pool", qkraw[:, s, :], qkraw[:, s, :], rp[:, 0, :], ALU.mult)
        QT = QT_r.next()
        for e in range(2):
            pb = 64 * e
            k.tt("pool", View(QT.t[pb:pb + 64, :, e, :], QT.tk), View(qkraw.t[pb:pb + 64, 0:2, :], qkraw.tk),
                 View(ropt.t[pb:pb + 64, 0:2, :], ropt.tk), ALU.add)
        k.tt("pool", View(KT.t[:, :, j * 128:(j + 1) * 128], KTr[j].tk), qkraw[:, 2:4, :], ropt[:, 2:4, :], ALU.add)
        return sh, QT

    def attn_tile(j, QT, mix):
        on = on_r.next()
        rcp = rcp_r.next()
        groups = [(h, i0) for h in range(4) for i0 in range(0, j + 1, 4)]
        pss = {}

        def qk(g):
            h, i0 = groups[g]
            dh, e = h // 2, h % 2
            n = min(4, j + 1 - i0)
            ps = ps_rot.next()
            k.pe([(ps[:, s * 128:(s + 1) * 128],
                   View(KT.t[:, dh, (i0 + s) * 128:(i0 + s + 1) * 128], KTr[i0 + s].tk),
                   QT[:, dh, e, :], True, True) for s in range(n)])
            pss[g] = ps

        for g in range(min(2, len(groups))):
            qk(g)
        for g, (h, i0) in enumerate(groups):
            dh, e = h // 2, h % 2
            po = ps_o[h % 2]
            n = min(4, j + 1 - i0)
            ps = pss.pop(g)
            PT = PT_r.next()
            k.act(View(PT.t[:, 0:n, :], PT.tk), View(ps.t[:, 0:n * 128].rearrange("p (a b) -> p a b", a=n), ps.tk), AF.Exp, scale=0.125)
            if i0 + n - 1 == j:
                k.tt("pool", PT[:, n - 1, :], PT[:, n - 1, :], cmaskb[:], ALU.mult)
            if g + 2 < len(groups):
                qk(g + 2)
            k.pe([(po[:, 0:130], PT[:, s, :], View(VA.t[:, i0 + s, dh, :], VAr[i0 + s].tk), (i0 + s) == 0, (i0 + s) == j)
                  for s in range(n)])
            if i0 + n - 1 == j:
                k.recip(rcp[:, h:h + 1], po[:, 128:129])
                k.ts("dve", on[:, h, :], po[:, 0:128], rcp[:, h:h + 1], ALU.mult)
        for dh in range(2):
            o = dsc[:, dh * 128:(dh + 1) * 128]
            k.stt(o, on[:, 2 * dh + 1, :], neglam, on[:, 2 * dh, :], ALU.mult, ALU.add)
            k.act(junk[:], o, AF.Square, accum=rcp[:, dh:dh + 1])
        k.ts("pool", rcp[:, 0:2], rcp[:, 0:2], 1.0 / 128, ALU.mult, 1e-5, ALU.add)
        k.tt("pool", rcp[:, 2:4], rcp[:, 0:2], mhalf[:, 0:2], ALU.pow)
        for dh in range(2):
            o = dsc[:, dh * 128:(dh + 1) * 128]
            k.stt(mix[:, 256 + dh * 128:256 + (dh + 1) * 128], o, rcp[:, 2 + dh:3 + dh], subln_t[:], ALU.mult, ALU.mult)

    def rwkv_tile(j, sh, mix):
        r = sh[:, 0:2, :]
        kx = sh[:, 2:4, :]
        v = sh[:, 4:6, :]
        k.act(View(lorab.t[0:64, :], lorab.tk), View(sh.t[0:64, 6, :], sh.tk), AF.Tanh)
        k.cp("act", View(lorab.t[64:128, :], lorab.tk), View(sh.t[64:128, 6, :], sh.tk))
        k.act(sgd[:, 0, :], sh[:, 7, :], AF.Tanh, scale=0.5)
        k.act(View(sgd.t[0:32, 1, :], sgd.tk), View(sh.t[0:32, 8, :], sh.tk), AF.Tanh, scale=0.5)
        k.ts("pool", sgd[:, 0, :], sgd[:, 0, :], 0.5, ALU.mult, 0.5, ALU.add)
        k.ts("pool", View(sgd.t[0:32, 1, :], sgd.tk), View(sgd.t[0:32, 1, :], sgd.tk), 0.5, ALU.mult, 0.5, ALU.add)
        psZ = ps_rot.next()
        items = []
        for c2 in range(2):
            items.append((psZ[:, c2 * 128:(c2 + 1) * 128], w2p[:, 0, c2 * 128:(c2 + 1) * 128], lorab[:], True, True))
            items.append((psZ[:, 256 + c2 * 128:256 + (c2 + 1) * 128], w2p[:, 1, c2 * 128:(c2 + 1) * 128], lorab[:], True, True))
        k.pe(items)
        for c2 in range(2):
            k.act(sw[:, c2, :], psZ[:, c2 * 128:(c2 + 1) * 128], AF.Tanh, bias=hb[:, c2:c2 + 1], scale=0.5)
            k.act(aa[:, c2, :], psZ[:, 256 + c2 * 128:256 + (c2 + 1) * 128], AF.Tanh, bias=hb[:, 2 + c2:3 + c2], scale=0.5)
        k.ts("dve", sw[:], sw[:], 0.5, ALU.mult, 0.5, ALU.add)
        k.ts("pool", aa[:], aa[:], 0.5, ALU.mult, 0.5, ALU.add)
        psG = ps_rot.next()
        k.pe([(psG[:, 0:256], sgd[:, 0, :], g2b[:, 0, :], True, False),
              (psG[:, 0:256], sgd[:, 1, :], g2b[:, 1, :], False, True)])
        k.cp("act", gtm[:], psG[:, 0:256])
        for c2 in range(2):
            k.op("dve", lambda e, c2=c2: e.tensor_tensor_scan(out=cum.t[:, c2, :], data0=cs.t[:, 256:384], data1=sw.t[:, c2, :],
                                                           initial=0.0, op0=ALU.mult, op1=ALU.add), [cs.tk, sw.tk], [cum.tk])
        k.tt("pool", Ewx[:], cum[:], sw[:], ALU.subtract)
        k.act(Ew[:], cum[:], AF.Exp, scale=-C0)
        k.act(Ewx[:], Ewx[:], AF.Exp, scale=-C0)
        k.act(Einv[:], cum[:], AF.Exp, scale=C0)
        k.ts("dve", bC[:], View(cum.t[:, :, 63:128:64], cum.tk), -C0, ALU.mult)
        k.act(WC[:], bC[:], AF.Exp)
        for c2 in range(2):
            for ch in range(2):
                k.act(Eend[:, c2, ch * 64:(ch + 1) * 64], cum[:, c2, ch * 64:(ch + 1) * 64], AF.Exp, scale=C0, bias=bC[:, c2, ch:ch + 1])
        for c2 in range(2):
            k.act(kk[:, c2, :], sh[:, 2 + c2, :], AF.Copy, scale=pvt[:, 13 + c2:14 + c2])
        k.tt("pool", kq[:], kk[:], kk[:], ALU.mult)
        psN = ps_rot.next()
        k.pe([(psN[:, c2 * 128:(c2 + 1) * 128], cs[:, 0:128], kq[:, c2, :], True, True) for c2 in range(2)])
        k.act(rn[:], View(psN.t[:, 0:256].rearrange("p (a b) -> p a b", a=2), psN.tk), AF.Sqrt)
        k.ts("dve", rn[:], rn[:], 1e-12, ALU.max)
        k.recip(rn[:], rn[:])
        k.tt("pool", kk[:], kk[:], rn[:], ALU.mult)
        for c2 in range(2):
            k.ts("dve", k2[:, c2, :], aa[:, c2, :], pvt[:, 15 + c2:16 + c2], ALU.mult, omka[:, c2:c2 + 1], ALU.add)
        k.tt("pool", k2[:], kx, k2[:], ALU.mult)
        k.tt("pool", bvec[:], kk[:], aa[:], ALU.mult)

        def c4(buf):
            return buf[:, :, :]

        k.stt(FT[:, 2, :, :], kk[:], -1.0, Ewx[:], ALU.mult, ALU.mult)
        k.tt("dve", RT[:], r, Ew[:], ALU.mult)
        k.tt("pool", BK[:, :, 0, :], bvec[:], Einv[:], ALU.mult)
        k.tt("dve", BK[:, :, 1, :], k2[:], Einv[:], ALU.mult)
        k.tt("pool", FT[:, 0, :, :], bvec[:], Eend[:], ALU.mult)
        k.tt("dve", FT[:, 1, :, :], k2[:], Eend[:], ALU.mult)
        k.cp("act", FT[:, 3, :, :], v)
        for e in range(2):
            pb = 64 * e
            k.cp("act", View(ARbd.t[pb:pb + 64, :, e, 0, :], ARbd.tk), View(FT.t[pb:pb + 64, 2, :, :], FT.tk))
            k.cp("act", View(ARbd.t[pb:pb + 64, :, e, 1, :], ARbd.tk), View(RT.t[pb:pb + 64, :, :], RT.tk))
            k.cp("act", View(Bbd.t[pb:pb + 64, :, e, :], Bbd.tk), View(BK.t[pb:pb + 64, :, 0, :], BK.tk))
        k.tt("pool", kq[:], r, k2[:], ALU.mult)
        for c2 in range(2):
            k.act(rk[:, c2, :], kq[:, c2, :], AF.Copy, scale=pvt[:, 17 + c2:18 + c2])
        psT = ps_rot.next()
        psT16 = psT.same(BF16)
        k.pe([("T", psT16[:, (kind * 2 + c2) * 128:(kind * 2 + c2 + 1) * 128], FT[:, kind, c2, :], identb[:])
              for kind in range(4) for c2 in range(2)])
        psTv = psT.t.ap().bitcast(BF16)[:, 0:1024].rearrange("p (a b c) -> p a b c", a=4, b=2)
        k.cp("act", TM[:], View(psTv, psT.tk))
        for ch in range(2):
            pt = 64 * ch
            k.cp("act", View(TMbd.t[pt:pt + 64, :, :, ch, :], TMbd.tk), View(psTv[pt:pt + 64, 0:2, :, :], psT.tk))
        UL = ULr.next()
        mk1 = View(mkb.t[:, 0:512].rearrange("p (a b c) -> p a b c", a=2, b=2), mkb.tk)
        for c2 in range(2):
            arv = View(ARbd.t[:, c2, :, :, :].rearrange("p a b c -> p (a b c)"), ARbd.tk)
            for kind in range(2):
                psg = ps_rot.next()
                k.pe([(psg[:, :], BK[:, c2, kind, :], arv, True, True)])
                k.tt("dve", S4[:, 2 * c2:2 * c2 + 2, 2 * kind:2 * kind + 2, :],
                     View(psg.t[:, :].rearrange("p (a b c) -> p a b c", a=2, b=2), psg.tk), mk1, ALU.mult)
        psLo = ps_rot.next()
        k.pe([(psLo[:, c2 * 256:(c2 + 1) * 256], FT[:, 2, c2, :], View(Bbd.t[:, c2, :, :].rearrange("p a b -> p (a b)"), Bbd.tk), True, True)
              for c2 in range(2)])
        k.tt("dve", UL[:, :, 1, :], View(psLo.t[:, :].rearrange("p (h q) -> p h q", h=4), psLo.tk),
             View(mkb.t[:, 512:1024].rearrange("p (h q) -> p h q", h=4), mkb.tk), ALU.mult)
        k.cp("act", UL[:, :, 0, :], S4[:, :, 0, :])
        Y = Yr.next()
        psY = ps_rot.next()
        k.pe([(psY[:, hh * 64:(hh + 1) * 64], S4[:, hh, 2, :], TM[:, 3, c2, 64 * e:64 * e + 64], True, True)
              for hh, (c2, e) in enumerate(heads)])
        k.cp("act", Y[:, :, 0:64], View(TM.t[:, 2, :, :].rearrange("p a (e q) -> p (a e) q", e=2), TM.tk))
        k.cp("dve", Y[:, :, 64:128], View(psY.t[:, 0:256].rearrange("p (h q) -> p h q", h=4), psY.tk))
        for lvl in range(6):
            psY = ps_rot.next()
            k.pe([(psY[:, hh * 128:(hh + 1) * 128], UL[:, hh, 0, :], Y[:, hh, :], True, True) for hh in range(4)])
            if lvl < 5:
                psU = [ps_rot.next(), ps_rot.next()]
                items = []
                for hh in range(4):
                    pu = psU[hh // 2]
                    o0 = (hh % 2) * 256
                    items.append((pu[:, o0:o0 + 128], UL[:, hh, 1, :], UL[:, hh, 0, :], True, True))
                    if lvl < 4:
                        items.append((pu[:, o0 + 128:o0 + 256], UL[:, hh, 0, :], UL[:, hh, 1, :], True, True))
                k.pe(items)
            Yn = Yr.next()
            k.tt("dve", Yn[:], View(psY.t[:, :].rearrange("p (h q) -> p h q", h=4), psY.tk), Y[:], ALU.add)
            Y = Yn
            if lvl < 5:
                UL = ULr.next()
                for half in range(2):
                    k.cp("act", UL[:, 2 * half:2 * half + 2, :, :], View(psU[half].t[:, :].rearrange("p (h a q) -> p h a q", h=2, a=2), psU[half].tk))
        psP = ps_rot.next()
        items = []
        for hh, (c2, e) in enumerate(heads):
            pbk = 64 * e
            m1 = Y[:, hh, 0:64]
            items.append((View(psP.t[pbk:pbk + 64, c2 * 256:c2 * 256 + 128], psP.tk), m1,
                          View(TMbd.t[:, 0, c2, :, pbk:pbk + 64], TMbd.tk), True, True))
            items.append((View(psP.t[pbk:pbk + 64, c2 * 256 + 128:c2 * 256 + 256], psP.tk), m1, S4[:, hh, 1, :], True, True))
        k.pe(items)
        ppv = psP.t[:, :].rearrange("p (a x) -> p a x", a=2)
        for e in range(2):
            pbk = 64 * e
            k.cp("dve", View(PTbd.t[pbk:pbk + 64, :, :, pbk:pbk + 64], PTbd.tk),
                 View(ppv[pbk:pbk + 64, :, 0:128].rearrange("p a (c q) -> p a c q", c=2), psP.tk))
        k.tt("dve", GTs[:], View(ppv[:, :, 128:256].rearrange("p a (c q) -> p a c q", c=2), psP.tk),
             View(RT.t[:, :, :].rearrange("p a (c q) -> p a c q", c=2), RT.tk), ALU.add)
        psBn = ps_rot.next()
        k.pe([(psBn[:, 2 * c2:2 * c2 + 2], rk[:, c2, :], bones[:], True, True) for c2 in range(2)])
        k.cp("act", bsum[:], psBn[:, 0:4])
        psYo = ps_rot.next()
        for ch in range(2):
            pt = 64 * ch
            psH = ps_rot.next()
            items = []
            for c2 in range(2):
                items.append((View(psYo.t[pt:pt + 64, c2 * 128:(c2 + 1) * 128], psYo.tk), GTs[:, c2, ch, :],
                              View(Hbd.t[:, c2, :, :].rearrange("p a b -> p (a b)"), Hbd.tk), True, False))
                for e in range(2):
                    hh = 2 * c2 + e
                    yo = View(psYo.t[pt:pt + 64, hh * 64:(hh + 1) * 64], psYo.tk)
                    items.append((yo, S4[:, hh, 1, pt:pt + 64], Y[:, hh, 64:128], False, False))
                    items.append((yo, S4[:, hh, 3, pt:pt + 64], TM[:, 3, c2, 64 * e:64 * e + 64], False, e == 1))
            for c2 in range(2):
                items.append((psH[:, c2 * 64:(c2 + 1) * 64], PTbd[:, c2, ch, :], Hb[:, c2, :], True, False))
                for e in range(2):
                    hh = 2 * c2 + e
                    pbk = 64 * e
                    ho = View(psH.t[pbk:pbk + 64, c2 * 64:(c2 + 1) * 64], psH.tk)
                    items.append((ho, TMbd[:, 0, c2, ch, pbk:pbk + 64], Y[:, hh, 64:128], False, False))
                    items.append((ho, TMbd[:, 1, c2, ch, pbk:pbk + 64], TM[:, 3, c2, pbk:pbk + 64], False, True))
            k.pe(items)
            for c2 in range(2):
                k.stt(H32[:, c2, :], H32[:, c2, :], WC[:, c2, ch:ch + 1], psH[:, c2 * 64:(c2 + 1) * 64], ALU.mult, ALU.add)
            k.cp("act", Hb[:], H32[:])
            for e in range(2):
                pbk = 64 * e
                k.cp("act", View(Hbd.t[pbk:pbk + 64, :, e, :], Hbd.tk), View(H32.t[pbk:pbk + 64, :, :], H32.tk))
        if j == 0:
            return
        k.cp("act", yv[:], View(psYo.t[:, 0:256].rearrange("p (h q) -> p h q", h=4), psYo.tk))
        k.red(gst[:, 0:4], yv[:])
        k.tt("pool", ytmp[:], yv[:], yv[:], ALU.mult)
        k.red(gst[:, 4:8], ytmp[:])
        k.ts("dve", gst[:, 0:8], gst[:, 0:8], 1.0 / 64, ALU.mult)
        k.tt("dve", gst[:, 8:12], gst[:, 0:4], gst[:, 0:4], ALU.mult)
        k.tt("dve", gst[:, 8:12], gst[:, 4:8], gst[:, 8:12], ALU.subtract)
        k.act(gst[:, 8:12], gst[:, 8:12], AF.Sqrt, bias=64e-5)
        k.recip(gst[:, 12:16], gst[:, 8:12])
        for hh in range(4):
            k.ts("dve", ytmp[:, hh, :], yv[:, hh, :], gst[:, hh:hh + 1], ALU.subtract, gst[:, 12 + hh:13 + hh], ALU.mult)
        yt2 = View(ytmp.t[:, :, :].rearrange("p h q -> p (h q)"), ytmp.tk)
        k.tt("pool", yt2, yt2, lnw_t[:, 0, :], ALU.mult)
        k.tt("pool", yt2, yt2, lnw_t[:, 1, :], ALU.add)
        for hh, (c2, e) in enumerate(heads):
            k.stt(ytmp[:, hh, :], TM[:, 3, c2, 64 * e:64 * e + 64], bsum[:, hh:hh + 1], ytmp[:, hh, :], ALU.mult, ALU.add)
        k.tt("dve", mix[:, 0:256], yt2, gtm[:], ALU.mult)

    def outproj_tile(j, mix):
        psT = ps_rot.next()
        psT16 = psT.same(BF16)
        k.pe([("T", psT16[:, c * 128:(c + 1) * 128], mix[:, c * 128:(c + 1) * 128], identb[:]) for c in range(4)])
        mixT = mixT_r.next()
        k.cp("act", mixT[:], View(psT.t.ap().bitcast(BF16)[:, 0:512].rearrange("p (a b) -> p a b", a=4), psT.tk))
        po, spo = po_r.next()
        for br in range(2):
            for half in range(2):
                ps = ps_rot.next()
                k.pe([(ps[:, :], mixT[:, 2 * br + kc, :], wob[:, 2 * br + kc, half * 512:(half + 1) * 512], kc == 0, kc == 1) for kc in range(2)])
                k.cp("act" if half == 0 else "dve", po[:, br * 1024 + half * 512:br * 1024 + (half + 1) * 512], ps[:, :])
        i = j - 1
        k.dma("sp", View(rs_in[i // 16].t[(i % 16) * 128:(i % 16 + 1) * 128, :], rs_tk[i // 16][i % 16]), po[:], spo)
        if dbg:
            dm, sdm = dmix_r.next()
            k.cp("pool", dm[:], mix[:])
            k.dma("sp", d_mix[j * 128:(j + 1) * 128, :], dm[:], sdm)

    cc_sem = es.enter_context(nc.semaphore("cc_sem"))
    k.engs["cc"] = EngS("cc", None, cc_sem)
    n_cc = 0
    def issue_rs(j):
        nonlocal n_cc
        if do_rs and j >= 1 and j % 16 == 0:
            q = j // 16 - 1
            E = k.engs["pool"]
            k._wait(E, rs_tk[q], [rs_out[q].tk])
            nc.gpsimd.collective_compute("ReduceScatter", ALU.add, replica_groups=[[0, 1, 2, 3], [4, 5, 6, 7]],
                                         ins=[rs_in[q].t.ap().opt()], outs=[rs_out[q].t.ap().opt()]).then_inc(cc_sem)
            n_cc += 1
            for t_ in rs_tk[q]:
                t_.rd["cc"] = n_cc
            rs_out[q].tk.lw = ("cc", n_cc)
            rs_out[q].tk.rd = {}

    tiles = [{"j": j} for j in range(NT1)]

    def P(t):
        t["sh"], t["QT"] = p1_tile(t["j"])

    def R(t):
        t["mix"] = mix_r.next()
        rwkv_tile(t["j"], t["sh"], t["mix"])

    def Y(t):
        attn_tile(t["j"], t["QT"], t["mix"])
        outproj_tile(t["j"], t["mix"])

    lenP, lenR = 40, 230
    if NT1 > 0:
        n0 = k.nops
        P(tiles[0])
        lenP = k.nops - n0
    for j in range(NT1):
        fns, ests = [], []
        fns.append(lambda t=tiles[j]: R(t))
        ests.append(lenR)
        if j >= 2:
            jj = j - 1
            fns.append(lambda t=tiles[jj]: Y(t))
            ests.append(4 * (3 * ((jj + 4) // 4) + 2) + 40)
        else:
            fns.append(lambda: None)
            ests.append(1)
        if j + 1 < NT1:
            fns.append(lambda t=tiles[j + 1]: P(t))
            ests.append(lenP)
        done = k.weave.run(fns, ests)
        lenR = max(1, done[0])
        if j >= 2:
            issue_rs(j - 1)
    if NT1 >= 2:
        Y(tiles[NT1 - 1])
        issue_rs(NT1 - 1)

    k.barrier()
    es1.close()
    if do_p2:
        src = rs_dbg if rs_dbg is not None else rs_out
        esA = ExitStack()
        k.es = es
        h2 = k.sb([128, 16, D], F32, "h2")
        k.es = esA
        wgb = k.sb([128, 8, 2048], BF16, "wgb")
        woutb = k.sb([128, 8, D], BF16, "woutb")
        stgA = Rot([(k.sb([128, D], F32, f"stgA{i}"), k.dsem()) for i in range(4)])
        for kc in range(8):
            for hf in range(2):
                k.ldcast(wgb[:, kc, hf * D:(hf + 1) * D], wg[kc * 128:(kc + 1) * 128, hf * D:(hf + 1) * D], stgA, "pool" if hf == 0 else "dve")
        for kc in range(8):
            k.ldcast(woutb[:, kc, :], wout[kc * 128:(kc + 1) * 128, :], stgA, "pool" if kc % 2 == 0 else "dve")
        nw2 = k.sb([128, D], F32, "nw2")
        k.dma("sp", nw2[:], nwb[:, :], k.dsem())
        x2_r = Rot([(k.sb([128, 4, D], F32, f"x2t{i}"), k.dsem()) for i in range(1)])
        rs_r = Rot([(k.sb([128, 4, 2048], BF16, f"rst{i}"), k.dsem()) for i in range(1)])
        st2 = k.sb([128, 8], F32, "st2")
        xs2 = k.sb([128, D], BF16, "xs2")
        hT2 = k.sb([128, 8, 512], BF16, "hT2")
        gate = k.sb([128, 2048], BF16, "gate")
        mrg = k.sb([128, D], BF16, "mrg")
        mrg2 = k.sb([128, D], BF16, "mrg2")
        mT = k.sb([128, 8, 128], BF16, "mT")
        gateB = k.sb([128, 2048], BF16, "gateB")
        mrgB = k.sb([128, D], BF16, "mrgB")
        mrg2B = k.sb([128, D], BF16, "mrg2B")
        mTB = k.sb([128, 8, 128], BF16, "mTB")

        def norm_T(xin, nwt, xs_, st_, dstT, col0):
            k.act(xs_[:], xin, AF.Square, accum=st_[:, 0:1])
            k.act(st_[:, 1:2], st_[:, 0:1], AF.Sqrt, bias=1e-5, scale=1.0 / D)
            k.recip(st_[:, 2:3], st_[:, 1:2])
            k.stt(xs_[:], xin, st_[:, 2:3], nwt, ALU.mult, ALU.mult)
            ps = ps_rot.next()
            ps16 = ps.same(BF16)
            k.pe([("T", ps16[:, kc * 128:(kc + 1) * 128], xs_[:, kc * 128:(kc + 1) * 128], identb[:]) for kc in range(8)])
            k.cp("act", View(dstT.t[:, :, col0:col0 + 128], dstT.tk),
                 View(ps.t.ap().bitcast(BF16)[:, 0:1024].rearrange("p (a b) -> p a b", a=8), ps.tk))

        for kq in range(4):
            x2t, sx2 = x2_r.next()
            k.dma("sp", x2t[:], View(x2.t[kq].rearrange("(s p) d -> p s d", p=128), x2.tk), sx2)
            rst, srs = rs_r.next()
            k.dma("sp", rst[:], View(src[kq].t.ap().rearrange("(s p) d -> p s d", p=128), src[kq].tk), srs)
            for s_ in range(4):
                norm_T(x2t[:, s_, :], nw2[:], xs2, st2, hT2, s_ * 128)
            def sub2a(s_, gate, mrg, mrg2, mT, kq=kq, x2t=x2t, rst=rst):
                for cb in range(4):
                    ps = ps_rot.next()
                    k.pe([(ps[:, :], hT2[:, kc, s_ * 128:(s_ + 1) * 128], wgb[:, kc, cb * 512:(cb + 1) * 512], kc == 0, kc == 7) for kc in range(8)])
                    k.act(gate[:, cb * 512:(cb + 1) * 512], ps[:, :], AF.Sigmoid)
                k.tt("pool", mrg[:], gate[:, 0:D], rst[:, s_, 0:D], ALU.mult)
                k.tt("dve", mrg2[:], gate[:, D:2 * D], rst[:, s_, D:2 * D], ALU.mult)
                k.tt("pool", mrg[:], mrg[:], mrg2[:], ALU.add)
                ps = ps_rot.next()
                ps16 = ps.same(BF16)
                k.pe([("T", ps16[:, kc * 128:(kc + 1) * 128], mrg[:, kc * 128:(kc + 1) * 128], identb[:]) for kc in range(8)])
                k.cp("act", mT[:], View(ps.t.ap().bitcast(BF16)[:, 0:1024].rearrange("p (a b) -> p a b", a=8), ps.tk))
                for cb in range(2):
                    ps = ps_rot.next()
                    k.pe([(ps[:, :], mT[:, kc, :], woutb[:, kc, cb * 512:(cb + 1) * 512], kc == 0, kc == 7) for kc in range(8)])
                    k.tt("dve", h2[:, kq * 4 + s_, cb * 512:(cb + 1) * 512], ps[:, :], x2t[:, s_, cb * 512:(cb + 1) * 512], ALU.add)

            for s0 in (0, 2):
                k.weave.run([lambda: sub2a(s0, gate, mrg, mrg2, mT), lambda: sub2a(s0 + 1, gateB, mrgB, mrg2B, mTB)], [1, 1])
        k.barrier()
        esA.close()
        esB = ExitStack()
        k.es = esB
        nmt = k.sb([128, 2, D], F32, "nmt")
        k.dma("sp", nmt[:], nm2[:, :, :], k.dsem())
        hnT = k.sb([128, 8, 2048], BF16, "hnT")
        xs3 = k.sb([128, D], BF16, "xs3")
        st3 = k.sb([128, 8], F32, "st3")
        esC = ExitStack()
        k.es = esC
        wq_r = Rot([(k.sb([128, 8, D], BF16, f"w1q{i}"), k.sb([128, 8, D], BF16, f"w2q{i}")) for i in range(2)])
        aT_r = rot(2, [128, 8, 512], BF16, "aT")
        rl_r = rot(2, [128, 512], BF16, "rl")
        stgB = Rot([(k.sb([128, D], F32, f"stgB{i}"), k.dsem()) for i in range(4)])

        def load_q(p):
            w1q, w2q = wq_r.next()
            for kc in range(8):
                k.ldcast(w1q[:, kc, :], mw1[kc * 128:(kc + 1) * 128, p * D:(p + 1) * D], stgB, "pool")
            for fc in range(8):
                k.ldcast(w2q[:, fc, :], mw2[p * D + fc * 128:p * D + (fc + 1) * 128, :], stgB, "pool")
            return w1q, w2q

        nxt = load_q(0)
        for t16 in range(16):
            norm_T(h2[:, t16, :], nmt[:, 0, :], xs3, st3, hnT, t16 * 128)
        for p in range(4):
            w1q, w2q = nxt
            if p < 3:
                nxt = load_q(p + 1)
            def up(kq, aT, w1q=w1q):
                for fc in range(8):
                    ps = ps_rot.next()
                    k.pe([(ps[:, :], w1q[:, kc, fc * 128:(fc + 1) * 128], hnT[:, kc, kq * 512:(kq + 1) * 512], kc == 0, kc == 7) for kc in range(8)])
                    rl = rl_r.next()
                    k.act(rl[:], ps[:, :], AF.Relu)
                    k.tt("pool", aT[:, fc, :], rl[:], rl[:], ALU.mult)

            def down(kq, aT, w2q=w2q):
                for s_ in range(4):
                    for cb in range(2):
                        ps = ps_rot.next()
                        k.pe([(ps[:, :], aT[:, fc, s_ * 128:(s_ + 1) * 128], w2q[:, fc, cb * 512:(cb + 1) * 512], fc == 0, fc == 7) for fc in range(8)])
                        hv = h2[:, kq * 4 + s_, cb * 512:(cb + 1) * 512]
                        k.tt("dve", hv, ps[:, :], hv, ALU.add)

            aTs = [aT_r.next() for _ in range(4)]
            up(0, aTs[0])
            for kq in range(4):
                if kq < 3:
                    k.weave.run([lambda: down(kq, aTs[kq]), lambda: up(kq + 1, aTs[kq + 1])], [1, 1])
                else:
                    down(kq, aTs[kq])
        k.barrier()
        esC.close()
        k.es = esB
        yo_r = Rot([(k.sb([128, D], F32, f"yo{i}"), k.dsem()) for i in range(2)])
        for t16 in range(16):
            yo, syo = yo_r.next()
            hv = h2[:, t16, :]
            k.act(yo[:], hv, AF.Square, accum=st3[:, 0:1])
            k.act(st3[:, 1:2], st3[:, 0:1], AF.Sqrt, bias=1e-5, scale=1.0 / D)
            k.recip(st3[:, 2:3], st3[:, 1:2])
            k.stt(yo[:], hv, st3[:, 2:3], nmt[:, 1, :], ALU.mult, ALU.mult)
            k.dma("sp", yout[t16 // 4, (t16 % 4) * 128:(t16 % 4 + 1) * 128, :], yo[:], syo)
        k.barrier()
        esB.close()
    k.barrier()
    print("total ops", k.nops, {n: e.count for n, e in k.engs.items() if e.count})
    return nc


def make_consts():
    c = np.zeros((128, 1024), np.float32)
    p = np.arange(128)[:, None]
    q = np.arange(128)[None, :]
    c[:, 0:128] = (p == q)
    c[:, 128:256] = (p <= q)
    c[:, 256:384] = (p // 64 == q // 64)
    pm = np.zeros((128, 128), np.float32)
    for hb in (0, 64):
        for d in range(8):
            pm[hb + d + 8, hb + d] = -1.0
            pm[hb + d, hb + d + 8] = 1.0
    c[:, 384:512] = pm
    c[:, 512:576] = (p % 64 == np.arange(64)[None, :])
    s = (np.arange(128) % 64)[:, None]
    t = np.arange(64)[None, :]
    c[:, 576:640] = (s < t)
    c[:, 640:704] = (s <= t)
    c[:, 704:768] = (s < t)
    c[:, 768:832] = (s <= t)
    c[:, 832:896] = (s > t)
    c[:, 896:1024] = 1.0
    c[:, 896] = 0.0
    c[:, 960] = 0.0
    return c


def make_consts2():
    c = np.zeros((128, 1024), np.float32)
    p = np.arange(128)[:, None]
    q = np.arange(128)[None, :]
    same = (p // 64 == q // 64)
    strict = same & ((p % 64) < (q % 64))
    incl = same & ((p % 64) <= (q % 64))
    lo = same & ((p % 64) > (q % 64))
    for e in range(2):
        c[:, e * 256:e * 256 + 128] = strict
        c[:, e * 256 + 128:e * 256 + 256] = incl
    for h in range(4):
        c[:, 512 + h * 128:512 + (h + 1) * 128] = lo
    return c


def make_rope():
    half = 8
    inv = (np.float32(500000.0) ** (-(np.arange(0, 16, 2, dtype=np.float32)) / np.float32(16))).astype(np.float32)
    pos = np.maximum(np.arange(LP) - 112, 0).astype(np.float32)
    ang = pos[:, None] * inv[None, :]
    cos = np.cos(ang).astype(np.float32)
    sin = np.sin(ang).astype(np.float32)
    ct = np.ones((128, LP), np.float32)
    stb = np.zeros((128, LP), np.float32)
    for hb in (0, 64):
        for d in range(16):
            ct[hb + d] = cos[:, d % 8]
            stb[hb + d] = sin[:, d % 8]
    r = np.stack([ct.reshape(128, NT, 128), stb.reshape(128, NT, 128)], axis=2)
    return np.ascontiguousarray(r)


def core_inputs(inp, c, consts, rope):
    b, g = c // 4, c % 4
    f = lambda a: np.ascontiguousarray(np.asarray(a, dtype=np.float32))
    w_in = inp["w_in"][0]
    sl = slice(256 * g, 256 * g + 256)
    cols = np.concatenate([np.arange(256 * g, 256 * g + 256), 1024 + np.arange(256 * g, 256 * g + 256),
                           2048 + np.arange(256 * g, 256 * g + 256), np.arange(3072, 3360),
                           3360 + np.arange(256 * g, 256 * g + 256), 4384 + np.arange(256 * g, 256 * g + 256),
                           5408 + np.arange(256 * g, 256 * g + 256)])
    mu = inp["rwkv_mu"][0]
    pv = np.zeros((128, 32), np.float32)
    mucols = [mu[0 + 256 * g:0 + 256 * g + 128], mu[256 * g + 128:256 * g + 256],
              mu[1024 + 256 * g:1024 + 256 * g + 128], mu[1024 + 256 * g + 128:1024 + 256 * g + 256],
              mu[2048 + 256 * g:2048 + 256 * g + 128], mu[2048 + 256 * g + 128:2048 + 256 * g + 256],
              mu[3072:3200], mu[3200:3328]]
    for i, m in enumerate(mucols):
        pv[:, i] = m
    pv[0:32, 8] = mu[3328:3360]
    for i, nm in enumerate(["rwkv_w0", "rwkv_a0", "rwkv_k_k", "rwkv_k_a"]):
        vv = inp[nm][0][sl]
        pv[:, 9 + 2 * i] = vv[0:128]
        pv[:, 10 + 2 * i] = vv[128:256]
    rkv = inp["rwkv_r_k"][0].reshape(-1)[sl]
    pv[:, 17] = rkv[0:128]
    pv[:, 18] = rkv[128:256]
    q = g
    x2 = np.stack([inp["x"][b, 2048 * kk + 512 * q:2048 * kk + 512 * q + 512] for kk in range(4)], 0)
    d = {
        "xb": f(inp["x"][b]),
        "meta": f(inp["meta_tokens"]),
        "x2": f(x2),
        "w1c": f(w_in[:, cols]),
        "wg": f(w_in[:, 6432:8480]),
        "pv": pv,
        "nwb": f(np.broadcast_to(inp["norm_mix_w"][0][None, :], (128, D))),
        "lnwb": f(np.broadcast_to(np.stack([inp["rwkv_ln_w"][0][sl], inp["rwkv_ln_b"][0][sl]], 0)[None], (128, 2, 256))),
        "sublnb": f(np.broadcast_to(inp["diff_subln_w"][0][None, :], (128, 128))),
        "lamv": f(np.broadcast_to(np.stack([inp["diff_lq1"][0], inp["diff_lk1"][0], inp["diff_lq2"][0], inp["diff_lk2"][0]], 0)[None], (128, 4, 64))),
        "w2a2": f(np.concatenate([inp["rwkv_w2"][0][:, sl], inp["rwkv_a2"][0][:, sl]], 0)),
        "g2": f(inp["rwkv_g2"][0][:, sl]),
        "wor": f(inp["rwkv_w_o"][0][sl, :]),
        "wod": f(inp["diff_w_o"][0][sl, :]),
        "wout": f(inp["w_out"][0]),
        "mw1": f(inp["mlp_w1"][0]),
        "mw2": f(inp["mlp_w2"][0]),
        "nm2": f(np.broadcast_to(np.stack([inp["norm_mlp_w"][0], inp["final_norm_w"]], 0)[None], (128, 2, D))),
        "rope": rope,
        "cst": consts,
        "cst2": make_consts2(),
    }
    return d


_CACHE = {}


def kernel(**inputs):
    inp = {kk: np.asarray(v) for kk, v in inputs.items()}
    if "nc" not in _CACHE:
        _CACHE["nc"] = build({})
    nc = _CACHE["nc"]
    consts = make_consts()
    rope = make_rope()
    in_maps = [core_inputs(inp, c, consts, rope) for c in range(8)]
    res = run_bass_kernel_spmd(nc, in_maps, core_ids=list(range(8)))
    out = np.zeros((2, 8192, D), np.float32)
    for c in range(8):
        b, q = c // 4, c % 4
        y = np.asarray(res.results[c]["yout"])
        for kk in range(4):
            out[b, 2048 * kk + 512 * q:2048 * kk + 512 * q + 512] = y[kk]
    return out
```

```python
import math
import threading
from contextlib import ExitStack
import numpy as np
import ml_dtypes
import concourse.bass as bass
import concourse.mybir as mybir
from concourse.bass_utils import run_bass_kernel_spmd

F32 = mybir.dt.float32
BF16 = mybir.dt.bfloat16
AF = mybir.ActivationFunctionType
ALU = mybir.AluOpType
AX = mybir.AxisListType

D = 1024
NT = 65
LP = NT * 128
C0 = math.exp(-0.5)
LAMBDA_INIT = 0.8 - 0.6 * math.exp(0.0)


class Tk:
    __slots__ = ("lw", "rd", "name", "psum")

    def __init__(self, name=""):
        self.lw = None
        self.rd = {}
        self.name = name
        self.psum = False


class View:
    __slots__ = ("ap", "tk")

    def __init__(self, ap, tk):
        self.ap = ap
        self.tk = tk


class Buf:
    def __init__(self, t, name, tk=None, dt=None):
        self.t = t
        self.name = name
        self.tk = tk if tk is not None else Tk(name)
        self.dt = dt

    def __getitem__(self, idx):
        ap = self.t[idx] if self.dt is None else self.t.ap().bitcast(self.dt)[idx]
        return View(ap, self.tk)

    def alias(self, name, dt=None):
        return Buf(self.t, name, dt=dt if dt is not None else self.dt)

    def same(self, dt):
        return Buf(self.t, self.name, tk=self.tk, dt=dt)


class EngS:
    def __init__(self, name, eng, sem):
        self.name = name
        self.eng = eng
        self.sem = sem
        self.count = 0
        self.waited = {}


class K:
    def __init__(self, nc, es):
        self.nc = nc
        self.es = es
        self.es_sem = es
        self.engs = {}
        for n, e in (("pe", nc.tensor), ("act", nc.scalar), ("dve", nc.vector), ("pool", nc.gpsimd), ("sp", nc.sync)):
            self.engs[n] = EngS(n, e, es.enter_context(nc.semaphore("sem_" + n)))
        self.nd = 0
        self.nsb = 0
        self.nops = 0
        self.limit = 10 ** 9
        self.weave = Weave()

    def sb(self, shape, dt, name=None):
        self.nsb += 1
        name = name or f"sb{self.nsb}"
        return Buf(self.es.enter_context(self.nc.sbuf_tensor(name, list(shape), dt)), name)

    def dsem(self, name=None):
        self.nd += 1
        name = name or f"dsem{self.nd}"
        e = EngS(name, None, self.es_sem.enter_context(self.nc.semaphore(name)))
        self.engs[name] = e
        return e

    def _wait(self, E, reads, writes):
        deps = {}
        for t in reads:
            if t.lw is not None and deps.get(t.lw[0], 0) < t.lw[1]:
                deps[t.lw[0]] = t.lw[1]
            if t.psum:
                for n, c in t.rd.items():
                    if n != E.name and deps.get(n, 0) < c:
                        deps[n] = c
        for t in writes:
            if t.lw is not None and deps.get(t.lw[0], 0) < t.lw[1]:
                deps[t.lw[0]] = t.lw[1]
            for n, c in t.rd.items():
                if deps.get(n, 0) < c:
                    deps[n] = c
        for n, c in deps.items():
            if n == E.name and n == "pe":
                continue
            if E.waited.get(n, 0) >= c:
                continue
            E.eng.wait_ge(self.engs[n].sem, c)
            E.waited[n] = c

    def op(self, en, fn, reads, writes):
        self.weave.checkpoint()
        self.nops += 1
        if self.nops > self.limit:
            return
        E = self.engs[en]
        self._wait(E, reads, writes)
        ins = fn(E.eng)
        E.count += 1
        ins.then_inc(E.sem, 1)
        for t in reads:
            t.rd[en] = E.count
        for t in writes:
            t.lw = (en, E.count)
            t.rd = {}

    def dma(self, qn, out, in_, dsem, **kw):
        self.weave.checkpoint()
        self.nops += 1
        if self.nops > self.limit:
            return
        Q = self.engs[qn]
        self._wait(Q, [in_.tk], [out.tk])
        ins = Q.eng.dma_start(out=out.ap, in_=in_.ap, **kw)
        dsem.count += 16
        ins.then_inc(dsem.sem, 16)
        in_.tk.rd[dsem.name] = dsem.count
        out.tk.lw = (dsem.name, dsem.count)
        out.tk.rd = {}

    def barrier(self):
        for en in ("sp", "pool", "act", "dve", "pe"):
            E = self.engs[en]
            for n, o in self.engs.items():
                if n == en or o.count == 0:
                    continue
                if E.waited.get(n, 0) < o.count:
                    E.eng.wait_ge(o.sem, o.count)
                    E.waited[n] = o.count

    def ldcast(self, dst, src, stg_r, eng="pool"):
        stg, sem = stg_r.next()
        sv = View(stg.t[:, 0:dst.ap.shape[-1]], stg.tk)
        self.dma("sp", sv, src, sem)
        self.cp(eng, dst, sv)

    def tt(self, en, out, a, b, op):
        self.op(en, lambda e: e.tensor_tensor(out=out.ap, in0=a.ap, in1=b.ap, op=op), [a.tk, b.tk], [out.tk])

    def ts(self, en, out, a, s1, op0, s2=None, op1=None):
        rd = [a.tk]
        v1 = s1
        v2 = s2
        if isinstance(s1, View):
            rd.append(s1.tk)
            v1 = s1.ap
        if isinstance(s2, View):
            rd.append(s2.tk)
            v2 = s2.ap
        kw = {}
        if en == "pool" and op1 is None and s2 is None and op0 == ALU.mult:
            op1 = ALU.add
            v2 = 0.0
        if op1 is not None:
            kw["op1"] = op1
        self.op(en, lambda e: e.tensor_scalar(out=out.ap, in0=a.ap, scalar1=v1, scalar2=v2, op0=op0, **kw), rd, [out.tk])

    def stt(self, out, a, s, b, op0, op1):
        rd = [a.tk, b.tk]
        v = s
        if isinstance(s, View):
            rd.append(s.tk)
            v = s.ap
        self.op("dve", lambda e: e.scalar_tensor_tensor(out=out.ap, in0=a.ap, scalar=v, in1=b.ap, op0=op0, op1=op1), rd, [out.tk])

    def cp(self, en, out, a):
        if en == "act":
            self.op(en, lambda e: e.copy(out=out.ap, in_=a.ap), [a.tk], [out.tk])
        else:
            self.op(en, lambda e: e.tensor_copy(out=out.ap, in_=a.ap), [a.tk], [out.tk])

    def act(self, out, a, func, bias=None, scale=1.0, accum=None):
        rd = [a.tk]
        wr = [out.tk]
        kw = {}
        if isinstance(bias, View):
            rd.append(bias.tk)
            kw["bias"] = bias.ap
        elif bias is not None:
            kw["bias"] = bias
        if isinstance(scale, View):
            rd.append(scale.tk)
            kw["scale"] = scale.ap
        else:
            kw["scale"] = scale
        if accum is not None:
            wr.append(accum.tk)
            kw["accum_out"] = accum.ap
        self.op("act", lambda e: e.activation(out=out.ap, in_=a.ap, func=func, **kw), rd, wr)

    def memset(self, en, out, val):
        self.op(en, lambda e: e.memset(out.ap, val), [], [out.tk])

    def red(self, out, a, op=ALU.add):
        self.op("dve", lambda e: e.tensor_reduce(out=out.ap, in_=a.ap, op=op, axis=AX.X), [a.tk], [out.tk])

    def recip(self, out, a):
        self.op("dve", lambda e: e.reciprocal(out=out.ap, in_=a.ap), [a.tk], [out.tk])

    def pe(self, items):
        rd = []
        wr = []
        for it in items:
            if it[0] == "T":
                wr.append(it[1].tk)
                rd += [it[2].tk, it[3].tk]
            else:
                wr.append(it[0].tk)
                rd += [it[1].tk, it[2].tk]

        def fn(e):
            ins = None
            for it in items:
                if it[0] == "T":
                    ins = e.transpose(it[1].ap, it[2].ap, it[3].ap)
                else:
                    ins = e.matmul(it[0].ap, lhsT=it[1].ap, rhs=it[2].ap, start=it[3], stop=it[4])
            return ins

        self.op("pe", fn, rd, wr)


class Weave:
    def __init__(self):
        self.cv = threading.Condition()
        self.active = None
        self.tl = threading.local()

    def current(self):
        return getattr(self.tl, "sid", 0) if self.active is not None else None

    def _pick(self):
        best = None
        for i in range(len(self.alive)):
            if self.alive[i]:
                f = self.done[i] / self.est[i]
                if best is None or f < best[0]:
                    best = (f, i)
        self.active = best[1] if best is not None else -1

    def checkpoint(self):
        if self.active is None:
            return
        sid = self.tl.sid
        with self.cv:
            self.done[sid] += 1
            self._pick()
            self.cv.notify_all()
            while self.active != sid:
                self.cv.wait()

    def run(self, fns, ests):
        n = len(fns)
        self.done = [0] * n
        self.est = [max(1, e) for e in ests]
        self.alive = [True] * n
        self.err = []

        def body(i):
            self.tl.sid = i
            with self.cv:
                while self.active != i:
                    self.cv.wait()
            try:
                fns[i]()
            except BaseException as ex:
                self.err.append(ex)
            with self.cv:
                self.alive[i] = False
                self._pick()
                self.cv.notify_all()

        ths = [threading.Thread(target=body, args=(i,)) for i in range(n)]
        with self.cv:
            self.active = 0
        for t in ths:
            t.start()
        for t in ths:
            t.join()
        self.active = None
        if self.err:
            raise self.err[0]
        return list(self.done)


class RotSel:
    def __init__(self, weave, pools):
        self.weave = weave
        self.pools = pools

    def next(self):
        c = self.weave.current()
        return self.pools[-1 if c is None else c].next()


class Rot:
    def __init__(self, bufs):
        self.bufs = bufs
        self.i = 0

    def next(self):
        b = self.bufs[self.i % len(self.bufs)]
        self.i += 1
        return b


def build(cfg):
    NT1 = cfg.get("nt1", NT)
    do_p2 = cfg.get("p2", True)
    do_rs = cfg.get("rs", True)
    dbg = cfg.get("dbg", False)
    nc = bass.Bass("TRN2", target_bir_lowering=False)
    es = ExitStack()
    k = K(nc, es)
    k.limit = cfg.get("limit", 10 ** 9)

    def din(name, shape, dt=F32):
        return Buf(nc.dram_tensor(name, list(shape), dt, kind="ExternalInput"), name)

    def dout(name, shape, dt=F32):
        return Buf(nc.dram_tensor(name, list(shape), dt, kind="ExternalOutput"), name)

    xb = din("xb", [8192, D])
    meta = din("meta", [16, D])
    x2 = din("x2", [4, 512, D])
    w1c = din("w1c", [D, 1824])
    wg = din("wg", [D, 2048])
    pv = din("pv", [128, 32])
    nwb = din("nwb", [128, D])
    lnwb = din("lnwb", [128, 2, 256])
    sublnb = din("sublnb", [128, 128])
    lamv = din("lamv", [128, 4, 64])
    w2a2 = din("w2a2", [128, 256])
    g2 = din("g2", [160, 256])
    wor = din("wor", [256, D])
    wod = din("wod", [256, D])
    wout = din("wout", [D, D])
    mw1 = din("mw1", [D, 4096])
    mw2 = din("mw2", [4096, D])
    nm2 = din("nm2", [128, 2, D])
    rs_dbg = [din(f"rs_dbg{i}", [512, 2048], BF16) for i in range(4)] if cfg.get("p2only") else None
    rope = din("rope", [128, NT, 2, 128])
    cst = din("cst", [128, 1024])
    cst2 = din("cst2", [128, 1024])
    yout = dout("yout", [4, 512, D])
    rs_in = [Buf(nc.dram_tensor(f"rs_in{i}", [2048, 2048], BF16), f"rs_in{i}") for i in range(4)]
    rs_tk = [[Tk(f"rs{i}_{r}") for r in range(16)] for i in range(4)]
    rs_out = [Buf(nc.dram_tensor(f"rs_out{i}", [512, 2048], BF16), f"rs_out{i}") for i in range(4)]
    if dbg:
        d_mix = dout("d_mix", [NT1 * 128, 512])

    psb = [Buf(es.enter_context(nc.psum_tensor(f"ps{i}", [128, 512], F32)), f"ps{i}") for i in range(8)]
    for pb_ in psb:
        pb_.tk.psum = True
    ps_rot = RotSel(k.weave, [Rot(psb[0:3]), Rot(psb[3:6]), Rot(psb[6:7]), Rot(psb[0:7])])
    ps_o = [psb[7], psb[7]]

    mhalf = k.sb([128, 2], F32, "mhalf")
    k.memset("dve", mhalf[:], -0.5)
    identb = k.sb([128, 128], BF16, "identb")
    k.dma("pool", identb[:], cst[:, 0:128], k.dsem())
    es1 = ExitStack()
    k.es = es1
    cs = k.sb([128, 384], F32, "cs")
    sem_c = k.dsem("sem_c")
    k.dma("sp", cs[:, 0:256], cst[:, 256:512], sem_c)
    k.dma("sp", cs[:, 256:384], cst[:, 896:1024], sem_c)

    cmaskb = k.sb([128, 128], BF16, "cmaskb")
    k.dma("pool", cmaskb[:], cst[:, 128:256], k.dsem())
    mkb = k.sb([128, 1024], BF16, "mkb")
    k.dma("pool", mkb[:], cst2[:, :], k.dsem())
    bones = k.sb([128, 2], BF16, "bones")
    k.cp("dve", bones[:], cs[:, 0:128:64])
    pvt = k.sb([128, 32], F32, "pvt")
    k.dma("sp", pvt[:], pv[:, :], k.dsem())
    hb = k.sb([128, 4], F32, "hb")
    k.ts("dve", hb[:], pvt[:, 9:13], 0.5, ALU.mult)
    omka = k.sb([128, 2], F32, "omka")
    k.ts("dve", omka[:], pvt[:, 15:17], -1.0, ALU.mult, 1.0, ALU.add)
    nw_t = k.sb([128, D], BF16, "nw_t")
    k.dma("pool", nw_t[:], nwb[:, :], k.dsem())
    lnw_t = k.sb([128, 2, 256], F32, "lnw_t")
    k.dma("sp", lnw_t[:], lnwb[:, :, :], k.dsem())
    subln_t = k.sb([128, 128], F32, "subln_t")
    k.dma("sp", subln_t[:], sublnb[:, :], k.dsem())
    k.ts("dve", subln_t[:], subln_t[:], 1.0 - LAMBDA_INIT, ALU.mult)
    lam_in = k.sb([128, 4, 64], F32, "lam_in")
    k.dma("sp", lam_in[:], lamv[:, :, :], k.dsem())
    lam_t = k.sb([128, 4], F32, "lam_t")
    lam_j = k.sb([128, 64], F32, "lam_j")
    for i in range(2):
        k.tt("dve", lam_j[:], lam_in[:, 2 * i, :], lam_in[:, 2 * i + 1, :], ALU.mult)
        k.red(lam_t[:, i:i + 1], lam_j[:])
    k.act(lam_t[:, 0:2], lam_t[:, 0:2], AF.Exp)
    k.tt("dve", lam_t[:, 2:3], lam_t[:, 0:1], lam_t[:, 1:2], ALU.subtract)
    k.ts("dve", lam_t[:, 3:4], lam_t[:, 2:3], LAMBDA_INIT, ALU.add, -1.0, ALU.mult)
    neglam = lam_t[:, 3:4]

    mub = k.sb([128, 9, 128], F32, "mub")
    for c in range(9):
        k.ts("pool", mub[:, c, :], cs[:, 256:384], 0.0, ALU.mult, pvt[:, c:c + 1], ALU.add)

    w1b = k.sb([128, 8, 1824], BF16, "w1b")
    esS = ExitStack()
    k.es = esS
    stg1 = Rot([(k.sb([128, 1824], F32, f"stg1_{i}"), k.dsem()) for i in range(2)])
    for kc in range(8):
        k.ldcast(w1b[:, kc, :], w1c[kc * 128:(kc + 1) * 128, :], stg1, "act" if kc % 2 == 0 else "dve")
    k.barrier()
    esS.close()
    k.es = es1
    w2p = k.sb([128, 2, 256], BF16, "w2p")
    k.memset("pool", w2p[:], 0.0)
    sem_w2 = k.dsem()
    k.dma("pool", View(w2p.t[0:64, 0, :], w2p.tk), w2a2[0:64, :], sem_w2)
    k.dma("pool", View(w2p.t[64:128, 1, :], w2p.tk), w2a2[64:128, :], sem_w2)
    g2b = k.sb([128, 2, 256], BF16, "g2b")
    k.memset("pool", g2b[:], 0.0)
    sem_g2 = k.dsem()
    k.dma("pool", g2b[:, 0, :], g2[0:128, :], sem_g2)
    k.dma("pool", View(g2b.t[0:32, 1, :], g2b.tk), g2[128:160, :], sem_g2)
    wob = k.sb([128, 4, D], BF16, "wob")
    sem_wo = k.dsem()
    for i in range(2):
        k.dma("pool", wob[:, i, :], wor[i * 128:(i + 1) * 128, :], sem_wo)
        k.dma("pool", wob[:, 2 + i, :], wod[i * 128:(i + 1) * 128, :], sem_wo)

    KT = k.sb([128, 2, LP], BF16, "KT")
    KTr = [KT.alias(f"KT{j}") for j in range(NT)]
    VA = k.sb([128, NT, 2, 130], BF16, "VA")
    VAr = [VA.alias(f"VA{j}") for j in range(NT)]
    k.op("pool", lambda e: e.memset(VA.t[:, :, :, 128:130], 1.0), [], [VA.tk] + [r.tk for r in VAr])
    k.op("pool", lambda e: e.memset(VA.t[0:112, 0, :, 128:130], 0.0), [], [VA.tk, VAr[0].tk])
    raw = k.sb([128, 9, 129], F32, "raw")
    k.memset("pool", raw[:], 0.0)
    H32 = k.sb([128, 2, 64], F32, "H32")
    k.memset("dve", H32[:], 0.0)
    Hb = k.sb([128, 2, 64], BF16, "Hb")
    k.memset("dve", Hb[:], 0.0)

    def rot(n, shape, dt, name):
        return Rot([k.sb(shape, dt, f"{name}{i}") for i in range(n)])

    xt_r = Rot([(k.sb([128, D], F32, f"xt{i}"), k.dsem(f"sem_xt{i}")) for i in range(1)])
    rp_r = Rot([(k.sb([128, 2, 128], F32, f"rp{i}"), k.dsem(f"sem_rp{i}")) for i in range(2)])
    junk = k.sb([128, 128], F32, "junk")
    st_r = rot(2, [128, 8], F32, "st")
    xs_r = rot(1, [128, D], BF16, "xs")
    hT_r = rot(1, [128, 8, 128], BF16, "hT")
    sh_r = rot(2, [128, 9, 128], F32, "sh")
    qkraw_r = rot(1, [128, 4, 128], F32, "qkraw")
    QT_r = rot(2, [128, 2, 2, 128], BF16, "QT")
    for qb in QT_r.bufs:
        k.memset("pool", qb[:], 0.0)
    ropt = k.sb([128, 4, 128], F32, "ropt")
    PT_r = rot(3, [128, 4, 128], BF16, "PT")
    on_r = rot(1, [128, 4, 128], F32, "on")
    rcp_r = rot(2, [128, 4], F32, "rcp")
    dsc = k.sb([128, 256], F32, "dsc")
    mix_r = rot(2, [128, 512], BF16, "mix")
    mixT_r = rot(2, [128, 4, 128], BF16, "mixT")
    po_r = Rot([(k.sb([128, 2048], BF16, f"po{i}"), k.dsem(f"sem_po{i}")) for i in range(1)])
    if dbg:
        dmix_r = Rot([(k.sb([128, 512], F32, f"dmix{i}"), k.dsem(f"sem_dmix{i}")) for i in range(1)])

    def f32t(name, shape=(128, 2, 128)):
        return k.sb(list(shape), F32, name)

    lorab = k.sb([128, 128], BF16, "lorab")
    sgd = k.sb([128, 2, 128], BF16, "sgd")
    k.memset("pool", sgd[:], 0.0)
    sw = f32t("sw")
    cum = f32t("cum")
    aa = f32t("aa")
    Ew = f32t("Ew")
    Ewx = f32t("Ewx")
    Einv = f32t("Einv")
    Eend = f32t("Eend")
    bC = k.sb([128, 2, 2], F32, "bC")
    WC = k.sb([128, 2, 2], F32, "WC")
    kk = f32t("kk")
    kq = f32t("kq")
    rn = f32t("rn")
    k2 = f32t("k2")
    bvec = f32t("bvec")
    rk = k.sb([128, 2, 128], BF16, "rk")
    BK = k.sb([128, 2, 2, 128], BF16, "BK")
    RT = k.sb([128, 2, 128], BF16, "RT")
    ARbd = k.sb([128, 2, 2, 2, 128], BF16, "ARbd")
    Bbd = k.sb([128, 2, 2, 128], BF16, "Bbd")
    TMbd = k.sb([128, 2, 2, 2, 128], BF16, "TMbd")
    PTbd = k.sb([128, 2, 2, 128], BF16, "PTbd")
    Hbd = k.sb([128, 2, 2, 64], BF16, "Hbd")
    for zb in (ARbd, Bbd, TMbd, PTbd, Hbd):
        k.memset("pool", zb[:], 0.0)
    FT = k.sb([128, 4, 2, 128], BF16, "FT")
    TM = k.sb([128, 4, 2, 128], BF16, "TM")
    S4 = k.sb([128, 4, 4, 128], BF16, "S4")
    ULr = rot(2, [128, 4, 2, 128], BF16, "UL")
    Yr = rot(2, [128, 4, 128], BF16, "Y")
    GTs = k.sb([128, 2, 2, 64], BF16, "GTs")
    yv = k.sb([128, 4, 64], F32, "yv")
    gst = k.sb([128, 16], F32, "gst")
    bsum = k.sb([128, 4], F32, "bsum")
    gtm = k.sb([128, 256], F32, "gtm")
    ytmp = k.sb([128, 4, 64], F32, "ytmp")

    heads = [(c2, e) for c2 in range(2) for e in range(2)]

    def p1_tile(j):
        xt, sx = xt_r.next()
        if j == 0:
            k.memset("pool", xt[:], 0.0)
            k.dma("sp", View(xt.t[112:128, :], xt.tk), meta[:, :], sx)
        else:
            k.dma("sp", xt[:], xb[(j - 1) * 128:j * 128, :], sx)
        rp, srp = rp_r.next()
        k.dma("sp", rp[:], rope[:, j, :, :], srp)
        st = st_r.next()
        xs = xs_r.next()
        k.act(xs[:], xt[:], AF.Square, accum=st[:, 0:1])
        k.ts("pool", st[:, 1:2], st[:, 0:1], 1.0 / D, ALU.mult, 1e-5, ALU.add)
        k.tt("pool", st[:, 2:3], st[:, 1:2], mhalf[:, 0:1], ALU.pow)
        k.stt(xs[:], xt[:], st[:, 2:3], nw_t[:], ALU.mult, ALU.mult)
        ps = ps_rot.next()
        psb16 = ps.same(BF16)
        k.pe([("T", psb16[:, kc * 128:(kc + 1) * 128], xs[:, kc * 128:(kc + 1) * 128], identb[:]) for kc in range(8)])
        hT = hT_r.next()
        k.cp("act", hT[:], View(psb16.t.ap().bitcast(BF16)[:, 0:1024].rearrange("p (a b) -> p a b", a=8), ps.tk))
        def proj(ps, slot, col0, width=128, rows=128):
            return [(View(ps.t[0:rows, slot * 128:slot * 128 + 128], ps.tk) if rows != 128 else ps[:, slot * 128:slot * 128 + 128],
                     w1b[:, kc, col0:col0 + width], hT[:, kc, :], kc == 0, kc == 7) for kc in range(8)]
        psA = ps_rot.next()
        items = []
        for s in range(4):
            items += proj(psA, s, s * 128)
        k.pe(items)
        k.cp("act", raw[:, 0:4, 1:129], View(psA.t[:, :].rearrange("p (a b) -> p a b", a=4), psA.tk))
        psB = ps_rot.next()
        items = []
        for s in range(4):
            items += proj(psB, s, 512 + s * 128)
        k.pe(items)
        k.cp("act", raw[:, 4:8, 1:129], View(psB.t[:, :].rearrange("p (a b) -> p a b", a=4), psB.tk))
        psC = ps_rot.next()
        items = []
        for s in range(4):
            items += proj(psC, s, 1056 + s * 128)
        k.pe(items)
        qkraw = qkraw_r.next()
        k.cp("dve", qkraw[:], View(psC.t[:, :].rearrange("p (a b) -> p a b", a=4), psC.tk))
        psD = ps_rot.next()
        items = proj(psD, 0, 1024, width=32, rows=32)
        items += [(psD[:, 128:384], hT[:, kc, :], w1b[:, kc, 1568:1824], kc == 0, kc == 7) for kc in range(8)]
        k.pe(items)
        k.cp("act", View(raw.t[0:32, 8, 1:129], raw.tk), View(psD.t[0:32, 0:128], psD.tk))
        k.cp(cfg.get("vaeng", "act"), View(VA.t[:, j, :, 0:128], VAr[j].tk), View(psD.t[:, 128:384].rearrange("p (a b) -> p a b", a=2), psD.tk))
        if j == 0:
            k.memset("pool", View(VA.t[0:112, 0, :, 0:128], VAr[0].tk), 0.0)
        sh = sh_r.next()
        k.tt("pool", sh[:], raw[:, :, 0:128], raw[:, :, 1:129], ALU.subtract)
        k.tt("pool", sh[:], sh[:], mub[:], ALU.mult)
        k.tt("pool", sh[:], sh[:], raw[:, :, 1:129], ALU.add)
        k.cp("pool", raw[:, :, 0:1], raw[:, :, 128:129])
        psR = ps_rot.next()
        k.pe([(psR[:, s * 128:(s + 1) * 128], cs[:, 128:256], qkraw[:, s, :], True, True) for s in range(4)])
        for s in range(4):
            k.tt("dve", ropt[:, s, :], psR[:, s * 128:(s + 1) * 128], rp[:, 1, :], ALU.mult)
            k.tt("pool", qkraw[:, s, :], qkraw[:, s, :], rp[:, 0, :], ALU.mult)
        QT = QT_r.next()
        for e in range(2):
            pb = 64 * e
            k.tt("pool", View(QT.t[pb:pb + 64, :, e, :], QT.tk), View(qkraw.t[pb:pb + 64, 0:2, :], qkraw.tk),
                 View(ropt.t[pb:pb + 64, 0:2, :], ropt.tk), ALU.add)
        k.tt("pool", View(KT.t[:, :, j * 128:(j + 1) * 128], KTr[j].tk), qkraw[:, 2:4, :], ropt[:, 2:4, :], ALU.add)
        return sh, QT

    def attn_tile(j, QT, mix):
        on = on_r.next()
        rcp = rcp_r.next()
        groups = [(h, i0) for h in range(4) for i0 in range(0, j + 1, 4)]
        pss = {}

        def qk(g):
            h, i0 = groups[g]
            dh, e = h // 2, h % 2
            n = min(4, j + 1 - i0)
            ps = ps_rot.next()
            k.pe([(ps[:, s * 128:(s + 1) * 128],
                   View(KT.t[:, dh, (i0 + s) * 128:(i0 + s + 1) * 128], KTr[i0 + s].tk),
                   QT[:, dh, e, :], True, True) for s in range(n)])
            pss[g] = ps

        for g in range(min(2, len(groups))):
            qk(g)
        for g, (h, i0) in enumerate(groups):
            dh, e = h // 2, h % 2
            po = ps_o[h % 2]
            n = min(4, j + 1 - i0)
            ps = pss.pop(g)
            PT = PT_r.next()
            k.act(View(PT.t[:, 0:n, :], PT.tk), View(ps.t[:, 0:n * 128].rearrange("p (a b) -> p a b", a=n), ps.tk), AF.Exp, scale=0.125)
            if i0 + n - 1 == j:
                k.tt("pool", PT[:, n - 1, :], PT[:, n - 1, :], cmaskb[:], ALU.mult)
            if g + 2 < len(groups):
                qk(g + 2)
            k.pe([(po[:, 0:130], PT[:, s, :], View(VA.t[:, i0 + s, dh, :], VAr[i0 + s].tk), (i0 + s) == 0, (i0 + s) == j)
                  for s in range(n)])
            if i0 + n - 1 == j:
                k.recip(rcp[:, h:h + 1], po[:, 128:129])
                k.ts("dve", on[:, h, :], po[:, 0:128], rcp[:, h:h + 1], ALU.mult)
        for dh in range(2):
            o = dsc[:, dh * 128:(dh + 1) * 128]
            k.stt(o, on[:, 2 * dh + 1, :], neglam, on[:, 2 * dh, :], ALU.mult, ALU.add)
            k.act(junk[:], o, AF.Square, accum=rcp[:, dh:dh + 1])
        k.ts("pool", rcp[:, 0:2], rcp[:, 0:2], 1.0 / 128, ALU.mult, 1e-5, ALU.add)
        k.tt("pool", rcp[:, 2:4], rcp[:, 0:2], mhalf[:, 0:2], ALU.pow)
        for dh in range(2):
            o = dsc[:, dh * 128:(dh + 1) * 128]
            k.stt(mix[:, 256 + dh * 128:256 + (dh + 1) * 128], o, rcp[:, 2 + dh:3 + dh], subln_t[:], ALU.mult, ALU.mult)

    def rwkv_tile(j, sh, mix):
        r = sh[:, 0:2, :]
        kx = sh[:, 2:4, :]
        v = sh[:, 4:6, :]
        k.act(View(lorab.t[0:64, :], lorab.tk), View(sh.t[0:64, 6, :], sh.tk), AF.Tanh)
        k.cp("act", View(lorab.t[64:128, :], lorab.tk), View(sh.t[64:128, 6, :], sh.tk))
        k.act(sgd[:, 0, :], sh[:, 7, :], AF.Tanh, scale=0.5)
        k.act(View(sgd.t[0:32, 1, :], sgd.tk), View(sh.t[0:32, 8, :], sh.tk), AF.Tanh, scale=0.5)
        k.ts("pool", sgd[:, 0, :], sgd[:, 0, :], 0.5, ALU.mult, 0.5, ALU.add)
        k.ts("pool", View(sgd.t[0:32, 1, :], sgd.tk), View(sgd.t[0:32, 1, :], sgd.tk), 0.5, ALU.mult, 0.5, ALU.add)
        psZ = ps_rot.next()
        items = []
        for c2 in range(2):
            items.append((psZ[:, c2 * 128:(c2 + 1) * 128], w2p[:, 0, c2 * 128:(c2 + 1) * 128], lorab[:], True, True))
            items.append((psZ[:, 256 + c2 * 128:256 + (c2 + 1) * 128], w2p[:, 1, c2 * 128:(c2 + 1) * 128], lorab[:], True, True))
        k.pe(items)
        for c2 in range(2):
            k.act(sw[:, c2, :], psZ[:, c2 * 128:(c2 + 1) * 128], AF.Tanh, bias=hb[:, c2:c2 + 1], scale=0.5)
            k.act(aa[:, c2, :], psZ[:, 256 + c2 * 128:256 + (c2 + 1) * 128], AF.Tanh, bias=hb[:, 2 + c2:3 + c2], scale=0.5)
        k.ts("dve", sw[:], sw[:], 0.5, ALU.mult, 0.5, ALU.add)
        k.ts("pool", aa[:], aa[:], 0.5, ALU.mult, 0.5, ALU.add)
        psG = ps_rot.next()
        k.pe([(psG[:, 0:256], sgd[:, 0, :], g2b[:, 0, :], True, False),
              (psG[:, 0:256], sgd[:, 1, :], g2b[:, 1, :], False, True)])
        k.cp("act", gtm[:], psG[:, 0:256])
        for c2 in range(2):
            k.op("dve", lambda e, c2=c2: e.tensor_tensor_scan(out=cum.t[:, c2, :], data0=cs.t[:, 256:384], data1=sw.t[:, c2, :],
                                                           initial=0.0, op0=ALU.mult, op1=ALU.add), [cs.tk, sw.tk], [cum.tk])
        k.tt("pool", Ewx[:], cum[:], sw[:], ALU.subtract)
        k.act(Ew[:], cum[:], AF.Exp, scale=-C0)
        k.act(Ewx[:], Ewx[:], AF.Exp, scale=-C0)
        k.act(Einv[:], cum[:], AF.Exp, scale=C0)
        k.ts("dve", bC[:], View(cum.t[:, :, 63:128:64], cum.tk), -C0, ALU.mult)
        k.act(WC[:], bC[:], AF.Exp)
        for c2 in range(2):
            for ch in range(2):
                k.act(Eend[:, c2, ch * 64:(ch + 1) * 64], cum[:, c2, ch * 64:(ch + 1) * 64], AF.Exp, scale=C0, bias=bC[:, c2, ch:ch + 1])
        for c2 in range(2):
            k.act(kk[:, c2, :], sh[:, 2 + c2, :], AF.Copy, scale=pvt[:, 13 + c2:14 + c2])
        k.tt("pool", kq[:], kk[:], kk[:], ALU.mult)
        psN = ps_rot.next()
        k.pe([(psN[:, c2 * 128:(c2 + 1) * 128], cs[:, 0:128], kq[:, c2, :], True, True) for c2 in range(2)])
        k.act(rn[:], View(psN.t[:, 0:256].rearrange("p (a b) -> p a b", a=2), psN.tk), AF.Sqrt)
        k.ts("dve", rn[:], rn[:], 1e-12, ALU.max)
        k.recip(rn[:], rn[:])
        k.tt("pool", kk[:], kk[:], rn[:], ALU.mult)
        for c2 in range(2):
            k.ts("dve", k2[:, c2, :], aa[:, c2, :], pvt[:, 15 + c2:16 + c2], ALU.mult, omka[:, c2:c2 + 1], ALU.add)
        k.tt("pool", k2[:], kx, k2[:], ALU.mult)
        k.tt("pool", bvec[:], kk[:], aa[:], ALU.mult)

        def c4(buf):
            return buf[:, :, :]

        k.stt(FT[:, 2, :, :], kk[:], -1.0, Ewx[:], ALU.mult, ALU.mult)
        k.tt("dve", RT[:], r, Ew[:], ALU.mult)
        k.tt("pool", BK[:, :, 0, :], bvec[:], Einv[:], ALU.mult)
        k.tt("dve", BK[:, :, 1, :], k2[:], Einv[:], ALU.mult)
        k.tt("pool", FT[:, 0, :, :], bvec[:], Eend[:], ALU.mult)
        k.tt("dve", FT[:, 1, :, :], k2[:], Eend[:], ALU.mult)
        k.cp("act", FT[:, 3, :, :], v)
        for e in range(2):
            pb = 64 * e
            k.cp("act", View(ARbd.t[pb:pb + 64, :, e, 0, :], ARbd.tk), View(FT.t[pb:pb + 64, 2, :, :], FT.tk))
            k.cp("act", View(ARbd.t[pb:pb + 64, :, e, 1, :], ARbd.tk), View(RT.t[pb:pb + 64, :, :], RT.tk))
            k.cp("act", View(Bbd.t[pb:pb + 64, :, e, :], Bbd.tk), View(BK.t[pb:pb + 64, :, 0, :], BK.tk))
        k.tt("pool", kq[:], r, k2[:], ALU.mult)
        for c2 in range(2):
            k.act(rk[:, c2, :], kq[:, c2, :], AF.Copy, scale=pvt[:, 17 + c2:18 + c2])
        psT = ps_rot.next()
        psT16 = psT.same(BF16)
        k.pe([("T", psT16[:, (kind * 2 + c2) * 128:(kind * 2 + c2 + 1) * 128], FT[:, kind, c2, :], identb[:])
              for kind in range(4) for c2 in range(2)])
        psTv = psT.t.ap().bitcast(BF16)[:, 0:1024].rearrange("p (a b c) -> p a b c", a=4, b=2)
        k.cp("act", TM[:], View(psTv, psT.tk))
        for ch in range(2):
            pt = 64 * ch
            k.cp("act", View(TMbd.t[pt:pt + 64, :, :, ch, :], TMbd.tk), View(psTv[pt:pt + 64, 0:2, :, :], psT.tk))
        UL = ULr.next()
        mk1 = View(mkb.t[:, 0:512].rearrange("p (a b c) -> p a b c", a=2, b=2), mkb.tk)
        for c2 in range(2):
            arv = View(ARbd.t[:, c2, :, :, :].rearrange("p a b c -> p (a b c)"), ARbd.tk)
            for kind in range(2):
                psg = ps_rot.next()
                k.pe([(psg[:, :], BK[:, c2, kind, :], arv, True, True)])
                k.tt("dve", S4[:, 2 * c2:2 * c2 + 2, 2 * kind:2 * kind + 2, :],
                     View(psg.t[:, :].rearrange("p (a b c) -> p a b c", a=2, b=2), psg.tk), mk1, ALU.mult)
        psLo = ps_rot.next()
        k.pe([(psLo[:, c2 * 256:(c2 + 1) * 256], FT[:, 2, c2, :], View(Bbd.t[:, c2, :, :].rearrange("p a b -> p (a b)"), Bbd.tk), True, True)
              for c2 in range(2)])
        k.tt("dve", UL[:, :, 1, :], View(psLo.t[:, :].rearrange("p (h q) -> p h q", h=4), psLo.tk),
             View(mkb.t[:, 512:1024].rearrange("p (h q) -> p h q", h=4), mkb.tk), ALU.mult)
        k.cp("act", UL[:, :, 0, :], S4[:, :, 0, :])
        Y = Yr.next()
        psY = ps_rot.next()
        k.pe([(psY[:, hh * 64:(hh + 1) * 64], S4[:, hh, 2, :], TM[:, 3, c2, 64 * e:64 * e + 64], True, True)
              for hh, (c2, e) in enumerate(heads)])
        k.cp("act", Y[:, :, 0:64], View(TM.t[:, 2, :, :].rearrange("p a (e q) -> p (a e) q", e=2), TM.tk))
        k.cp("dve", Y[:, :, 64:128], View(psY.t[:, 0:256].rearrange("p (h q) -> p h q", h=4), psY.tk))
        for lvl in range(6):
            psY = ps_rot.next()
            k.pe([(psY[:, hh * 128:(hh + 1) * 128], UL[:, hh, 0, :], Y[:, hh, :], True, True) for hh in range(4)])
            if lvl < 5:
                psU = [ps_rot.next(), ps_rot.next()]
                items = []
                for hh in range(4):
                    pu = psU[hh // 2]
                    o0 = (hh % 2) * 256
                    items.append((pu[:, o0:o0 + 128], UL[:, hh, 1, :], UL[:, hh, 0, :], True, True))
                    if lvl < 4:
                        items.append((pu[:, o0 + 128:o0 + 256], UL[:, hh, 0, :], UL[:, hh, 1, :], True, True))
                k.pe(items)
            Yn = Yr.next()
            k.tt("dve", Yn[:], View(psY.t[:, :].rearrange("p (h q) -> p h q", h=4), psY.tk), Y[:], ALU.add)
            Y = Yn
            if lvl < 5:
                UL = ULr.next()
                for half in range(2):
                    k.cp("act", UL[:, 2 * half:2 * half + 2, :, :], View(psU[half].t[:, :].rearrange("p (h a q) -> p h a q", h=2, a=2), psU[half].tk))
        psP = ps_rot.next()
        items = []
        for hh, (c2, e) in enumerate(heads):
            pbk = 64 * e
            m1 = Y[:, hh, 0:64]
            items.append((View(psP.t[pbk:pbk + 64, c2 * 256:c2 * 256 + 128], psP.tk), m1,
                          View(TMbd.t[:, 0, c2, :, pbk:pbk + 64], TMbd.tk), True, True))
            items.append((View(psP.t[pbk:pbk + 64, c2 * 256 + 128:c2 * 256 + 256], psP.tk), m1, S4[:, hh, 1, :], True, True))
        k.pe(items)
        ppv = psP.t[:, :].rearrange("p (a x) -> p a x", a=2)
        for e in range(2):
            pbk = 64 * e
            k.cp("dve", View(PTbd.t[pbk:pbk + 64, :, :, pbk:pbk + 64], PTbd.tk),
                 View(ppv[pbk:pbk + 64, :, 0:128].rearrange("p a (c q) -> p a c q", c=2), psP.tk))
        k.tt("dve", GTs[:], View(ppv[:, :, 128:256].rearrange("p a (c q) -> p a c q", c=2), psP.tk),
             View(RT.t[:, :, :].rearrange("p a (c q) -> p a c q", c=2), RT.tk), ALU.add)
        psBn = ps_rot.next()
        k.pe([(psBn[:, 2 * c2:2 * c2 + 2], rk[:, c2, :], bones[:], True, True) for c2 in range(2)])
        k.cp("act", bsum[:], psBn[:, 0:4])
        psYo = ps_rot.next()
        for ch in range(2):
            pt = 64 * ch
            psH = ps_rot.next()
            items = []
            for c2 in range(2):
                items.append((View(psYo.t[pt:pt + 64, c2 * 128:(c2 + 1) * 128], psYo.tk), GTs[:, c2, ch, :],
                              View(Hbd.t[:, c2, :, :].rearrange("p a b -> p (a b)"), Hbd.tk), True, False))
                for e in range(2):
                    hh = 2 * c2 + e
                    yo = View(psYo.t[pt:pt + 64, hh * 64:(hh + 1) * 64], psYo.tk)
                    items.append((yo, S4[:, hh, 1, pt:pt + 64], Y[:, hh, 64:128], False, False))
                    items.append((yo, S4[:, hh, 3, pt:pt + 64], TM[:, 3, c2, 64 * e:64 * e + 64], False, e == 1))
            for c2 in range(2):
                items.append((psH[:, c2 * 64:(c2 + 1) * 64], PTbd[:, c2, ch, :], Hb[:, c2, :], True, False))
                for e in range(2):
                    hh = 2 * c2 + e
                    pbk = 64 * e
                    ho = View(psH.t[pbk:pbk + 64, c2 * 64:(c2 + 1) * 64], psH.tk)
                    items.append((ho, TMbd[:, 0, c2, ch, pbk:pbk + 64], Y[:, hh, 64:128], False, False))
                    items.append((ho, TMbd[:, 1, c2, ch, pbk:pbk + 64], TM[:, 3, c2, pbk:pbk + 64], False, True))
            k.pe(items)
            for c2 in range(2):
                k.stt(H32[:, c2, :], H32[:, c2, :], WC[:, c2, ch:ch + 1], psH[:, c2 * 64:(c2 + 1) * 64], ALU.mult, ALU.add)
            k.cp("act", Hb[:], H32[:])
            for e in range(2):
                pbk = 64 * e
                k.cp("act", View(Hbd.t[pbk:pbk + 64, :, e, :], Hbd.tk), View(H32.t[pbk:pbk + 64, :, :], H32.tk))
        if j == 0:
            return
        k.cp("act", yv[:], View(psYo.t[:, 0:256].rearrange("p (h q) -> p h q", h=4), psYo.tk))
        k.red(gst[:, 0:4], yv[:])
        k.tt("pool", ytmp[:], yv[:], yv[:], ALU.mult)
        k.red(gst[:, 4:8], ytmp[:])
        k.ts("dve", gst[:, 0:8], gst[:, 0:8], 1.0 / 64, ALU.mult)
        k.tt("dve", gst[:, 8:12], gst[:, 0:4], gst[:, 0:4], ALU.mult)
        k.tt("dve", gst[:, 8:12], gst[:, 4:8], gst[:, 8:12], ALU.subtract)
        k.act(gst[:, 8:12], gst[:, 8:12], AF.Sqrt, bias=64e-5)
        k.recip(gst[:, 12:16], gst[:, 8:12])
        for hh in range(4):
            k.ts("dve", ytmp[:, hh, :], yv[:, hh, :], gst[:, hh:hh + 1], ALU.subtract, gst[:, 12 + hh:13 + hh], ALU.mult)
        yt2 = View(ytmp.t[:, :, :].rearrange("p h q -> p (h q)"), ytmp.tk)
        k.tt("pool", yt2, yt2, lnw_t[:, 0, :], ALU.mult)
        k.tt("pool", yt2, yt2, lnw_t[:, 1, :], ALU.add)
        for hh, (c2, e) in enumerate(heads):
            k.stt(ytmp[:, hh, :], TM[:, 3, c2, 64 * e:64 * e + 64], bsum[:, hh:hh + 1], ytmp[:, hh, :], ALU.mult, ALU.add)
        k.tt("dve", mix[:, 0:256], yt2, gtm[:], ALU.mult)

    def outproj_tile(j, mix):
        psT = ps_rot.next()
        psT16 = psT.same(BF16)
        k.pe([("T", psT16[:, c * 128:(c + 1) * 128], mix[:, c * 128:(c + 1) * 128], identb[:]) for c in range(4)])
        mixT = mixT_r.next()
        k.cp("act", mixT[:], View(psT.t.ap().bitcast(BF16)[:, 0:512].rearrange("p (a b) -> p a b", a=4), psT.tk))
        po, spo = po_r.next()
        for br in range(2):
            for half in range(2):
                ps = ps_rot.next()
                k.pe([(ps[:, :], mixT[:, 2 * br + kc, :], wob[:, 2 * br + kc, half * 512:(half + 1) * 512], kc == 0, kc == 1) for kc in range(2)])
                k.cp("act" if half == 0 else "dve", po[:, br * 1024 + half * 512:br * 1024 + (half + 1) * 512], ps[:, :])
        i = j - 1
        k.dma("sp", View(rs_in[i // 16].t[(i % 16) * 128:(i % 16 + 1) * 128, :], rs_tk[i // 16][i % 16]), po[:], spo)
        if dbg:
            dm, sdm = dmix_r.next()
            k.cp("pool", dm[:], mix[:])
            k.dma("sp", d_mix[j * 128:(j + 1) * 128, :], dm[:], sdm)

    cc_sem = es.enter_context(nc.semaphore("cc_sem"))
    k.engs["cc"] = EngS("cc", None, cc_sem)
    n_cc = 0
    def issue_rs(j):
        nonlocal n_cc
        if do_rs and j >= 1 and j % 16 == 0:
            q = j // 16 - 1
            E = k.engs["pool"]
            k._wait(E, rs_tk[q], [rs_out[q].tk])
            nc.gpsimd.collective_compute("ReduceScatter", ALU.add, replica_groups=[[0, 1, 2, 3], [4, 5, 6, 7]],
                                         ins=[rs_in[q].t.ap().opt()], outs=[rs_out[q].t.ap().opt()]).then_inc(cc_sem)
            n_cc += 1
            for t_ in rs_tk[q]:
                t_.rd["cc"] = n_cc
            rs_out[q].tk.lw = ("cc", n_cc)
            rs_out[q].tk.rd = {}

    tiles = [{"j": j} for j in range(NT1)]

    def P(t):
        t["sh"], t["QT"] = p1_tile(t["j"])

    def R(t):
        t["mix"] = mix_r.next()
        rwkv_tile(t["j"], t["sh"], t["mix"])

    def Y(t):
        attn_tile(t["j"], t["QT"], t["mix"])
        outproj_tile(t["j"], t["mix"])

    lenP, lenR = 40, 230
    if NT1 > 0:
        n0 = k.nops
        P(tiles[0])
        lenP = k.nops - n0
    for j in range(NT1):
        fns, ests = [], []
        fns.append(lambda t=tiles[j]: R(t))
        ests.append(lenR)
        if j >= 2:
            jj = j - 1
            fns.append(lambda t=tiles[jj]: Y(t))
            ests.append(4 * (3 * ((jj + 4) // 4) + 2) + 40)
        else:
            fns.append(lambda: None)
            ests.append(1)
        if j + 1 < NT1:
            fns.append(lambda t=tiles[j + 1]: P(t))
            ests.append(lenP)
        done = k.weave.run(fns, ests)
        lenR = max(1, done[0])
        if j >= 2:
            issue_rs(j - 1)
    if NT1 >= 2:
        Y(tiles[NT1 - 1])
        issue_rs(NT1 - 1)

    k.barrier()
    es1.close()
    if do_p2:
        src = rs_dbg if rs_dbg is not None else rs_out
        esA = ExitStack()
        k.es = es
        h2 = k.sb([128, 16, D], F32, "h2")
        k.es = esA
        wgb = k.sb([128, 8, 2048], BF16, "wgb")
        woutb = k.sb([128, 8, D], BF16, "woutb")
        stgA = Rot([(k.sb([128, D], F32, f"stgA{i}"), k.dsem()) for i in range(4)])
        for kc in range(8):
            for hf in range(2):
                k.ldcast(wgb[:, kc, hf * D:(hf + 1) * D], wg[kc * 128:(kc + 1) * 128, hf * D:(hf + 1) * D], stgA, "act" if hf == 0 else "dve")
        for kc in range(8):
            k.ldcast(woutb[:, kc, :], wout[kc * 128:(kc + 1) * 128, :], stgA, "act" if kc % 2 == 0 else "dve")
        nw2 = k.sb([128, D], F32, "nw2")
        k.dma("sp", nw2[:], nwb[:, :], k.dsem())
        x2_r = Rot([(k.sb([128, 4, D], F32, f"x2t{i}"), k.dsem()) for i in range(1)])
        rs_r = Rot([(k.sb([128, 4, 2048], BF16, f"rst{i}"), k.dsem()) for i in range(1)])
        st2 = k.sb([128, 8], F32, "st2")
        xs2 = k.sb([128, D], BF16, "xs2")
        hT2 = k.sb([128, 8, 512], BF16, "hT2")
        gate = k.sb([128, 2048], BF16, "gate")
        mrg = k.sb([128, D], BF16, "mrg")
        mrg2 = k.sb([128, D], BF16, "mrg2")
        mT = k.sb([128, 8, 128], BF16, "mT")
        gateB = k.sb([128, 2048], BF16, "gateB")
        mrgB = k.sb([128, D], BF16, "mrgB")
        mrg2B = k.sb([128, D], BF16, "mrg2B")
        mTB = k.sb([128, 8, 128], BF16, "mTB")

        def norm_T(xin, nwt, xs_, st_, dstT, col0):
            k.act(xs_[:], xin, AF.Square, accum=st_[:, 0:1])
            k.act(st_[:, 1:2], st_[:, 0:1], AF.Sqrt, bias=1e-5, scale=1.0 / D)
            k.recip(st_[:, 2:3], st_[:, 1:2])
            k.stt(xs_[:], xin, st_[:, 2:3], nwt, ALU.mult, ALU.mult)
            ps = ps_rot.next()
            ps16 = ps.same(BF16)
            k.pe([("T", ps16[:, kc * 128:(kc + 1) * 128], xs_[:, kc * 128:(kc + 1) * 128], identb[:]) for kc in range(8)])
            k.cp("act", View(dstT.t[:, :, col0:col0 + 128], dstT.tk),
                 View(ps.t.ap().bitcast(BF16)[:, 0:1024].rearrange("p (a b) -> p a b", a=8), ps.tk))

        for kq in range(4):
            x2t, sx2 = x2_r.next()
            k.dma("sp", x2t[:], View(x2.t[kq].rearrange("(s p) d -> p s d", p=128), x2.tk), sx2)
            rst, srs = rs_r.next()
            k.dma("sp", rst[:], View(src[kq].t.ap().rearrange("(s p) d -> p s d", p=128), src[kq].tk), srs)
            for s_ in range(4):
                norm_T(x2t[:, s_, :], nw2[:], xs2, st2, hT2, s_ * 128)
            def sub2a(s_, gate, mrg, mrg2, mT, kq=kq, x2t=x2t, rst=rst):
                for cb in range(4):
                    ps = ps_rot.next()
                    k.pe([(ps[:, :], hT2[:, kc, s_ * 128:(s_ + 1) * 128], wgb[:, kc, cb * 512:(cb + 1) * 512], kc == 0, kc == 7) for kc in range(8)])
                    k.act(gate[:, cb * 512:(cb + 1) * 512], ps[:, :], AF.Sigmoid)
                k.tt("pool", mrg[:], gate[:, 0:D], rst[:, s_, 0:D], ALU.mult)
                k.tt("dve", mrg2[:], gate[:, D:2 * D], rst[:, s_, D:2 * D], ALU.mult)
                k.tt("pool", mrg[:], mrg[:], mrg2[:], ALU.add)
                ps = ps_rot.next()
                ps16 = ps.same(BF16)
                k.pe([("T", ps16[:, kc * 128:(kc + 1) * 128], mrg[:, kc * 128:(kc + 1) * 128], identb[:]) for kc in range(8)])
                k.cp("act", mT[:], View(ps.t.ap().bitcast(BF16)[:, 0:1024].rearrange("p (a b) -> p a b", a=8), ps.tk))
                for cb in range(2):
                    ps = ps_rot.next()
                    k.pe([(ps[:, :], mT[:, kc, :], woutb[:, kc, cb * 512:(cb + 1) * 512], kc == 0, kc == 7) for kc in range(8)])
                    k.tt("dve", h2[:, kq * 4 + s_, cb * 512:(cb + 1) * 512], ps[:, :], x2t[:, s_, cb * 512:(cb + 1) * 512], ALU.add)

            for s0 in (0, 2):
                k.weave.run([lambda: sub2a(s0, gate, mrg, mrg2, mT), lambda: sub2a(s0 + 1, gateB, mrgB, mrg2B, mTB)], [1, 1])
        k.barrier()
        esA.close()
        esB = ExitStack()
        k.es = esB
        nmt = k.sb([128, 2, D], F32, "nmt")
        k.dma("sp", nmt[:], nm2[:, :, :], k.dsem())
        hnT = k.sb([128, 8, 2048], BF16, "hnT")
        xs3 = k.sb([128, D], BF16, "xs3")
        st3 = k.sb([128, 8], F32, "st3")
        esC = ExitStack()
        k.es = esC
        wq_r = Rot([(k.sb([128, 8, D], BF16, f"w1q{i}"), k.sb([128, 8, D], BF16, f"w2q{i}")) for i in range(2)])
        aT_r = rot(2, [128, 8, 512], BF16, "aT")
        rl_r = rot(2, [128, 512], BF16, "rl")
        stgB = Rot([(k.sb([128, D], F32, f"stgB{i}"), k.dsem()) for i in range(4)])

        def load_q(p):
            w1q, w2q = wq_r.next()
            for kc in range(8):
                k.ldcast(w1q[:, kc, :], mw1[kc * 128:(kc + 1) * 128, p * D:(p + 1) * D], stgB, "pool" if p > 0 else ("act", "dve")[kc % 2])
            for fc in range(8):
                k.ldcast(w2q[:, fc, :], mw2[p * D + fc * 128:p * D + (fc + 1) * 128, :], stgB, "pool" if p > 0 else ("act", "dve")[fc % 2])
            return w1q, w2q

        nxt = load_q(0)
        for t16 in range(16):
            norm_T(h2[:, t16, :], nmt[:, 0, :], xs3, st3, hnT, t16 * 128)
        for p in range(4):
            w1q, w2q = nxt
            if p < 3:
                nxt = load_q(p + 1)
            def up(kq, aT, w1q=w1q):
                for fc in range(8):
                    ps = ps_rot.next()
                    k.pe([(ps[:, :], w1q[:, kc, fc * 128:(fc + 1) * 128], hnT[:, kc, kq * 512:(kq + 1) * 512], kc == 0, kc == 7) for kc in range(8)])
                    rl = rl_r.next()
                    k.act(rl[:], ps[:, :], AF.Relu)
                    k.tt("pool", aT[:, fc, :], rl[:], rl[:], ALU.mult)

            def down(kq, aT, w2q=w2q):
                for s_ in range(4):
                    for cb in range(2):
                        ps = ps_rot.next()
                        k.pe([(ps[:, :], aT[:, fc, s_ * 128:(s_ + 1) * 128], w2q[:, fc, cb * 512:(cb + 1) * 512], fc == 0, fc == 7) for fc in range(8)])
                        hv = h2[:, kq * 4 + s_, cb * 512:(cb + 1) * 512]
                        k.tt("dve", hv, ps[:, :], hv, ALU.add)

            aTs = [aT_r.next() for _ in range(4)]
            up(0, aTs[0])
            for kq in range(4):
                if kq < 3:
                    k.weave.run([lambda: down(kq, aTs[kq]), lambda: up(kq + 1, aTs[kq + 1])], [1, 1])
                else:
                    down(kq, aTs[kq])
        k.barrier()
        esC.close()
        k.es = esB
        yo_r = Rot([(k.sb([128, D], F32, f"yo{i}"), k.dsem()) for i in range(2)])
        for t16 in range(16):
            yo, syo = yo_r.next()
            hv = h2[:, t16, :]
            k.act(yo[:], hv, AF.Square, accum=st3[:, 0:1])
            k.act(st3[:, 1:2], st3[:, 0:1], AF.Sqrt, bias=1e-5, scale=1.0 / D)
            k.recip(st3[:, 2:3], st3[:, 1:2])
            k.stt(yo[:], hv, st3[:, 2:3], nmt[:, 1, :], ALU.mult, ALU.mult)
            k.dma("sp", yout[t16 // 4, (t16 % 4) * 128:(t16 % 4 + 1) * 128, :], yo[:], syo)
        k.barrier()
        esB.close()
    k.barrier()
    print("total ops", k.nops, {n: e.count for n, e in k.engs.items() if e.count})
    return nc


def make_consts():
    c = np.zeros((128, 1024), np.float32)
    p = np.arange(128)[:, None]
    q = np.arange(128)[None, :]
    c[:, 0:128] = (p == q)
    c[:, 128:256] = (p <= q)
    c[:, 256:384] = (p // 64 == q // 64)
    pm = np.zeros((128, 128), np.float32)
    for hb in (0, 64):
        for d in range(8):
            pm[hb + d + 8, hb + d] = -1.0
            pm[hb + d, hb + d + 8] = 1.0
    c[:, 384:512] = pm
    c[:, 512:576] = (p % 64 == np.arange(64)[None, :])
    s = (np.arange(128) % 64)[:, None]
    t = np.arange(64)[None, :]
    c[:, 576:640] = (s < t)
    c[:, 640:704] = (s <= t)
    c[:, 704:768] = (s < t)
    c[:, 768:832] = (s <= t)
    c[:, 832:896] = (s > t)
    c[:, 896:1024] = 1.0
    c[:, 896] = 0.0
    c[:, 960] = 0.0
    return c


def make_consts2():
    c = np.zeros((128, 1024), np.float32)
    p = np.arange(128)[:, None]
    q = np.arange(128)[None, :]
    same = (p // 64 == q // 64)
    strict = same & ((p % 64) < (q % 64))
    incl = same & ((p % 64) <= (q % 64))
    lo = same & ((p % 64) > (q % 64))
    for e in range(2):
        c[:, e * 256:e * 256 + 128] = strict
        c[:, e * 256 + 128:e * 256 + 256] = incl
    for h in range(4):
        c[:, 512 + h * 128:512 + (h + 1) * 128] = lo
    return c


def make_rope():
    half = 8
    inv = (np.float32(500000.0) ** (-(np.arange(0, 16, 2, dtype=np.float32)) / np.float32(16))).astype(np.float32)
    pos = np.maximum(np.arange(LP) - 112, 0).astype(np.float32)
    ang = pos[:, None] * inv[None, :]
    cos = np.cos(ang).astype(np.float32)
    sin = np.sin(ang).astype(np.float32)
    ct = np.ones((128, LP), np.float32)
    stb = np.zeros((128, LP), np.float32)
    for hb in (0, 64):
        for d in range(16):
            ct[hb + d] = cos[:, d % 8]
            stb[hb + d] = sin[:, d % 8]
    r = np.stack([ct.reshape(128, NT, 128), stb.reshape(128, NT, 128)], axis=2)
    return np.ascontiguousarray(r)


def core_inputs(inp, c, consts, rope):
    b, g = c // 4, c % 4
    f = lambda a: np.ascontiguousarray(np.asarray(a, dtype=np.float32))
    w_in = inp["w_in"][0]
    sl = slice(256 * g, 256 * g + 256)
    cols = np.concatenate([np.arange(256 * g, 256 * g + 256), 1024 + np.arange(256 * g, 256 * g + 256),
                           2048 + np.arange(256 * g, 256 * g + 256), np.arange(3072, 3360),
                           3360 + np.arange(256 * g, 256 * g + 256), 4384 + np.arange(256 * g, 256 * g + 256),
                           5408 + np.arange(256 * g, 256 * g + 256)])
    mu = inp["rwkv_mu"][0]
    pv = np.zeros((128, 32), np.float32)
    mucols = [mu[0 + 256 * g:0 + 256 * g + 128], mu[256 * g + 128:256 * g + 256],
              mu[1024 + 256 * g:1024 + 256 * g + 128], mu[1024 + 256 * g + 128:1024 + 256 * g + 256],
              mu[2048 + 256 * g:2048 + 256 * g + 128], mu[2048 + 256 * g + 128:2048 + 256 * g + 256],
              mu[3072:3200], mu[3200:3328]]
    for i, m in enumerate(mucols):
        pv[:, i] = m
    pv[0:32, 8] = mu[3328:3360]
    for i, nm in enumerate(["rwkv_w0", "rwkv_a0", "rwkv_k_k", "rwkv_k_a"]):
        vv = inp[nm][0][sl]
        pv[:, 9 + 2 * i] = vv[0:128]
        pv[:, 10 + 2 * i] = vv[128:256]
    rkv = inp["rwkv_r_k"][0].reshape(-1)[sl]
    pv[:, 17] = rkv[0:128]
    pv[:, 18] = rkv[128:256]
    q = g
    x2 = np.stack([inp["x"][b, 2048 * kk + 512 * q:2048 * kk + 512 * q + 512] for kk in range(4)], 0)
    d = {
        "xb": f(inp["x"][b]),
        "meta": f(inp["meta_tokens"]),
        "x2": f(x2),
        "w1c": f(w_in[:, cols]),
        "wg": f(w_in[:, 6432:8480]),
        "pv": pv,
        "nwb": f(np.broadcast_to(inp["norm_mix_w"][0][None, :], (128, D))),
        "lnwb": f(np.broadcast_to(np.stack([inp["rwkv_ln_w"][0][sl], inp["rwkv_ln_b"][0][sl]], 0)[None], (128, 2, 256))),
        "sublnb": f(np.broadcast_to(inp["diff_subln_w"][0][None, :], (128, 128))),
        "lamv": f(np.broadcast_to(np.stack([inp["diff_lq1"][0], inp["diff_lk1"][0], inp["diff_lq2"][0], inp["diff_lk2"][0]], 0)[None], (128, 4, 64))),
        "w2a2": f(np.concatenate([inp["rwkv_w2"][0][:, sl], inp["rwkv_a2"][0][:, sl]], 0)),
        "g2": f(inp["rwkv_g2"][0][:, sl]),
        "wor": f(inp["rwkv_w_o"][0][sl, :]),
        "wod": f(inp["diff_w_o"][0][sl, :]),
        "wout": f(inp["w_out"][0]),
        "mw1": f(inp["mlp_w1"][0]),
        "mw2": f(inp["mlp_w2"][0]),
        "nm2": f(np.broadcast_to(np.stack([inp["norm_mlp_w"][0], inp["final_norm_w"]], 0)[None], (128, 2, D))),
        "rope": rope,
        "cst": consts,
        "cst2": make_consts2(),
    }
    return d


_CACHE = {}


def kernel(**inputs):
    inp = {kk: np.asarray(v) for kk, v in inputs.items()}
    if "nc" not in _CACHE:
        _CACHE["nc"] = build({})
    nc = _CACHE["nc"]
    consts = make_consts()
    rope = make_rope()
    in_maps = [core_inputs(inp, c, consts, rope) for c in range(8)]
    res = run_bass_kernel_spmd(nc, in_maps, core_ids=list(range(8)))
    out = np.zeros((2, 8192, D), np.float32)
    for c in range(8):
        b, q = c // 4, c % 4
        y = np.asarray(res.results[c]["yout"])
        for kk in range(4):
            out[b, 2048 * kk + 512 * q:2048 * kk + 512 * q + 512] = y[kk]
    return out
```

```python
import math
import threading
from contextlib import ExitStack
import numpy as np
import ml_dtypes
import concourse.bass as bass
import concourse.mybir as mybir
from concourse.bass_utils import run_bass_kernel_spmd

F32 = mybir.dt.float32
BF16 = mybir.dt.bfloat16
AF = mybir.ActivationFunctionType
ALU = mybir.AluOpType
AX = mybir.AxisListType

D = 1024
NT = 65
LP = NT * 128
C0 = math.exp(-0.5)
LAMBDA_INIT = 0.8 - 0.6 * math.exp(0.0)


class Tk:
    __slots__ = ("lw", "rd", "name", "psum")

    def __init__(self, name=""):
        self.lw = None
        self.rd = {}
        self.name = name
        self.psum = False


class View:
    __slots__ = ("ap", "tk")

    def __init__(self, ap, tk):
        self.ap = ap
        self.tk = tk


class Buf:
    def __init__(self, t, name, tk=None, dt=None):
        self.t = t
        self.name = name
        self.tk = tk if tk is not None else Tk(name)
        self.dt = dt

    def __getitem__(self, idx):
        ap = self.t[idx] if self.dt is None else self.t.ap().bitcast(self.dt)[idx]
        return View(ap, self.tk)

    def alias(self, name, dt=None):
        return Buf(self.t, name, dt=dt if dt is not None else self.dt)

    def same(self, dt):
        return Buf(self.t, self.name, tk=self.tk, dt=dt)


class EngS:
    def __init__(self, name, eng, sem):
        self.name = name
        self.eng = eng
        self.sem = sem
        self.count = 0
        self.waited = {}


class K:
    def __init__(self, nc, es):
        self.nc = nc
        self.es = es
        self.es_sem = es
        self.engs = {}
        for n, e in (("pe", nc.tensor), ("act", nc.scalar), ("dve", nc.vector), ("pool", nc.gpsimd), ("sp", nc.sync)):
            self.engs[n] = EngS(n, e, es.enter_context(nc.semaphore("sem_" + n)))
        self.nd = 0
        self.nsb = 0
        self.nops = 0
        self.limit = 10 ** 9
        self.weave = Weave()

    def sb(self, shape, dt, name=None):
        self.nsb += 1
        name = name or f"sb{self.nsb}"
        return Buf(self.es.enter_context(self.nc.sbuf_tensor(name, list(shape), dt)), name)

    def dsem(self, name=None):
        self.nd += 1
        name = name or f"dsem{self.nd}"
        e = EngS(name, None, self.es_sem.enter_context(self.nc.semaphore(name)))
        self.engs[name] = e
        return e

    def _wait(self, E, reads, writes):
        deps = {}
        for t in reads:
            if t.lw is not None and deps.get(t.lw[0], 0) < t.lw[1]:
                deps[t.lw[0]] = t.lw[1]
            if t.psum:
                for n, c in t.rd.items():
                    if n != E.name and deps.get(n, 0) < c:
                        deps[n] = c
        for t in writes:
            if t.lw is not None and deps.get(t.lw[0], 0) < t.lw[1]:
                deps[t.lw[0]] = t.lw[1]
            for n, c in t.rd.items():
                if deps.get(n, 0) < c:
                    deps[n] = c
        for n, c in deps.items():
            if n == E.name and n == "pe":
                continue
            if E.waited.get(n, 0) >= c:
                continue
            E.eng.wait_ge(self.engs[n].sem, c)
            E.waited[n] = c

    def op(self, en, fn, reads, writes):
        self.weave.checkpoint()
        self.nops += 1
        if self.nops > self.limit:
            return
        E = self.engs[en]
        self._wait(E, reads, writes)
        ins = fn(E.eng)
        E.count += 1
        ins.then_inc(E.sem, 1)
        for t in reads:
            t.rd[en] = E.count
        for t in writes:
            t.lw = (en, E.count)
            t.rd = {}

    def dma(self, qn, out, in_, dsem, **kw):
        self.weave.checkpoint()
        self.nops += 1
        if self.nops > self.limit:
            return
        Q = self.engs[qn]
        self._wait(Q, [in_.tk], [out.tk])
        ins = Q.eng.dma_start(out=out.ap, in_=in_.ap, **kw)
        dsem.count += 16
        ins.then_inc(dsem.sem, 16)
        in_.tk.rd[dsem.name] = dsem.count
        out.tk.lw = (dsem.name, dsem.count)
        out.tk.rd = {}

    def barrier(self):
        for en in ("sp", "pool", "act", "dve", "pe"):
            E = self.engs[en]
            for n, o in self.engs.items():
                if n == en or o.count == 0:
                    continue
                if E.waited.get(n, 0) < o.count:
                    E.eng.wait_ge(o.sem, o.count)
                    E.waited[n] = o.count

    def ldcast(self, dst, src, stg_r, eng="pool"):
        stg, sem = stg_r.next()
        sv = View(stg.t[:, 0:dst.ap.shape[-1]], stg.tk)
        self.dma("sp", sv, src, sem)
        self.cp(eng, dst, sv)

    def tt(self, en, out, a, b, op):
        self.op(en, lambda e: e.tensor_tensor(out=out.ap, in0=a.ap, in1=b.ap, op=op), [a.tk, b.tk], [out.tk])

    def ts(self, en, out, a, s1, op0, s2=None, op1=None):
        rd = [a.tk]
        v1 = s1
        v2 = s2
        if isinstance(s1, View):
            rd.append(s1.tk)
            v1 = s1.ap
        if isinstance(s2, View):
            rd.append(s2.tk)
            v2 = s2.ap
        kw = {}
        if en == "pool" and op1 is None and s2 is None and op0 == ALU.mult:
            op1 = ALU.add
            v2 = 0.0
        if op1 is not None:
            kw["op1"] = op1
        self.op(en, lambda e: e.tensor_scalar(out=out.ap, in0=a.ap, scalar1=v1, scalar2=v2, op0=op0, **kw), rd, [out.tk])

    def stt(self, out, a, s, b, op0, op1):
        rd = [a.tk, b.tk]
        v = s
        if isinstance(s, View):
            rd.append(s.tk)
            v = s.ap
        self.op("dve", lambda e: e.scalar_tensor_tensor(out=out.ap, in0=a.ap, scalar=v, in1=b.ap, op0=op0, op1=op1), rd, [out.tk])

    def cp(self, en, out, a):
        if en == "act":
            self.op(en, lambda e: e.copy(out=out.ap, in_=a.ap), [a.tk], [out.tk])
        else:
            self.op(en, lambda e: e.tensor_copy(out=out.ap, in_=a.ap), [a.tk], [out.tk])

    def act(self, out, a, func, bias=None, scale=1.0, accum=None):
        rd = [a.tk]
        wr = [out.tk]
        kw = {}
        if isinstance(bias, View):
            rd.append(bias.tk)
            kw["bias"] = bias.ap
        elif bias is not None:
            kw["bias"] = bias
        if isinstance(scale, View):
            rd.append(scale.tk)
            kw["scale"] = scale.ap
        else:
            kw["scale"] = scale
        if accum is not None:
            wr.append(accum.tk)
            kw["accum_out"] = accum.ap
        self.op("act", lambda e: e.activation(out=out.ap, in_=a.ap, func=func, **kw), rd, wr)

    def memset(self, en, out, val):
        self.op(en, lambda e: e.memset(out.ap, val), [], [out.tk])

    def red(self, out, a, op=ALU.add):
        self.op("dve", lambda e: e.tensor_reduce(out=out.ap, in_=a.ap, op=op, axis=AX.X), [a.tk], [out.tk])

    def recip(self, out, a):
        self.op("dve", lambda e: e.reciprocal(out=out.ap, in_=a.ap), [a.tk], [out.tk])

    def pe(self, items):
        rd = []
        wr = []
        for it in items:
            if it[0] == "T":
                wr.append(it[1].tk)
                rd += [it[2].tk, it[3].tk]
            else:
                wr.append(it[0].tk)
                rd += [it[1].tk, it[2].tk]

        def fn(e):
            ins = None
            for it in items:
                if it[0] == "T":
                    ins = e.transpose(it[1].ap, it[2].ap, it[3].ap)
                else:
                    ins = e.matmul(it[0].ap, lhsT=it[1].ap, rhs=it[2].ap, start=it[3], stop=it[4])
            return ins

        self.op("pe", fn, rd, wr)


class Weave:
    def __init__(self):
        self.cv = threading.Condition()
        self.active = None
        self.tl = threading.local()

    def current(self):
        return getattr(self.tl, "sid", 0) if self.active is not None else None

    def _pick(self):
        best = None
        for i in range(len(self.alive)):
            if self.alive[i]:
                f = self.done[i] / self.est[i]
                if best is None or f < best[0]:
                    best = (f, i)
        self.active = best[1] if best is not None else -1

    def checkpoint(self):
        if self.active is None:
            return
        sid = self.tl.sid
        with self.cv:
            self.done[sid] += 1
            self._pick()
            self.cv.notify_all()
            while self.active != sid:
                self.cv.wait()

    def run(self, fns, ests):
        n = len(fns)
        self.done = [0] * n
        self.est = [max(1, e) for e in ests]
        self.alive = [True] * n
        self.err = []

        def body(i):
            self.tl.sid = i
            with self.cv:
                while self.active != i:
                    self.cv.wait()
            try:
                fns[i]()
            except BaseException as ex:
                self.err.append(ex)
            with self.cv:
                self.alive[i] = False
                self._pick()
                self.cv.notify_all()

        ths = [threading.Thread(target=body, args=(i,)) for i in range(n)]
        with self.cv:
            self.active = 0
        for t in ths:
            t.start()
        for t in ths:
            t.join()
        self.active = None
        if self.err:
            raise self.err[0]
        return list(self.done)


class RotSel:
    def __init__(self, weave, pools):
        self.weave = weave
        self.pools = pools

    def next(self):
        c = self.weave.current()
        return self.pools[-1 if c is None else c].next()


class Rot:
    def __init__(self, bufs):
        self.bufs = bufs
        self.i = 0

    def next(self):
        b = self.bufs[self.i % len(self.bufs)]
        self.i += 1
        return b


def build(cfg):
    NT1 = cfg.get("nt1", NT)
    do_p2 = cfg.get("p2", True)
    do_rs = cfg.get("rs", True)
    dbg = cfg.get("dbg", False)
    nc = bass.Bass("TRN2", target_bir_lowering=False)
    es = ExitStack()
    k = K(nc, es)
    k.limit = cfg.get("limit", 10 ** 9)

    def din(name, shape, dt=F32):
        return Buf(nc.dram_tensor(name, list(shape), dt, kind="ExternalInput"), name)

    def dout(name, shape, dt=F32):
        return Buf(nc.dram_tensor(name, list(shape), dt, kind="ExternalOutput"), name)

    xb = din("xb", [8192, D])
    meta = din("meta", [16, D])
    x2 = din("x2", [4, 512, D])
    w1c = din("w1c", [D, 1824])
    wg = din("wg", [D, 2048])
    pv = din("pv", [128, 32])
    nwb = din("nwb", [128, D])
    lnwb = din("lnwb", [128, 2, 256])
    sublnb = din("sublnb", [128, 128])
    lamv = din("lamv", [128, 4, 64])
    w2a2 = din("w2a2", [128, 256])
    g2 = din("g2", [160, 256])
    wor = din("wor", [256, D])
    wod = din("wod", [256, D])
    wout = din("wout", [D, D])
    mw1 = din("mw1", [D, 4096])
    mw2 = din("mw2", [4096, D])
    nm2 = din("nm2", [128, 2, D])
    rs_dbg = [din(f"rs_dbg{i}", [512, 2048], BF16) for i in range(4)] if cfg.get("p2only") else None
    rope = din("rope", [128, NT, 2, 128])
    cst = din("cst", [128, 1024])
    cst2 = din("cst2", [128, 1024])
    yout = dout("yout", [4, 512, D])
    rs_in = [Buf(nc.dram_tensor(f"rs_in{i}", [2048, 2048], BF16), f"rs_in{i}") for i in range(4)]
    rs_tk = [[Tk(f"rs{i}_{r}") for r in range(16)] for i in range(4)]
    rs_out = [Buf(nc.dram_tensor(f"rs_out{i}", [512, 2048], BF16), f"rs_out{i}") for i in range(4)]
    if dbg:
        d_mix = dout("d_mix", [NT1 * 128, 512])

    psb = [Buf(es.enter_context(nc.psum_tensor(f"ps{i}", [128, 512], F32)), f"ps{i}") for i in range(8)]
    for pb_ in psb:
        pb_.tk.psum = True
    ps_rot = RotSel(k.weave, [Rot(psb[0:3]), Rot(psb[3:6]), Rot(psb[6:7]), Rot(psb[0:7])])
    ps_o = [psb[7], psb[7]]

    mhalf = k.sb([128, 2], F32, "mhalf")
    k.memset("dve", mhalf[:], -0.5)
    identb = k.sb([128, 128], BF16, "identb")
    k.dma("pool", identb[:], cst[:, 0:128], k.dsem())
    es1 = ExitStack()
    k.es = es1
    cs = k.sb([128, 384], F32, "cs")
    sem_c = k.dsem("sem_c")
    k.dma("sp", cs[:, 0:256], cst[:, 256:512], sem_c)
    k.dma("sp", cs[:, 256:384], cst[:, 896:1024], sem_c)

    cmaskb = k.sb([128, 128], BF16, "cmaskb")
    k.dma("pool", cmaskb[:], cst[:, 128:256], k.dsem())
    mkb = k.sb([128, 1024], BF16, "mkb")
    k.dma("pool", mkb[:], cst2[:, :], k.dsem())
    bones = k.sb([128, 2], BF16, "bones")
    k.cp("dve", bones[:], cs[:, 0:128:64])
    pvt = k.sb([128, 32], F32, "pvt")
    k.dma("sp", pvt[:], pv[:, :], k.dsem())
    hb = k.sb([128, 4], F32, "hb")
    k.ts("dve", hb[:], pvt[:, 9:13], 0.5, ALU.mult)
    omka = k.sb([128, 2], F32, "omka")
    k.ts("dve", omka[:], pvt[:, 15:17], -1.0, ALU.mult, 1.0, ALU.add)
    nw_t = k.sb([128, D], BF16, "nw_t")
    k.dma("pool", nw_t[:], nwb[:, :], k.dsem())
    lnw_t = k.sb([128, 2, 256], F32, "lnw_t")
    k.dma("sp", lnw_t[:], lnwb[:, :, :], k.dsem())
    subln_t = k.sb([128, 128], F32, "subln_t")
    k.dma("sp", subln_t[:], sublnb[:, :], k.dsem())
    k.ts("dve", subln_t[:], subln_t[:], 1.0 - LAMBDA_INIT, ALU.mult)
    lam_in = k.sb([128, 4, 64], F32, "lam_in")
    k.dma("sp", lam_in[:], lamv[:, :, :], k.dsem())
    lam_t = k.sb([128, 4], F32, "lam_t")
    lam_j = k.sb([128, 64], F32, "lam_j")
    for i in range(2):
        k.tt("dve", lam_j[:], lam_in[:, 2 * i, :], lam_in[:, 2 * i + 1, :], ALU.mult)
        k.red(lam_t[:, i:i + 1], lam_j[:])
    k.act(lam_t[:, 0:2], lam_t[:, 0:2], AF.Exp)
    k.tt("dve", lam_t[:, 2:3], lam_t[:, 0:1], lam_t[:, 1:2], ALU.subtract)
    k.ts("dve", lam_t[:, 3:4], lam_t[:, 2:3], LAMBDA_INIT, ALU.add, -1.0, ALU.mult)
    neglam = lam_t[:, 3:4]

    mub = k.sb([128, 9, 128], F32, "mub")
    for c in range(9):
        k.ts("pool", mub[:, c, :], cs[:, 256:384], 0.0, ALU.mult, pvt[:, c:c + 1], ALU.add)

    w1b = k.sb([128, 8, 1824], BF16, "w1b")
    esS = ExitStack()
    k.es = esS
    stg1 = Rot([(k.sb([128, 1824], F32, f"stg1_{i}"), k.dsem()) for i in range(2)])
    for kc in range(8):
        k.ldcast(w1b[:, kc, :], w1c[kc * 128:(kc + 1) * 128, :], stg1, "act" if kc % 2 == 0 else "dve")
    k.barrier()
    esS.close()
    k.es = es1
    w2p = k.sb([128, 2, 256], BF16, "w2p")
    k.memset("pool", w2p[:], 0.0)
    sem_w2 = k.dsem()
    k.dma("pool", View(w2p.t[0:64, 0, :], w2p.tk), w2a2[0:64, :], sem_w2)
    k.dma("pool", View(w2p.t[64:128, 1, :], w2p.tk), w2a2[64:128, :], sem_w2)
    g2b = k.sb([128, 2, 256], BF16, "g2b")
    k.memset("pool", g2b[:], 0.0)
    sem_g2 = k.dsem()
    k.dma("pool", g2b[:, 0, :], g2[0:128, :], sem_g2)
    k.dma("pool", View(g2b.t[0:32, 1, :], g2b.tk), g2[128:160, :], sem_g2)
    wob = k.sb([128, 4, D], BF16, "wob")
    sem_wo = k.dsem()
    for i in range(2):
        k.dma("pool", wob[:, i, :], wor[i * 128:(i + 1) * 128, :], sem_wo)
        k.dma("pool", wob[:, 2 + i, :], wod[i * 128:(i + 1) * 128, :], sem_wo)

    KT = k.sb([128, 2, LP], BF16, "KT")
    KTr = [KT.alias(f"KT{j}") for j in range(NT)]
    VA = k.sb([128, NT, 2, 130], BF16, "VA")
    VAr = [VA.alias(f"VA{j}") for j in range(NT)]
    k.op("pool", lambda e: e.memset(VA.t[:, :, :, 128:130], 1.0), [], [VA.tk] + [r.tk for r in VAr])
    k.op("pool", lambda e: e.memset(VA.t[0:112, 0, :, 128:130], 0.0), [], [VA.tk, VAr[0].tk])
    raw = k.sb([128, 9, 129], F32, "raw")
    k.memset("pool", raw[:], 0.0)
    H32 = k.sb([128, 2, 64], F32, "H32")
    k.memset("dve", H32[:], 0.0)
    Hb = k.sb([128, 2, 64], BF16, "Hb")
    k.memset("dve", Hb[:], 0.0)

    def rot(n, shape, dt, name):
        return Rot([k.sb(shape, dt, f"{name}{i}") for i in range(n)])

    xt_r = Rot([(k.sb([128, D], F32, f"xt{i}"), k.dsem(f"sem_xt{i}")) for i in range(1)])
    rp_r = Rot([(k.sb([128, 2, 128], F32, f"rp{i}"), k.dsem(f"sem_rp{i}")) for i in range(2)])
    junk = k.sb([128, 128], F32, "junk")
    st_r = rot(2, [128, 8], F32, "st")
    xs_r = rot(1, [128, D], BF16, "xs")
    hT_r = rot(1, [128, 8, 128], BF16, "hT")
    sh_r = rot(2, [128, 9, 128], F32, "sh")
    qkraw_r = rot(1, [128, 4, 128], F32, "qkraw")
    QT_r = rot(2, [128, 2, 2, 128], BF16, "QT")
    for qb in QT_r.bufs:
        k.memset("pool", qb[:], 0.0)
    ropt = k.sb([128, 4, 128], F32, "ropt")
    PT_r = rot(3, [128, 4, 128], BF16, "PT")
    on_r = rot(1, [128, 4, 128], F32, "on")
    rcp_r = rot(2, [128, 4], F32, "rcp")
    dsc = k.sb([128, 256], F32, "dsc")
    mix_r = rot(2, [128, 512], BF16, "mix")
    mixT_r = rot(2, [128, 4, 128], BF16, "mixT")
    po_r = Rot([(k.sb([128, 2048], BF16, f"po{i}"), k.dsem(f"sem_po{i}")) for i in range(1)])
    if dbg:
        dmix_r = Rot([(k.sb([128, 512], F32, f"dmix{i}"), k.dsem(f"sem_dmix{i}")) for i in range(1)])

    def f32t(name, shape=(128, 2, 128)):
        return k.sb(list(shape), F32, name)

    lorab = k.sb([128, 128], BF16, "lorab")
    sgd = k.sb([128, 2, 128], BF16, "sgd")
    k.memset("pool", sgd[:], 0.0)
    sw = f32t("sw")
    cum = f32t("cum")
    aa = f32t("aa")
    Ew = f32t("Ew")
    Ewx = f32t("Ewx")
    Einv = f32t("Einv")
    Eend = f32t("Eend")
    bC = k.sb([128, 2, 2], F32, "bC")
    WC = k.sb([128, 2, 2], F32, "WC")
    kk = f32t("kk")
    kq = f32t("kq")
    rn = f32t("rn")
    k2 = f32t("k2")
    bvec = f32t("bvec")
    rk = k.sb([128, 2, 128], BF16, "rk")
    BK = k.sb([128, 2, 2, 128], BF16, "BK")
    RT = k.sb([128, 2, 128], BF16, "RT")
    ARbd = k.sb([128, 2, 2, 2, 128], BF16, "ARbd")
    Bbd = k.sb([128, 2, 2, 128], BF16, "Bbd")
    TMbd = k.sb([128, 2, 2, 2, 128], BF16, "TMbd")
    PTbd = k.sb([128, 2, 2, 128], BF16, "PTbd")
    Hbd = k.sb([128, 2, 2, 64], BF16, "Hbd")
    for zb in (ARbd, Bbd, TMbd, PTbd, Hbd):
        k.memset("pool", zb[:], 0.0)
    FT = k.sb([128, 4, 2, 128], BF16, "FT")
    TM = k.sb([128, 4, 2, 128], BF16, "TM")
    S4 = k.sb([128, 4, 4, 128], BF16, "S4")
    ULr = rot(2, [128, 4, 2, 128], BF16, "UL")
    Yr = rot(2, [128, 4, 128], BF16, "Y")
    GTs = k.sb([128, 2, 2, 64], BF16, "GTs")
    yv = k.sb([128, 4, 64], F32, "yv")
    gst = k.sb([128, 16], F32, "gst")
    bsum = k.sb([128, 4], F32, "bsum")
    gtm = k.sb([128, 256], F32, "gtm")
    ytmp = k.sb([128, 4, 64], F32, "ytmp")

    heads = [(c2, e) for c2 in range(2) for e in range(2)]

    def p1_tile(j):
        xt, sx = xt_r.next()
        if j == 0:
            k.memset("pool", xt[:], 0.0)
            k.dma("sp", View(xt.t[112:128, :], xt.tk), meta[:, :], sx)
        else:
            k.dma("sp", xt[:], xb[(j - 1) * 128:j * 128, :], sx)
        rp, srp = rp_r.next()
        k.dma("sp", rp[:], rope[:, j, :, :], srp)
        st = st_r.next()
        xs = xs_r.next()
        k.act(xs[:], xt[:], AF.Square, accum=st[:, 0:1])
        k.ts("pool", st[:, 1:2], st[:, 0:1], 1.0 / D, ALU.mult, 1e-5, ALU.add)
        k.tt("pool", st[:, 2:3], st[:, 1:2], mhalf[:, 0:1], ALU.pow)
        k.stt(xs[:], xt[:], st[:, 2:3], nw_t[:], ALU.mult, ALU.mult)
        ps = ps_rot.next()
        psb16 = ps.same(BF16)
        k.pe([("T", psb16[:, kc * 128:(kc + 1) * 128], xs[:, kc * 128:(kc + 1) * 128], identb[:]) for kc in range(8)])
        hT = hT_r.next()
        k.cp("act", hT[:], View(psb16.t.ap().bitcast(BF16)[:, 0:1024].rearrange("p (a b) -> p a b", a=8), ps.tk))
        def proj(ps, slot, col0, width=128, rows=128):
            return [(View(ps.t[0:rows, slot * 128:slot * 128 + 128], ps.tk) if rows != 128 else ps[:, slot * 128:slot * 128 + 128],
                     w1b[:, kc, col0:col0 + width], hT[:, kc, :], kc == 0, kc == 7) for kc in range(8)]
        psA = ps_rot.next()
        items = []
        for s in range(4):
            items += proj(psA, s, s * 128)
        k.pe(items)
        k.cp("dve", raw[:, 0:4, 1:129], View(psA.t[:, :].rearrange("p (a b) -> p a b", a=4), psA.tk))
        psB = ps_rot.next()
        items = []
        for s in range(4):
            items += proj(psB, s, 512 + s * 128)
        k.pe(items)
        k.cp("dve", raw[:, 4:8, 1:129], View(psB.t[:, :].rearrange("p (a b) -> p a b", a=4), psB.tk))
        psC = ps_rot.next()
        items = []
        for s in range(4):
            items += proj(psC, s, 1056 + s * 128)
        k.pe(items)
        qkraw = qkraw_r.next()
        k.cp("dve", qkraw[:], View(psC.t[:, :].rearrange("p (a b) -> p a b", a=4), psC.tk))
        psD = ps_rot.next()
        items = proj(psD, 0, 1024, width=32, rows=32)
        items += [(psD[:, 128:384], hT[:, kc, :], w1b[:, kc, 1568:1824], kc == 0, kc == 7) for kc in range(8)]
        k.pe(items)
        k.cp("act", View(raw.t[0:32, 8, 1:129], raw.tk), View(psD.t[0:32, 0:128], psD.tk))
        k.cp(cfg.get("vaeng", "act"), View(VA.t[:, j, :, 0:128], VAr[j].tk), View(psD.t[:, 128:384].rearrange("p (a b) -> p a b", a=2), psD.tk))
        if j == 0:
            k.memset("pool", View(VA.t[0:112, 0, :, 0:128], VAr[0].tk), 0.0)
        sh = sh_r.next()
        k.tt("pool", sh[:], raw[:, :, 0:128], raw[:, :, 1:129], ALU.subtract)
        k.tt("pool", sh[:], sh[:], mub[:], ALU.mult)
        k.tt("pool", sh[:], sh[:], raw[:, :, 1:129], ALU.add)
        k.cp("pool", raw[:, :, 0:1], raw[:, :, 128:129])
        psR = ps_rot.next()
        k.pe([(psR[:, s * 128:(s + 1) * 128], cs[:, 128:256], qkraw[:, s, :], True, True) for s in range(4)])
        for s in range(4):
            k.tt("dve", ropt[:, s, :], psR[:, s * 128:(s + 1) * 128], rp[:, 1, :], ALU.mult)
            k.tt("pool", qkraw[:, s, :], qkraw[:, s, :], rp[:, 0, :], ALU.mult)
        QT = QT_r.next()
        for e in range(2):
            pb = 64 * e
            k.tt("pool", View(QT.t[pb:pb + 64, :, e, :], QT.tk), View(qkraw.t[pb:pb + 64, 0:2, :], qkraw.tk),
                 View(ropt.t[pb:pb + 64, 0:2, :], ropt.tk), ALU.add)
        k.tt("pool", View(KT.t[:, :, j * 128:(j + 1) * 128], KTr[j].tk), qkraw[:, 2:4, :], ropt[:, 2:4, :], ALU.add)
        return sh, QT

    def attn_tile(j, QT, mix):
        on = on_r.next()
        rcp = rcp_r.next()
        groups = [(h, i0) for h in range(4) for i0 in range(0, j + 1, 4)]
        pss = {}

        def qk(g):
            h, i0 = groups[g]
            dh, e = h // 2, h % 2
            n = min(4, j + 1 - i0)
            ps = ps_rot.next()
            k.pe([(ps[:, s * 128:(s + 1) * 128],
                   View(KT.t[:, dh, (i0 + s) * 128:(i0 + s + 1) * 128], KTr[i0 + s].tk),
                   QT[:, dh, e, :], True, True) for s in range(n)])
            pss[g] = ps

        for g in range(min(2, len(groups))):
            qk(g)
        for g, (h, i0) in enumerate(groups):
            dh, e = h // 2, h % 2
            po = ps_o[h % 2]
            n = min(4, j + 1 - i0)
            ps = pss.pop(g)
            PT = PT_r.next()
            k.act(View(PT.t[:, 0:n, :], PT.tk), View(ps.t[:, 0:n * 128].rearrange("p (a b) -> p a b", a=n), ps.tk), AF.Exp, scale=0.125)
            if i0 + n - 1 == j:
                k.tt("pool", PT[:, n - 1, :], PT[:, n - 1, :], cmaskb[:], ALU.mult)
            if g + 2 < len(groups):
                qk(g + 2)
            k.pe([(po[:, 0:130], PT[:, s, :], View(VA.t[:, i0 + s, dh, :], VAr[i0 + s].tk), (i0 + s) == 0, (i0 + s) == j)
                  for s in range(n)])
            if i0 + n - 1 == j:
                k.recip(rcp[:, h:h + 1], po[:, 128:129])
                k.ts("dve", on[:, h, :], po[:, 0:128], rcp[:, h:h + 1], ALU.mult)
        for dh in range(2):
            o = dsc[:, dh * 128:(dh + 1) * 128]
            k.stt(o, on[:, 2 * dh + 1, :], neglam, on[:, 2 * dh, :], ALU.mult, ALU.add)
            k.act(junk[:], o, AF.Square, accum=rcp[:, dh:dh + 1])
        k.ts("pool", rcp[:, 0:2], rcp[:, 0:2], 1.0 / 128, ALU.mult, 1e-5, ALU.add)
        k.tt("pool", rcp[:, 2:4], rcp[:, 0:2], mhalf[:, 0:2], ALU.pow)
        for dh in range(2):
            o = dsc[:, dh * 128:(dh + 1) * 128]
            k.stt(mix[:, 256 + dh * 128:256 + (dh + 1) * 128], o, rcp[:, 2 + dh:3 + dh], subln_t[:], ALU.mult, ALU.mult)

    def rwkv_tile(j, sh, mix):
        r = sh[:, 0:2, :]
        kx = sh[:, 2:4, :]
        v = sh[:, 4:6, :]
        k.act(View(lorab.t[0:64, :], lorab.tk), View(sh.t[0:64, 6, :], sh.tk), AF.Tanh)
        k.cp("act", View(lorab.t[64:128, :], lorab.tk), View(sh.t[64:128, 6, :], sh.tk))
        k.act(sgd[:, 0, :], sh[:, 7, :], AF.Tanh, scale=0.5)
        k.act(View(sgd.t[0:32, 1, :], sgd.tk), View(sh.t[0:32, 8, :], sh.tk), AF.Tanh, scale=0.5)
        k.ts("pool", sgd[:, 0, :], sgd[:, 0, :], 0.5, ALU.mult, 0.5, ALU.add)
        k.ts("pool", View(sgd.t[0:32, 1, :], sgd.tk), View(sgd.t[0:32, 1, :], sgd.tk), 0.5, ALU.mult, 0.5, ALU.add)
        psZ = ps_rot.next()
        items = []
        for c2 in range(2):
            items.append((psZ[:, c2 * 128:(c2 + 1) * 128], w2p[:, 0, c2 * 128:(c2 + 1) * 128], lorab[:], True, True))
            items.append((psZ[:, 256 + c2 * 128:256 + (c2 + 1) * 128], w2p[:, 1, c2 * 128:(c2 + 1) * 128], lorab[:], True, True))
        k.pe(items)
        for c2 in range(2):
            k.act(sw[:, c2, :], psZ[:, c2 * 128:(c2 + 1) * 128], AF.Tanh, bias=hb[:, c2:c2 + 1], scale=0.5)
            k.act(aa[:, c2, :], psZ[:, 256 + c2 * 128:256 + (c2 + 1) * 128], AF.Tanh, bias=hb[:, 2 + c2:3 + c2], scale=0.5)
        k.ts("dve", sw[:], sw[:], 0.5, ALU.mult, 0.5, ALU.add)
        k.ts("pool", aa[:], aa[:], 0.5, ALU.mult, 0.5, ALU.add)
        psG = ps_rot.next()
        k.pe([(psG[:, 0:256], sgd[:, 0, :], g2b[:, 0, :], True, False),
              (psG[:, 0:256], sgd[:, 1, :], g2b[:, 1, :], False, True)])
        k.cp("act", gtm[:], psG[:, 0:256])
        for c2 in range(2):
            k.op("dve", lambda e, c2=c2: e.tensor_tensor_scan(out=cum.t[:, c2, :], data0=cs.t[:, 256:384], data1=sw.t[:, c2, :],
                                                           initial=0.0, op0=ALU.mult, op1=ALU.add), [cs.tk, sw.tk], [cum.tk])
        k.tt("pool", Ewx[:], cum[:], sw[:], ALU.subtract)
        k.act(Ew[:], cum[:], AF.Exp, scale=-C0)
        k.act(Ewx[:], Ewx[:], AF.Exp, scale=-C0)
        k.act(Einv[:], cum[:], AF.Exp, scale=C0)
        k.ts("dve", bC[:], View(cum.t[:, :, 63:128:64], cum.tk), -C0, ALU.mult)
        k.act(WC[:], bC[:], AF.Exp)
        for c2 in range(2):
            for ch in range(2):
                k.act(Eend[:, c2, ch * 64:(ch + 1) * 64], cum[:, c2, ch * 64:(ch + 1) * 64], AF.Exp, scale=C0, bias=bC[:, c2, ch:ch + 1])
        for c2 in range(2):
            k.act(kk[:, c2, :], sh[:, 2 + c2, :], AF.Copy, scale=pvt[:, 13 + c2:14 + c2])
        k.tt("pool", kq[:], kk[:], kk[:], ALU.mult)
        psN = ps_rot.next()
        k.pe([(psN[:, c2 * 128:(c2 + 1) * 128], cs[:, 0:128], kq[:, c2, :], True, True) for c2 in range(2)])
        k.act(rn[:], View(psN.t[:, 0:256].rearrange("p (a b) -> p a b", a=2), psN.tk), AF.Sqrt)
        k.ts("dve", rn[:], rn[:], 1e-12, ALU.max)
        k.recip(rn[:], rn[:])
        k.tt("pool", kk[:], kk[:], rn[:], ALU.mult)
        for c2 in range(2):
            k.ts("dve", k2[:, c2, :], aa[:, c2, :], pvt[:, 15 + c2:16 + c2], ALU.mult, omka[:, c2:c2 + 1], ALU.add)
        k.tt("pool", k2[:], kx, k2[:], ALU.mult)
        k.tt("pool", bvec[:], kk[:], aa[:], ALU.mult)

        def c4(buf):
            return buf[:, :, :]

        k.stt(FT[:, 2, :, :], kk[:], -1.0, Ewx[:], ALU.mult, ALU.mult)
        k.tt("dve", RT[:], r, Ew[:], ALU.mult)
        k.tt("pool", BK[:, :, 0, :], bvec[:], Einv[:], ALU.mult)
        k.tt("dve", BK[:, :, 1, :], k2[:], Einv[:], ALU.mult)
        k.tt("pool", FT[:, 0, :, :], bvec[:], Eend[:], ALU.mult)
        k.tt("dve", FT[:, 1, :, :], k2[:], Eend[:], ALU.mult)
        k.cp("act", FT[:, 3, :, :], v)
        for e in range(2):
            pb = 64 * e
            k.cp("act", View(ARbd.t[pb:pb + 64, :, e, 0, :], ARbd.tk), View(FT.t[pb:pb + 64, 2, :, :], FT.tk))
            k.cp("act", View(ARbd.t[pb:pb + 64, :, e, 1, :], ARbd.tk), View(RT.t[pb:pb + 64, :, :], RT.tk))
            k.cp("act", View(Bbd.t[pb:pb + 64, :, e, :], Bbd.tk), View(BK.t[pb:pb + 64, :, 0, :], BK.tk))
        k.tt("pool", kq[:], r, k2[:], ALU.mult)
        for c2 in range(2):
            k.act(rk[:, c2, :], kq[:, c2, :], AF.Copy, scale=pvt[:, 17 + c2:18 + c2])
        psT = ps_rot.next()
        psT16 = psT.same(BF16)
        k.pe([("T", psT16[:, (kind * 2 + c2) * 128:(kind * 2 + c2 + 1) * 128], FT[:, kind, c2, :], identb[:])
              for kind in range(4) for c2 in range(2)])
        psTv = psT.t.ap().bitcast(BF16)[:, 0:1024].rearrange("p (a b c) -> p a b c", a=4, b=2)
        k.cp("act", TM[:], View(psTv, psT.tk))
        for ch in range(2):
            pt = 64 * ch
            k.cp("act", View(TMbd.t[pt:pt + 64, :, :, ch, :], TMbd.tk), View(psTv[pt:pt + 64, 0:2, :, :], psT.tk))
        UL = ULr.next()
        mk1 = View(mkb.t[:, 0:512].rearrange("p (a b c) -> p a b c", a=2, b=2), mkb.tk)
        for c2 in range(2):
            arv = View(ARbd.t[:, c2, :, :, :].rearrange("p a b c -> p (a b c)"), ARbd.tk)
            for kind in range(2):
                psg = ps_rot.next()
                k.pe([(psg[:, :], BK[:, c2, kind, :], arv, True, True)])
                k.tt("dve", S4[:, 2 * c2:2 * c2 + 2, 2 * kind:2 * kind + 2, :],
                     View(psg.t[:, :].rearrange("p (a b c) -> p a b c", a=2, b=2), psg.tk), mk1, ALU.mult)
        psLo = ps_rot.next()
        k.pe([(psLo[:, c2 * 256:(c2 + 1) * 256], FT[:, 2, c2, :], View(Bbd.t[:, c2, :, :].rearrange("p a b -> p (a b)"), Bbd.tk), True, True)
              for c2 in range(2)])
        k.tt("dve", UL[:, :, 1, :], View(psLo.t[:, :].rearrange("p (h q) -> p h q", h=4), psLo.tk),
             View(mkb.t[:, 512:1024].rearrange("p (h q) -> p h q", h=4), mkb.tk), ALU.mult)
        k.cp("act", UL[:, :, 0, :], S4[:, :, 0, :])
        Y = Yr.next()
        psY = ps_rot.next()
        k.pe([(psY[:, hh * 64:(hh + 1) * 64], S4[:, hh, 2, :], TM[:, 3, c2, 64 * e:64 * e + 64], True, True)
              for hh, (c2, e) in enumerate(heads)])
        k.cp("act", Y[:, :, 0:64], View(TM.t[:, 2, :, :].rearrange("p a (e q) -> p (a e) q", e=2), TM.tk))
        k.cp("dve", Y[:, :, 64:128], View(psY.t[:, 0:256].rearrange("p (h q) -> p h q", h=4), psY.tk))
        for lvl in range(6):
            psY = ps_rot.next()
            k.pe([(psY[:, hh * 128:(hh + 1) * 128], UL[:, hh, 0, :], Y[:, hh, :], True, True) for hh in range(4)])
            if lvl < 5:
                psU = [ps_rot.next(), ps_rot.next()]
                items = []
                for hh in range(4):
                    pu = psU[hh // 2]
                    o0 = (hh % 2) * 256
                    items.append((pu[:, o0:o0 + 128], UL[:, hh, 1, :], UL[:, hh, 0, :], True, True))
                    if lvl < 4:
                        items.append((pu[:, o0 + 128:o0 + 256], UL[:, hh, 0, :], UL[:, hh, 1, :], True, True))
                k.pe(items)
            Yn = Yr.next()
            k.tt("dve", Yn[:], View(psY.t[:, :].rearrange("p (h q) -> p h q", h=4), psY.tk), Y[:], ALU.add)
            Y = Yn
            if lvl < 5:
                UL = ULr.next()
                for half in range(2):
                    k.cp("act", UL[:, 2 * half:2 * half + 2, :, :], View(psU[half].t[:, :].rearrange("p (h a q) -> p h a q", h=2, a=2), psU[half].tk))
        psP = ps_rot.next()
        items = []
        for hh, (c2, e) in enumerate(heads):
            pbk = 64 * e
            m1 = Y[:, hh, 0:64]
            items.append((View(psP.t[pbk:pbk + 64, c2 * 256:c2 * 256 + 128], psP.tk), m1,
                          View(TMbd.t[:, 0, c2, :, pbk:pbk + 64], TMbd.tk), True, True))
            items.append((View(psP.t[pbk:pbk + 64, c2 * 256 + 128:c2 * 256 + 256], psP.tk), m1, S4[:, hh, 1, :], True, True))
        k.pe(items)
        ppv = psP.t[:, :].rearrange("p (a x) -> p a x", a=2)
        for e in range(2):
            pbk = 64 * e
            k.cp("dve", View(PTbd.t[pbk:pbk + 64, :, :, pbk:pbk + 64], PTbd.tk),
                 View(ppv[pbk:pbk + 64, :, 0:128].rearrange("p a (c q) -> p a c q", c=2), psP.tk))
        k.tt("dve", GTs[:], View(ppv[:, :, 128:256].rearrange("p a (c q) -> p a c q", c=2), psP.tk),
             View(RT.t[:, :, :].rearrange("p a (c q) -> p a c q", c=2), RT.tk), ALU.add)
        psBn = ps_rot.next()
        k.pe([(psBn[:, 2 * c2:2 * c2 + 2], rk[:, c2, :], bones[:], True, True) for c2 in range(2)])
        k.cp("act", bsum[:], psBn[:, 0:4])
        psYo = ps_rot.next()
        for ch in range(2):
            pt = 64 * ch
            psH = ps_rot.next()
            items = []
            for c2 in range(2):
                items.append((View(psYo.t[pt:pt + 64, c2 * 128:(c2 + 1) * 128], psYo.tk), GTs[:, c2, ch, :],
                              View(Hbd.t[:, c2, :, :].rearrange("p a b -> p (a b)"), Hbd.tk), True, False))
                for e in range(2):
                    hh = 2 * c2 + e
                    yo = View(psYo.t[pt:pt + 64, hh * 64:(hh + 1) * 64], psYo.tk)
                    items.append((yo, S4[:, hh, 1, pt:pt + 64], Y[:, hh, 64:128], False, False))
                    items.append((yo, S4[:, hh, 3, pt:pt + 64], TM[:, 3, c2, 64 * e:64 * e + 64], False, e == 1))
            for c2 in range(2):
                items.append((psH[:, c2 * 64:(c2 + 1) * 64], PTbd[:, c2, ch, :], Hb[:, c2, :], True, False))
                for e in range(2):
                    hh = 2 * c2 + e
                    pbk = 64 * e
                    ho = View(psH.t[pbk:pbk + 64, c2 * 64:(c2 + 1) * 64], psH.tk)
                    items.append((ho, TMbd[:, 0, c2, ch, pbk:pbk + 64], Y[:, hh, 64:128], False, False))
                    items.append((ho, TMbd[:, 1, c2, ch, pbk:pbk + 64], TM[:, 3, c2, pbk:pbk + 64], False, True))
            k.pe(items)
            for c2 in range(2):
                k.stt(H32[:, c2, :], H32[:, c2, :], WC[:, c2, ch:ch + 1], psH[:, c2 * 64:(c2 + 1) * 64], ALU.mult, ALU.add)
            k.cp("act", Hb[:], H32[:])
            for e in range(2):
                pbk = 64 * e
                k.cp("act", View(Hbd.t[pbk:pbk + 64, :, e, :], Hbd.tk), View(H32.t[pbk:pbk + 64, :, :], H32.tk))
        if j == 0:
            return
        k.cp("act", yv[:], View(psYo.t[:, 0:256].rearrange("p (h q) -> p h q", h=4), psYo.tk))
        k.red(gst[:, 0:4], yv[:])
        k.tt("pool", ytmp[:], yv[:], yv[:], ALU.mult)
        k.red(gst[:, 4:8], ytmp[:])
        k.ts("dve", gst[:, 0:8], gst[:, 0:8], 1.0 / 64, ALU.mult)
        k.tt("dve", gst[:, 8:12], gst[:, 0:4], gst[:, 0:4], ALU.mult)
        k.tt("dve", gst[:, 8:12], gst[:, 4:8], gst[:, 8:12], ALU.subtract)
        k.act(gst[:, 8:12], gst[:, 8:12], AF.Sqrt, bias=64e-5)
        k.recip(gst[:, 12:16], gst[:, 8:12])
        for hh in range(4):
            k.ts("dve", ytmp[:, hh, :], yv[:, hh, :], gst[:, hh:hh + 1], ALU.subtract, gst[:, 12 + hh:13 + hh], ALU.mult)
        yt2 = View(ytmp.t[:, :, :].rearrange("p h q -> p (h q)"), ytmp.tk)
        k.tt("pool", yt2, yt2, lnw_t[:, 0, :], ALU.mult)
        k.tt("pool", yt2, yt2, lnw_t[:, 1, :], ALU.add)
        for hh, (c2, e) in enumerate(heads):
            k.stt(ytmp[:, hh, :], TM[:, 3, c2, 64 * e:64 * e + 64], bsum[:, hh:hh + 1], ytmp[:, hh, :], ALU.mult, ALU.add)
        k.tt("dve", mix[:, 0:256], yt2, gtm[:], ALU.mult)

    def outproj_tile(j, mix):
        psT = ps_rot.next()
        psT16 = psT.same(BF16)
        k.pe([("T", psT16[:, c * 128:(c + 1) * 128], mix[:, c * 128:(c + 1) * 128], identb[:]) for c in range(4)])
        mixT = mixT_r.next()
        k.cp("act", mixT[:], View(psT.t.ap().bitcast(BF16)[:, 0:512].rearrange("p (a b) -> p a b", a=4), psT.tk))
        po, spo = po_r.next()
        for br in range(2):
            for half in range(2):
                ps = ps_rot.next()
                k.pe([(ps[:, :], mixT[:, 2 * br + kc, :], wob[:, 2 * br + kc, half * 512:(half + 1) * 512], kc == 0, kc == 1) for kc in range(2)])
                k.cp("act" if half == 0 else "dve", po[:, br * 1024 + half * 512:br * 1024 + (half + 1) * 512], ps[:, :])
        i = j - 1
        k.dma("sp", View(rs_in[i // 16].t[(i % 16) * 128:(i % 16 + 1) * 128, :], rs_tk[i // 16][i % 16]), po[:], spo)
        if dbg:
            dm, sdm = dmix_r.next()
            k.cp("pool", dm[:], mix[:])
            k.dma("sp", d_mix[j * 128:(j + 1) * 128, :], dm[:], sdm)

    cc_sem = es.enter_context(nc.semaphore("cc_sem"))
    k.engs["cc"] = EngS("cc", None, cc_sem)
    n_cc = 0
    def issue_rs(j):
        nonlocal n_cc
        if do_rs and j >= 1 and j % 16 == 0:
            q = j // 16 - 1
            E = k.engs["pool"]
            k._wait(E, rs_tk[q], [rs_out[q].tk])
            nc.gpsimd.collective_compute("ReduceScatter", ALU.add, replica_groups=[[0, 1, 2, 3], [4, 5, 6, 7]],
                                         ins=[rs_in[q].t.ap().opt()], outs=[rs_out[q].t.ap().opt()]).then_inc(cc_sem)
            n_cc += 1
            for t_ in rs_tk[q]:
                t_.rd["cc"] = n_cc
            rs_out[q].tk.lw = ("cc", n_cc)
            rs_out[q].tk.rd = {}

    tiles = [{"j": j} for j in range(NT1)]

    def P(t):
        t["sh"], t["QT"] = p1_tile(t["j"])

    def R(t):
        t["mix"] = mix_r.next()
        rwkv_tile(t["j"], t["sh"], t["mix"])

    def Y(t):
        attn_tile(t["j"], t["QT"], t["mix"])
        outproj_tile(t["j"], t["mix"])

    lenP, lenR = 40, 230
    if NT1 > 0:
        n0 = k.nops
        P(tiles[0])
        lenP = k.nops - n0
    for j in range(NT1):
        fns, ests = [], []
        fns.append(lambda t=tiles[j]: R(t))
        ests.append(lenR)
        if j >= 2:
            jj = j - 1
            fns.append(lambda t=tiles[jj]: Y(t))
            ests.append(4 * (3 * ((jj + 4) // 4) + 2) + 40)
        else:
            fns.append(lambda: None)
            ests.append(1)
        if j + 1 < NT1:
            fns.append(lambda t=tiles[j + 1]: P(t))
            ests.append(lenP)
        done = k.weave.run(fns, ests)
        lenR = max(1, done[0])
        if j >= 2:
            issue_rs(j - 1)
    if NT1 >= 2:
        Y(tiles[NT1 - 1])
        issue_rs(NT1 - 1)

    k.barrier()
    es1.close()
    if do_p2:
        src = rs_dbg if rs_dbg is not None else rs_out
        esA = ExitStack()
        k.es = es
        h2 = k.sb([128, 16, D], F32, "h2")
        k.es = esA
        wgb = k.sb([128, 8, 2048], BF16, "wgb")
        woutb = k.sb([128, 8, D], BF16, "woutb")
        stgA = Rot([(k.sb([128, D], F32, f"stgA{i}"), k.dsem()) for i in range(4)])
        nw2 = k.sb([128, D], F32, "nw2")
        k.dma("sp", nw2[:], nwb[:, :], k.dsem())
        x2_r = Rot([(k.sb([128, 4, D], F32, f"x2t{i}"), k.dsem()) for i in range(1)])
        rs_r = Rot([(k.sb([128, 4, 2048], BF16, f"rst{i}"), k.dsem()) for i in range(1)])
        pre2a = {}
        x2t, sx2 = x2_r.next()
        k.dma("sp", x2t[:], View(x2.t[0].rearrange("(s p) d -> p s d", p=128), x2.tk), sx2)
        rst, srs = rs_r.next()
        k.dma("sp", rst[:], View(src[0].t.ap().rearrange("(s p) d -> p s d", p=128), src[0].tk), srs)
        pre2a[0] = (x2t, rst)
        for kc in range(8):
            for hf in range(2):
                k.ldcast(wgb[:, kc, hf * D:(hf + 1) * D], wg[kc * 128:(kc + 1) * 128, hf * D:(hf + 1) * D], stgA, "act" if hf == 0 else "dve")
        for kc in range(8):
            k.ldcast(woutb[:, kc, :], wout[kc * 128:(kc + 1) * 128, :], stgA, "act" if kc % 2 == 0 else "dve")
        st2 = k.sb([128, 8], F32, "st2")
        xs2 = k.sb([128, D], BF16, "xs2")
        hT2 = k.sb([128, 8, 512], BF16, "hT2")
        gate = k.sb([128, 2048], BF16, "gate")
        mrg = k.sb([128, D], BF16, "mrg")
        mrg2 = k.sb([128, D], BF16, "mrg2")
        mT = k.sb([128, 8, 128], BF16, "mT")
        gateB = k.sb([128, 2048], BF16, "gateB")
        mrgB = k.sb([128, D], BF16, "mrgB")
        mrg2B = k.sb([128, D], BF16, "mrg2B")
        mTB = k.sb([128, 8, 128], BF16, "mTB")

        def norm_T(xin, nwt, xs_, st_, dstT, col0):
            k.act(xs_[:], xin, AF.Square, accum=st_[:, 0:1])
            k.act(st_[:, 1:2], st_[:, 0:1], AF.Sqrt, bias=1e-5, scale=1.0 / D)
            k.recip(st_[:, 2:3], st_[:, 1:2])
            k.stt(xs_[:], xin, st_[:, 2:3], nwt, ALU.mult, ALU.mult)
            ps = ps_rot.next()
            ps16 = ps.same(BF16)
            k.pe([("T", ps16[:, kc * 128:(kc + 1) * 128], xs_[:, kc * 128:(kc + 1) * 128], identb[:]) for kc in range(8)])
            k.cp("act", View(dstT.t[:, :, col0:col0 + 128], dstT.tk),
                 View(ps.t.ap().bitcast(BF16)[:, 0:1024].rearrange("p (a b) -> p a b", a=8), ps.tk))

        for kq in range(4):
            if kq in pre2a:
                x2t, rst = pre2a[kq]
            else:
                x2t, sx2 = x2_r.next()
                k.dma("sp", x2t[:], View(x2.t[kq].rearrange("(s p) d -> p s d", p=128), x2.tk), sx2)
                rst, srs = rs_r.next()
                k.dma("sp", rst[:], View(src[kq].t.ap().rearrange("(s p) d -> p s d", p=128), src[kq].tk), srs)
            for s_ in range(4):
                norm_T(x2t[:, s_, :], nw2[:], xs2, st2, hT2, s_ * 128)
            def sub2a(s_, gate, mrg, mrg2, mT, kq=kq, x2t=x2t, rst=rst):
                for cb in range(4):
                    ps = ps_rot.next()
                    k.pe([(ps[:, :], hT2[:, kc, s_ * 128:(s_ + 1) * 128], wgb[:, kc, cb * 512:(cb + 1) * 512], kc == 0, kc == 7) for kc in range(8)])
                    k.act(gate[:, cb * 512:(cb + 1) * 512], ps[:, :], AF.Sigmoid)
                k.tt("pool", mrg[:], gate[:, 0:D], rst[:, s_, 0:D], ALU.mult)
                k.tt("dve", mrg2[:], gate[:, D:2 * D], rst[:, s_, D:2 * D], ALU.mult)
                k.tt("pool", mrg[:], mrg[:], mrg2[:], ALU.add)
                ps = ps_rot.next()
                ps16 = ps.same(BF16)
                k.pe([("T", ps16[:, kc * 128:(kc + 1) * 128], mrg[:, kc * 128:(kc + 1) * 128], identb[:]) for kc in range(8)])
                k.cp("act", mT[:], View(ps.t.ap().bitcast(BF16)[:, 0:1024].rearrange("p (a b) -> p a b", a=8), ps.tk))
                for cb in range(2):
                    ps = ps_rot.next()
                    k.pe([(ps[:, :], mT[:, kc, :], woutb[:, kc, cb * 512:(cb + 1) * 512], kc == 0, kc == 7) for kc in range(8)])
                    k.tt("dve", h2[:, kq * 4 + s_, cb * 512:(cb + 1) * 512], ps[:, :], x2t[:, s_, cb * 512:(cb + 1) * 512], ALU.add)

            for s0 in (0, 2):
                k.weave.run([lambda: sub2a(s0, gate, mrg, mrg2, mT), lambda: sub2a(s0 + 1, gateB, mrgB, mrg2B, mTB)], [1, 1])
        k.barrier()
        esA.close()
        esB = ExitStack()
        k.es = esB
        nmt = k.sb([128, 2, D], F32, "nmt")
        k.dma("sp", nmt[:], nm2[:, :, :], k.dsem())
        hnT = k.sb([128, 8, 2048], BF16, "hnT")
        xs3 = k.sb([128, D], BF16, "xs3")
        st3 = k.sb([128, 8], F32, "st3")
        esC = ExitStack()
        k.es = esC
        wq_r = Rot([(k.sb([128, 8, D], BF16, f"w1q{i}"), k.sb([128, 8, D], BF16, f"w2q{i}")) for i in range(2)])
        aT_r = rot(2, [128, 8, 512], BF16, "aT")
        rl_r = rot(2, [128, 512], BF16, "rl")
        stgB = Rot([(k.sb([128, D], F32, f"stgB{i}"), k.dsem()) for i in range(4)])

        def load_q(p):
            w1q, w2q = wq_r.next()
            for kc in range(8):
                k.ldcast(w1q[:, kc, :], mw1[kc * 128:(kc + 1) * 128, p * D:(p + 1) * D], stgB, "pool" if p > 0 else ("act", "dve")[kc % 2])
            for fc in range(8):
                k.ldcast(w2q[:, fc, :], mw2[p * D + fc * 128:p * D + (fc + 1) * 128, :], stgB, "pool" if p > 0 else ("act", "dve")[fc % 2])
            return w1q, w2q

        nxt = load_q(0)
        for t16 in range(16):
            norm_T(h2[:, t16, :], nmt[:, 0, :], xs3, st3, hnT, t16 * 128)
        for p in range(4):
            w1q, w2q = nxt
            if p < 3:
                nxt = load_q(p + 1)
            def up(kq, aT, w1q=w1q):
                for fc in range(8):
                    ps = ps_rot.next()
                    k.pe([(ps[:, :], w1q[:, kc, fc * 128:(fc + 1) * 128], hnT[:, kc, kq * 512:(kq + 1) * 512], kc == 0, kc == 7) for kc in range(8)])
                    rl = rl_r.next()
                    k.act(rl[:], ps[:, :], AF.Relu)
                    k.tt("pool", aT[:, fc, :], rl[:], rl[:], ALU.mult)

            def down(kq, aT, w2q=w2q):
                for s_ in range(4):
                    for cb in range(2):
                        ps = ps_rot.next()
                        k.pe([(ps[:, :], aT[:, fc, s_ * 128:(s_ + 1) * 128], w2q[:, fc, cb * 512:(cb + 1) * 512], fc == 0, fc == 7) for fc in range(8)])
                        hv = h2[:, kq * 4 + s_, cb * 512:(cb + 1) * 512]
                        k.tt("dve", hv, ps[:, :], hv, ALU.add)

            aTs = [aT_r.next() for _ in range(4)]
            up(0, aTs[0])
            for kq in range(4):
                if kq < 3:
                    k.weave.run([lambda: down(kq, aTs[kq]), lambda: up(kq + 1, aTs[kq + 1])], [1, 1])
                else:
                    down(kq, aTs[kq])
        k.barrier()
        esC.close()
        k.es = esB
        yo_r = Rot([(k.sb([128, D], F32, f"yo{i}"), k.dsem()) for i in range(2)])
        for t16 in range(16):
            yo, syo = yo_r.next()
            hv = h2[:, t16, :]
            k.act(yo[:], hv, AF.Square, accum=st3[:, 0:1])
            k.act(st3[:, 1:2], st3[:, 0:1], AF.Sqrt, bias=1e-5, scale=1.0 / D)
            k.recip(st3[:, 2:3], st3[:, 1:2])
            k.stt(yo[:], hv, st3[:, 2:3], nmt[:, 1, :], ALU.mult, ALU.mult)
            k.dma("sp", yout[t16 // 4, (t16 % 4) * 128:(t16 % 4 + 1) * 128, :], yo[:], syo)
        k.barrier()
        esB.close()
    k.barrier()
    print("total ops", k.nops, {n: e.count for n, e in k.engs.items() if e.count})
    return nc


def make_consts():
    c = np.zeros((128, 1024), np.float32)
    p = np.arange(128)[:, None]
    q = np.arange(128)[None, :]
    c[:, 0:128] = (p == q)
    c[:, 128:256] = (p <= q)
    c[:, 256:384] = (p // 64 == q // 64)
    pm = np.zeros((128, 128), np.float32)
    for hb in (0, 64):
        for d in range(8):
            pm[hb + d + 8, hb + d] = -1.0
            pm[hb + d, hb + d + 8] = 1.0
    c[:, 384:512] = pm
    c[:, 512:576] = (p % 64 == np.arange(64)[None, :])
    s = (np.arange(128) % 64)[:, None]
    t = np.arange(64)[None, :]
    c[:, 576:640] = (s < t)
    c[:, 640:704] = (s <= t)
    c[:, 704:768] = (s < t)
    c[:, 768:832] = (s <= t)
    c[:, 832:896] = (s > t)
    c[:, 896:1024] = 1.0
    c[:, 896] = 0.0
    c[:, 960] = 0.0
    return c


def make_consts2():
    c = np.zeros((128, 1024), np.float32)
    p = np.arange(128)[:, None]
    q = np.arange(128)[None, :]
    same = (p // 64 == q // 64)
    strict = same & ((p % 64) < (q % 64))
    incl = same & ((p % 64) <= (q % 64))
    lo = same & ((p % 64) > (q % 64))
    for e in range(2):
        c[:, e * 256:e * 256 + 128] = strict
        c[:, e * 256 + 128:e * 256 + 256] = incl
    for h in range(4):
        c[:, 512 + h * 128:512 + (h + 1) * 128] = lo
    return c


def make_rope():
    half = 8
    inv = (np.float32(500000.0) ** (-(np.arange(0, 16, 2, dtype=np.float32)) / np.float32(16))).astype(np.float32)
    pos = np.maximum(np.arange(LP) - 112, 0).astype(np.float32)
    ang = pos[:, None] * inv[None, :]
    cos = np.cos(ang).astype(np.float32)
    sin = np.sin(ang).astype(np.float32)
    ct = np.ones((128, LP), np.float32)
    stb = np.zeros((128, LP), np.float32)
    for hb in (0, 64):
        for d in range(16):
            ct[hb + d] = cos[:, d % 8]
            stb[hb + d] = sin[:, d % 8]
    r = np.stack([ct.reshape(128, NT, 128), stb.reshape(128, NT, 128)], axis=2)
    return np.ascontiguousarray(r)


def core_inputs(inp, c, consts, rope):
    b, g = c // 4, c % 4
    f = lambda a: np.ascontiguousarray(np.asarray(a, dtype=np.float32))
    w_in = inp["w_in"][0]
    sl = slice(256 * g, 256 * g + 256)
    cols = np.concatenate([np.arange(256 * g, 256 * g + 256), 1024 + np.arange(256 * g, 256 * g + 256),
                           2048 + np.arange(256 * g, 256 * g + 256), np.arange(3072, 3360),
                           3360 + np.arange(256 * g, 256 * g + 256), 4384 + np.arange(256 * g, 256 * g + 256),
                           5408 + np.arange(256 * g, 256 * g + 256)])
    mu = inp["rwkv_mu"][0]
    pv = np.zeros((128, 32), np.float32)
    mucols = [mu[0 + 256 * g:0 + 256 * g + 128], mu[256 * g + 128:256 * g + 256],
              mu[1024 + 256 * g:1024 + 256 * g + 128], mu[1024 + 256 * g + 128:1024 + 256 * g + 256],
              mu[2048 + 256 * g:2048 + 256 * g + 128], mu[2048 + 256 * g + 128:2048 + 256 * g + 256],
              mu[3072:3200], mu[3200:3328]]
    for i, m in enumerate(mucols):
        pv[:, i] = m
    pv[0:32, 8] = mu[3328:3360]
    for i, nm in enumerate(["rwkv_w0", "rwkv_a0", "rwkv_k_k", "rwkv_k_a"]):
        vv = inp[nm][0][sl]
        pv[:, 9 + 2 * i] = vv[0:128]
        pv[:, 10 + 2 * i] = vv[128:256]
    rkv = inp["rwkv_r_k"][0].reshape(-1)[sl]
    pv[:, 17] = rkv[0:128]
    pv[:, 18] = rkv[128:256]
    q = g
    x2 = np.stack([inp["x"][b, 2048 * kk + 512 * q:2048 * kk + 512 * q + 512] for kk in range(4)], 0)
    d = {
        "xb": f(inp["x"][b]),
        "meta": f(inp["meta_tokens"]),
        "x2": f(x2),
        "w1c": f(w_in[:, cols]),
        "wg": f(w_in[:, 6432:8480]),
        "pv": pv,
        "nwb": f(np.broadcast_to(inp["norm_mix_w"][0][None, :], (128, D))),
        "lnwb": f(np.broadcast_to(np.stack([inp["rwkv_ln_w"][0][sl], inp["rwkv_ln_b"][0][sl]], 0)[None], (128, 2, 256))),
        "sublnb": f(np.broadcast_to(inp["diff_subln_w"][0][None, :], (128, 128))),
        "lamv": f(np.broadcast_to(np.stack([inp["diff_lq1"][0], inp["diff_lk1"][0], inp["diff_lq2"][0], inp["diff_lk2"][0]], 0)[None], (128, 4, 64))),
        "w2a2": f(np.concatenate([inp["rwkv_w2"][0][:, sl], inp["rwkv_a2"][0][:, sl]], 0)),
        "g2": f(inp["rwkv_g2"][0][:, sl]),
        "wor": f(inp["rwkv_w_o"][0][sl, :]),
        "wod": f(inp["diff_w_o"][0][sl, :]),
        "wout": f(inp["w_out"][0]),
        "mw1": f(inp["mlp_w1"][0]),
        "mw2": f(inp["mlp_w2"][0]),
        "nm2": f(np.broadcast_to(np.stack([inp["norm_mlp_w"][0], inp["final_norm_w"]], 0)[None], (128, 2, D))),
        "rope": rope,
        "cst": consts,
        "cst2": make_consts2(),
    }
    return d


_CACHE = {}


def kernel(**inputs):
    inp = {kk: np.asarray(v) for kk, v in inputs.items()}
    if "nc" not in _CACHE:
        _CACHE["nc"] = build({})
    nc = _CACHE["nc"]
    consts = make_consts()
    rope = make_rope()
    in_maps = [core_inputs(inp, c, consts, rope) for c in range(8)]
    res = run_bass_kernel_spmd(nc, in_maps, core_ids=list(range(8)))
    out = np.zeros((2, 8192, D), np.float32)
    for c in range(8):
        b, q = c // 4, c % 4
        y = np.asarray(res.results[c]["yout"])
        for kk in range(4):
            out[b, 2048 * kk + 512 * q:2048 * kk + 512 * q + 512] = y[kk]
    return out
```

```python
import math
import threading
from contextlib import ExitStack
import numpy as np
import ml_dtypes
import concourse.bass as bass
import concourse.mybir as mybir
from concourse.bass_utils import run_bass_kernel_spmd

F32 = mybir.dt.float32
BF16 = mybir.dt.bfloat16
AF = mybir.ActivationFunctionType
ALU = mybir.AluOpType
AX = mybir.AxisListType

D = 1024
NT = 65
LP = NT * 128
C0 = math.exp(-0.5)
LAMBDA_INIT = 0.8 - 0.6 * math.exp(0.0)


class Tk:
    __slots__ = ("lw", "rd", "name", "psum")

    def __init__(self, name=""):
        self.lw = None
        self.rd = {}
        self.name = name
        self.psum = False


class View:
    __slots__ = ("ap", "tk")

    def __init__(self, ap, tk):
        self.ap = ap
        self.tk = tk


class Buf:
    def __init__(self, t, name, tk=None, dt=None):
        self.t = t
        self.name = name
        self.tk = tk if tk is not None else Tk(name)
        self.dt = dt

    def __getitem__(self, idx):
        ap = self.t[idx] if self.dt is None else self.t.ap().bitcast(self.dt)[idx]
        return View(ap, self.tk)

    def alias(self, name, dt=None):
        return Buf(self.t, name, dt=dt if dt is not None else self.dt)

    def same(self, dt):
        return Buf(self.t, self.name, tk=self.tk, dt=dt)


class EngS:
    def __init__(self, name, eng, sem):
        self.name = name
        self.eng = eng
        self.sem = sem
        self.count = 0
        self.waited = {}


class K:
    def __init__(self, nc, es):
        self.nc = nc
        self.es = es
        self.es_sem = es
        self.engs = {}
        for n, e in (("pe", nc.tensor), ("act", nc.scalar), ("dve", nc.vector), ("pool", nc.gpsimd), ("sp", nc.sync)):
            self.engs[n] = EngS(n, e, es.enter_context(nc.semaphore("sem_" + n)))
        self.nd = 0
        self.nsb = 0
        self.nops = 0
        self.limit = 10 ** 9
        self.weave = Weave()

    def sb(self, shape, dt, name=None):
        self.nsb += 1
        name = name or f"sb{self.nsb}"
        return Buf(self.es.enter_context(self.nc.sbuf_tensor(name, list(shape), dt)), name)

    def dsem(self, name=None):
        self.nd += 1
        name = name or f"dsem{self.nd}"
        e = EngS(name, None, self.es_sem.enter_context(self.nc.semaphore(name)))
        self.engs[name] = e
        return e

    def _wait(self, E, reads, writes):
        deps = {}
        for t in reads:
            if t.lw is not None and deps.get(t.lw[0], 0) < t.lw[1]:
                deps[t.lw[0]] = t.lw[1]
            if t.psum:
                for n, c in t.rd.items():
                    if n != E.name and deps.get(n, 0) < c:
                        deps[n] = c
        for t in writes:
            if t.lw is not None and deps.get(t.lw[0], 0) < t.lw[1]:
                deps[t.lw[0]] = t.lw[1]
            for n, c in t.rd.items():
                if deps.get(n, 0) < c:
                    deps[n] = c
        for n, c in deps.items():
            if n == E.name and n == "pe":
                continue
            if E.waited.get(n, 0) >= c:
                continue
            E.eng.wait_ge(self.engs[n].sem, c)
            E.waited[n] = c

    def op(self, en, fn, reads, writes):
        self.weave.checkpoint()
        self.nops += 1
        if self.nops > self.limit:
            return
        E = self.engs[en]
        self._wait(E, reads, writes)
        ins = fn(E.eng)
        E.count += 1
        ins.then_inc(E.sem, 1)
        for t in reads:
            t.rd[en] = E.count
        for t in writes:
            t.lw = (en, E.count)
            t.rd = {}

    def dma(self, qn, out, in_, dsem, **kw):
        self.weave.checkpoint()
        self.nops += 1
        if self.nops > self.limit:
            return
        Q = self.engs[qn]
        self._wait(Q, [in_.tk], [out.tk])
        ins = Q.eng.dma_start(out=out.ap, in_=in_.ap, **kw)
        dsem.count += 16
        ins.then_inc(dsem.sem, 16)
        in_.tk.rd[dsem.name] = dsem.count
        out.tk.lw = (dsem.name, dsem.count)
        out.tk.rd = {}

    def barrier(self):
        for en in ("sp", "pool", "act", "dve", "pe"):
            E = self.engs[en]
            for n, o in self.engs.items():
                if n == en or o.count == 0:
                    continue
                if E.waited.get(n, 0) < o.count:
                    E.eng.wait_ge(o.sem, o.count)
                    E.waited[n] = o.count

    def ldcast(self, dst, src, stg_r, eng="pool"):
        stg, sem = stg_r.next()
        sv = View(stg.t[:, 0:dst.ap.shape[-1]], stg.tk)
        self.dma("sp", sv, src, sem)
        self.cp(eng, dst, sv)

    def tt(self, en, out, a, b, op):
        self.op(en, lambda e: e.tensor_tensor(out=out.ap, in0=a.ap, in1=b.ap, op=op), [a.tk, b.tk], [out.tk])

    def ts(self, en, out, a, s1, op0, s2=None, op1=None):
        rd = [a.tk]
        v1 = s1
        v2 = s2
        if isinstance(s1, View):
            rd.append(s1.tk)
            v1 = s1.ap
        if isinstance(s2, View):
            rd.append(s2.tk)
            v2 = s2.ap
        kw = {}
        if en == "pool" and op1 is None and s2 is None and op0 == ALU.mult:
            op1 = ALU.add
            v2 = 0.0
        if op1 is not None:
            kw["op1"] = op1
        self.op(en, lambda e: e.tensor_scalar(out=out.ap, in0=a.ap, scalar1=v1, scalar2=v2, op0=op0, **kw), rd, [out.tk])

    def stt(self, out, a, s, b, op0, op1):
        rd = [a.tk, b.tk]
        v = s
        if isinstance(s, View):
            rd.append(s.tk)
            v = s.ap
        self.op("dve", lambda e: e.scalar_tensor_tensor(out=out.ap, in0=a.ap, scalar=v, in1=b.ap, op0=op0, op1=op1), rd, [out.tk])

    def cp(self, en, out, a):
        if en == "act":
            self.op(en, lambda e: e.copy(out=out.ap, in_=a.ap), [a.tk], [out.tk])
        else:
            self.op(en, lambda e: e.tensor_copy(out=out.ap, in_=a.ap), [a.tk], [out.tk])

    def act(self, out, a, func, bias=None, scale=1.0, accum=None):
        rd = [a.tk]
        wr = [out.tk]
        kw = {}
        if isinstance(bias, View):
            rd.append(bias.tk)
            kw["bias"] = bias.ap
        elif bias is not None:
            kw["bias"] = bias
        if isinstance(scale, View):
            rd.append(scale.tk)
            kw["scale"] = scale.ap
        else:
            kw["scale"] = scale
        if accum is not None:
            wr.append(accum.tk)
            kw["accum_out"] = accum.ap
        self.op("act", lambda e: e.activation(out=out.ap, in_=a.ap, func=func, **kw), rd, wr)

    def memset(self, en, out, val):
        self.op(en, lambda e: e.memset(out.ap, val), [], [out.tk])

    def red(self, out, a, op=ALU.add):
        self.op("dve", lambda e: e.tensor_reduce(out=out.ap, in_=a.ap, op=op, axis=AX.X), [a.tk], [out.tk])

    def recip(self, out, a):
        self.op("dve", lambda e: e.reciprocal(out=out.ap, in_=a.ap), [a.tk], [out.tk])

    def pe(self, items):
        rd = []
        wr = []
        for it in items:
            if it[0] == "T":
                wr.append(it[1].tk)
                rd += [it[2].tk, it[3].tk]
            else:
                wr.append(it[0].tk)
                rd += [it[1].tk, it[2].tk]

        def fn(e):
            ins = None
            for it in items:
                if it[0] == "T":
                    ins = e.transpose(it[1].ap, it[2].ap, it[3].ap)
                else:
                    ins = e.matmul(it[0].ap, lhsT=it[1].ap, rhs=it[2].ap, start=it[3], stop=it[4])
            return ins

        self.op("pe", fn, rd, wr)


class Weave:
    def __init__(self):
        self.cv = threading.Condition()
        self.active = None
        self.tl = threading.local()

    def current(self):
        return getattr(self.tl, "sid", 0) if self.active is not None else None

    def _pick(self):
        best = None
        for i in range(len(self.alive)):
            if self.alive[i]:
                f = self.done[i] / self.est[i]
                if best is None or f < best[0]:
                    best = (f, i)
        self.active = best[1] if best is not None else -1

    def checkpoint(self):
        if self.active is None:
            return
        sid = self.tl.sid
        with self.cv:
            self.done[sid] += 1
            self._pick()
            self.cv.notify_all()
            while self.active != sid:
                self.cv.wait()

    def run(self, fns, ests):
        n = len(fns)
        self.done = [0] * n
        self.est = [max(1, e) for e in ests]
        self.alive = [True] * n
        self.err = []

        def body(i):
            self.tl.sid = i
            with self.cv:
                while self.active != i:
                    self.cv.wait()
            try:
                fns[i]()
            except BaseException as ex:
                self.err.append(ex)
            with self.cv:
                self.alive[i] = False
                self._pick()
                self.cv.notify_all()

        ths = [threading.Thread(target=body, args=(i,)) for i in range(n)]
        with self.cv:
            self.active = 0
        for t in ths:
            t.start()
        for t in ths:
            t.join()
        self.active = None
        if self.err:
            raise self.err[0]
        return list(self.done)


class RotSel:
    def __init__(self, weave, pools):
        self.weave = weave
        self.pools = pools

    def next(self):
        c = self.weave.current()
        return self.pools[-1 if c is None else c].next()


class Rot:
    def __init__(self, bufs):
        self.bufs = bufs
        self.i = 0

    def next(self):
        b = self.bufs[self.i % len(self.bufs)]
        self.i += 1
        return b


def build(cfg):
    NT1 = cfg.get("nt1", NT)
    do_p2 = cfg.get("p2", True)
    do_rs = cfg.get("rs", True)
    dbg = cfg.get("dbg", False)
    nc = bass.Bass("TRN2", target_bir_lowering=False)
    es = ExitStack()
    k = K(nc, es)
    k.limit = cfg.get("limit", 10 ** 9)

    def din(name, shape, dt=F32):
        return Buf(nc.dram_tensor(name, list(shape), dt, kind="ExternalInput"), name)

    def dout(name, shape, dt=F32):
        return Buf(nc.dram_tensor(name, list(shape), dt, kind="ExternalOutput"), name)

    xb = din("xb", [8192, D])
    meta = din("meta", [16, D])
    x2 = din("x2", [4, 512, D])
    w1c = din("w1c", [D, 1824])
    wg = din("wg", [D, 2048])
    pv = din("pv", [128, 32])
    nwb = din("nwb", [128, D])
    lnwb = din("lnwb", [128, 2, 256])
    sublnb = din("sublnb", [128, 128])
    lamv = din("lamv", [128, 4, 64])
    w2a2 = din("w2a2", [128, 256])
    g2 = din("g2", [160, 256])
    wor = din("wor", [256, D])
    wod = din("wod", [256, D])
    wout = din("wout", [D, D])
    mw1 = din("mw1", [D, 4096])
    mw2 = din("mw2", [4096, D])
    nm2 = din("nm2", [128, 2, D])
    rs_dbg = [din(f"rs_dbg{i}", [512, 2048], BF16) for i in range(4)] if cfg.get("p2only") else None
    rope = din("rope", [128, NT, 2, 128])
    cst = din("cst", [128, 1024])
    cst2 = din("cst2", [128, 1024])
    yout = dout("yout", [4, 512, D])
    rs_in = [Buf(nc.dram_tensor(f"rs_in{i}", [2048, 2048], BF16), f"rs_in{i}") for i in range(4)]
    rs_tk = [[Tk(f"rs{i}_{r}") for r in range(16)] for i in range(4)]
    rs_out = [Buf(nc.dram_tensor(f"rs_out{i}", [512, 2048], BF16), f"rs_out{i}") for i in range(4)]
    if dbg:
        d_mix = dout("d_mix", [NT1 * 128, 512])

    psb = [Buf(es.enter_context(nc.psum_tensor(f"ps{i}", [128, 512], F32)), f"ps{i}") for i in range(8)]
    for pb_ in psb:
        pb_.tk.psum = True
    ps_rot = RotSel(k.weave, [Rot(psb[0:3]), Rot(psb[3:6]), Rot(psb[6:7]), Rot(psb[0:7])])
    ps_o = [psb[7], psb[7]]

    mhalf = k.sb([128, 4], F32, "mhalf")
    k.memset("dve", mhalf[:], -0.5)
    identb = k.sb([128, 128], BF16, "identb")
    k.dma("pool", identb[:], cst[:, 0:128], k.dsem())
    es1 = ExitStack()
    k.es = es1
    cs = k.sb([128, 384], F32, "cs")
    sem_c = k.dsem("sem_c")
    k.dma("sp", cs[:, 0:256], cst[:, 256:512], sem_c)
    k.dma("sp", cs[:, 256:384], cst[:, 896:1024], sem_c)

    cmaskb = k.sb([128, 128], BF16, "cmaskb")
    k.dma("pool", cmaskb[:], cst[:, 128:256], k.dsem())
    mkb = k.sb([128, 1024], BF16, "mkb")
    k.dma("pool", mkb[:], cst2[:, :], k.dsem())
    bones = k.sb([128, 2], BF16, "bones")
    k.cp("dve", bones[:], cs[:, 0:128:64])
    pvt = k.sb([128, 32], F32, "pvt")
    k.dma("sp", pvt[:], pv[:, :], k.dsem())
    hb = k.sb([128, 4], F32, "hb")
    k.ts("dve", hb[:], pvt[:, 9:13], 0.5, ALU.mult)
    omka = k.sb([128, 2], F32, "omka")
    k.ts("dve", omka[:], pvt[:, 15:17], -1.0, ALU.mult, 1.0, ALU.add)
    nw_t = k.sb([128, D], BF16, "nw_t")
    k.dma("pool", nw_t[:], nwb[:, :], k.dsem())
    lnw_t = k.sb([128, 2, 256], F32, "lnw_t")
    k.dma("sp", lnw_t[:], lnwb[:, :, :], k.dsem())
    subln_t = k.sb([128, 128], F32, "subln_t")
    k.dma("sp", subln_t[:], sublnb[:, :], k.dsem())
    k.ts("dve", subln_t[:], subln_t[:], 1.0 - LAMBDA_INIT, ALU.mult)
    lam_in = k.sb([128, 4, 64], F32, "lam_in")
    k.dma("sp", lam_in[:], lamv[:, :, :], k.dsem())
    lam_t = k.sb([128, 4], F32, "lam_t")
    lam_j = k.sb([128, 64], F32, "lam_j")
    for i in range(2):
        k.tt("dve", lam_j[:], lam_in[:, 2 * i, :], lam_in[:, 2 * i + 1, :], ALU.mult)
        k.red(lam_t[:, i:i + 1], lam_j[:])
    k.act(lam_t[:, 0:2], lam_t[:, 0:2], AF.Exp)
    k.tt("dve", lam_t[:, 2:3], lam_t[:, 0:1], lam_t[:, 1:2], ALU.subtract)
    k.ts("dve", lam_t[:, 3:4], lam_t[:, 2:3], LAMBDA_INIT, ALU.add, -1.0, ALU.mult)
    neglam = lam_t[:, 3:4]

    mub = k.sb([128, 9, 128], F32, "mub")
    for c in range(9):
        k.ts("pool", mub[:, c, :], cs[:, 256:384], 0.0, ALU.mult, pvt[:, c:c + 1], ALU.add)

    w1b = k.sb([128, 8, 1824], BF16, "w1b")
    esS = ExitStack()
    k.es = esS
    stg1 = Rot([(k.sb([128, 1824], F32, f"stg1_{i}"), k.dsem()) for i in range(2)])
    for kc in range(8):
        k.ldcast(w1b[:, kc, :], w1c[kc * 128:(kc + 1) * 128, :], stg1, "act" if kc % 2 == 0 else "dve")
    k.barrier()
    esS.close()
    k.es = es1
    w2p = k.sb([128, 2, 256], BF16, "w2p")
    k.memset("pool", w2p[:], 0.0)
    sem_w2 = k.dsem()
    k.dma("pool", View(w2p.t[0:64, 0, :], w2p.tk), w2a2[0:64, :], sem_w2)
    k.dma("pool", View(w2p.t[64:128, 1, :], w2p.tk), w2a2[64:128, :], sem_w2)
    g2b = k.sb([128, 2, 256], BF16, "g2b")
    k.memset("pool", g2b[:], 0.0)
    sem_g2 = k.dsem()
    k.dma("pool", g2b[:, 0, :], g2[0:128, :], sem_g2)
    k.dma("pool", View(g2b.t[0:32, 1, :], g2b.tk), g2[128:160, :], sem_g2)
    wob = k.sb([128, 4, D], BF16, "wob")
    sem_wo = k.dsem()
    for i in range(2):
        k.dma("pool", wob[:, i, :], wor[i * 128:(i + 1) * 128, :], sem_wo)
        k.dma("pool", wob[:, 2 + i, :], wod[i * 128:(i + 1) * 128, :], sem_wo)

    KT = k.sb([128, 2, LP], BF16, "KT")
    KTr = [KT.alias(f"KT{j}") for j in range(NT)]
    VA = k.sb([128, NT, 2, 130], BF16, "VA")
    VAr = [VA.alias(f"VA{j}") for j in range(NT)]
    k.op("pool", lambda e: e.memset(VA.t[:, :, :, 128:130], 1.0), [], [VA.tk] + [r.tk for r in VAr])
    k.op("pool", lambda e: e.memset(VA.t[0:112, 0, :, 128:130], 0.0), [], [VA.tk, VAr[0].tk])
    raw = k.sb([128, 9, 129], F32, "raw")
    k.memset("pool", raw[:], 0.0)
    H32 = k.sb([128, 2, 64], F32, "H32")
    k.memset("dve", H32[:], 0.0)
    Hb = k.sb([128, 2, 64], BF16, "Hb")
    k.memset("dve", Hb[:], 0.0)

    def rot(n, shape, dt, name):
        return Rot([k.sb(shape, dt, f"{name}{i}") for i in range(n)])

    xt_r = Rot([(k.sb([128, D], F32, f"xt{i}"), k.dsem(f"sem_xt{i}")) for i in range(1)])
    rp_r = Rot([(k.sb([128, 2, 128], F32, f"rp{i}"), k.dsem(f"sem_rp{i}")) for i in range(2)])
    junk = k.sb([128, 128], F32, "junk")
    st_r = rot(2, [128, 8], F32, "st")
    xs_r = rot(1, [128, D], BF16, "xs")
    hT_r = rot(1, [128, 8, 128], BF16, "hT")
    sh_r = rot(2, [128, 9, 128], F32, "sh")
    qkraw_r = rot(1, [128, 4, 128], F32, "qkraw")
    QT_r = rot(2, [128, 2, 2, 128], BF16, "QT")
    for qb in QT_r.bufs:
        k.memset("pool", qb[:], 0.0)
    ropt = k.sb([128, 4, 128], F32, "ropt")
    PT_r = rot(3, [128, 4, 128], BF16, "PT")
    on_r = rot(1, [128, 4, 128], F32, "on")
    rcp_r = rot(2, [128, 4], F32, "rcp")
    dsc = k.sb([128, 256], F32, "dsc")
    mix_r = rot(2, [128, 512], BF16, "mix")
    mixT_r = rot(2, [128, 4, 128], BF16, "mixT")
    po_r = Rot([(k.sb([128, 2048], BF16, f"po{i}"), k.dsem(f"sem_po{i}")) for i in range(1)])
    if dbg:
        dmix_r = Rot([(k.sb([128, 512], F32, f"dmix{i}"), k.dsem(f"sem_dmix{i}")) for i in range(1)])

    def f32t(name, shape=(128, 2, 128)):
        return k.sb(list(shape), F32, name)

    lorab = k.sb([128, 128], BF16, "lorab")
    sgd = k.sb([128, 2, 128], BF16, "sgd")
    k.memset("pool", sgd[:], 0.0)
    sw = f32t("sw")
    cum = f32t("cum")
    aa = f32t("aa")
    Ew = f32t("Ew")
    Ewx = f32t("Ewx")
    Einv = f32t("Einv")
    Eend = f32t("Eend")
    bC = k.sb([128, 2, 2], F32, "bC")
    WC = k.sb([128, 2, 2], F32, "WC")
    kk = f32t("kk")
    kq = f32t("kq")
    rn = f32t("rn")
    k2 = f32t("k2")
    bvec = f32t("bvec")
    rk = k.sb([128, 2, 128], BF16, "rk")
    BK = k.sb([128, 2, 2, 128], BF16, "BK")
    RT = k.sb([128, 2, 128], BF16, "RT")
    ARbd = k.sb([128, 2, 2, 2, 128], BF16, "ARbd")
    Bbd = k.sb([128, 2, 2, 128], BF16, "Bbd")
    TMbd = k.sb([128, 2, 2, 2, 128], BF16, "TMbd")
    PTbd = k.sb([128, 2, 2, 128], BF16, "PTbd")
    Hbd = k.sb([128, 2, 2, 64], BF16, "Hbd")
    for zb in (ARbd, Bbd, TMbd, PTbd, Hbd):
        k.memset("pool", zb[:], 0.0)
    FT = k.sb([128, 4, 2, 128], BF16, "FT")
    TM = k.sb([128, 4, 2, 128], BF16, "TM")
    S4 = k.sb([128, 4, 4, 128], BF16, "S4")
    ULr = rot(2, [128, 4, 2, 128], BF16, "UL")
    Yr = rot(2, [128, 4, 128], BF16, "Y")
    GTs = k.sb([128, 2, 2, 64], BF16, "GTs")
    yv = k.sb([128, 4, 64], F32, "yv")
    gst = k.sb([128, 16], F32, "gst")
    bsum = k.sb([128, 4], F32, "bsum")
    gtm = k.sb([128, 256], F32, "gtm")
    ytmp = k.sb([128, 4, 64], F32, "ytmp")

    heads = [(c2, e) for c2 in range(2) for e in range(2)]

    def p1_tile(j):
        xt, sx = xt_r.next()
        if j == 0:
            k.memset("pool", xt[:], 0.0)
            k.dma("sp", View(xt.t[112:128, :], xt.tk), meta[:, :], sx)
        else:
            k.dma("sp", xt[:], xb[(j - 1) * 128:j * 128, :], sx)
        rp, srp = rp_r.next()
        k.dma("sp", rp[:], rope[:, j, :, :], srp)
        st = st_r.next()
        xs = xs_r.next()
        k.act(xs[:], xt[:], AF.Square, accum=st[:, 0:1])
        k.ts("pool", st[:, 1:2], st[:, 0:1], 1.0 / D, ALU.mult, 1e-5, ALU.add)
        k.tt("pool", st[:, 2:3], st[:, 1:2], mhalf[:, 0:1], ALU.pow)
        k.stt(xs[:], xt[:], st[:, 2:3], nw_t[:], ALU.mult, ALU.mult)
        ps = ps_rot.next()
        psb16 = ps.same(BF16)
        k.pe([("T", psb16[:, kc * 128:(kc + 1) * 128], xs[:, kc * 128:(kc + 1) * 128], identb[:]) for kc in range(8)])
        hT = hT_r.next()
        k.cp("act", hT[:], View(psb16.t.ap().bitcast(BF16)[:, 0:1024].rearrange("p (a b) -> p a b", a=8), ps.tk))
        def proj(ps, slot, col0, width=128, rows=128):
            return [(View(ps.t[0:rows, slot * 128:slot * 128 + 128], ps.tk) if rows != 128 else ps[:, slot * 128:slot * 128 + 128],
                     w1b[:, kc, col0:col0 + width], hT[:, kc, :], kc == 0, kc == 7) for kc in range(8)]
        psA = ps_rot.next()
        items = []
        for s in range(4):
            items += proj(psA, s, s * 128)
        k.pe(items)
        k.cp("act", raw[:, 0:4, 1:129], View(psA.t[:, :].rearrange("p (a b) -> p a b", a=4), psA.tk))
        psB = ps_rot.next()
        items = []
        for s in range(4):
            items += proj(psB, s, 512 + s * 128)
        k.pe(items)
        k.cp("act", raw[:, 4:8, 1:129], View(psB.t[:, :].rearrange("p (a b) -> p a b", a=4), psB.tk))
        psC = ps_rot.next()
        items = []
        for s in range(4):
            items += proj(psC, s, 1056 + s * 128)
        k.pe(items)
        qkraw = qkraw_r.next()
        k.cp("dve", qkraw[:], View(psC.t[:, :].rearrange("p (a b) -> p a b", a=4), psC.tk))
        psD = ps_rot.next()
        items = proj(psD, 0, 1024, width=32, rows=32)
        items += [(psD[:, 128:384], hT[:, kc, :], w1b[:, kc, 1568:1824], kc == 0, kc == 7) for kc in range(8)]
        k.pe(items)
        k.cp("act", View(raw.t[0:32, 8, 1:129], raw.tk), View(psD.t[0:32, 0:128], psD.tk))
        k.cp(cfg.get("vaeng", "act"), View(VA.t[:, j, :, 0:128], VAr[j].tk), View(psD.t[:, 128:384].rearrange("p (a b) -> p a b", a=2), psD.tk))
        if j == 0:
            k.memset("pool", View(VA.t[0:112, 0, :, 0:128], VAr[0].tk), 0.0)
        sh = sh_r.next()
        k.tt("pool", sh[:], raw[:, :, 0:128], raw[:, :, 1:129], ALU.subtract)
        k.tt("pool", sh[:], sh[:], mub[:], ALU.mult)
        k.tt("pool", sh[:], sh[:], raw[:, :, 1:129], ALU.add)
        k.cp("pool", raw[:, :, 0:1], raw[:, :, 128:129])
        psR = ps_rot.next()
        k.pe([(psR[:, s * 128:(s + 1) * 128], cs[:, 128:256], qkraw[:, s, :], True, True) for s in range(4)])
        for s in range(4):
            k.tt("dve", ropt[:, s, :], psR[:, s * 128:(s + 1) * 128], rp[:, 1, :], ALU.mult)
            k.tt("pool", qkraw[:, s, :], qkraw[:, s, :], rp[:, 0, :], ALU.mult)
        QT = QT_r.next()
        for e in range(2):
            pb = 64 * e
            k.tt("pool", View(QT.t[pb:pb + 64, :, e, :], QT.tk), View(qkraw.t[pb:pb + 64, 0:2, :], qkraw.tk),
                 View(ropt.t[pb:pb + 64, 0:2, :], ropt.tk), ALU.add)
        k.tt("pool", View(KT.t[:, :, j * 128:(j + 1) * 128], KTr[j].tk), qkraw[:, 2:4, :], ropt[:, 2:4, :], ALU.add)
        return sh, QT

    def attn_tile(j, QT, mix):
        on = on_r.next()
        rcp = rcp_r.next()
        groups = [(h, i0) for h in range(4) for i0 in range(0, j + 1, 4)]
        pss = {}

        def qk(g):
            h, i0 = groups[g]
            dh, e = h // 2, h % 2
            n = min(4, j + 1 - i0)
            ps = ps_rot.next()
            k.pe([(ps[:, s * 128:(s + 1) * 128],
                   View(KT.t[:, dh, (i0 + s) * 128:(i0 + s + 1) * 128], KTr[i0 + s].tk),
                   QT[:, dh, e, :], True, True) for s in range(n)])
            pss[g] = ps

        for g in range(min(2, len(groups))):
            qk(g)
        for g, (h, i0) in enumerate(groups):
            dh, e = h // 2, h % 2
            po = ps_o[h % 2]
            n = min(4, j + 1 - i0)
            ps = pss.pop(g)
            PT = PT_r.next()
            k.act(View(PT.t[:, 0:n, :], PT.tk), View(ps.t[:, 0:n * 128].rearrange("p (a b) -> p a b", a=n), ps.tk), AF.Exp, scale=0.125)
            if i0 + n - 1 == j:
                k.tt("pool", PT[:, n - 1, :], PT[:, n - 1, :], cmaskb[:], ALU.mult)
            if g + 2 < len(groups):
                qk(g + 2)
            k.pe([(po[:, 0:130], PT[:, s, :], View(VA.t[:, i0 + s, dh, :], VAr[i0 + s].tk), (i0 + s) == 0, (i0 + s) == j)
                  for s in range(n)])
            if i0 + n - 1 == j:
                k.recip(rcp[:, h:h + 1], po[:, 128:129])
                k.ts("dve", on[:, h, :], po[:, 0:128], rcp[:, h:h + 1], ALU.mult)
        for dh in range(2):
            o = dsc[:, dh * 128:(dh + 1) * 128]
            k.stt(o, on[:, 2 * dh + 1, :], neglam, on[:, 2 * dh, :], ALU.mult, ALU.add)
            k.act(junk[:], o, AF.Square, accum=rcp[:, dh:dh + 1])
        k.ts("pool", rcp[:, 0:2], rcp[:, 0:2], 1.0 / 128, ALU.mult, 1e-5, ALU.add)
        k.tt("pool", rcp[:, 2:4], rcp[:, 0:2], mhalf[:, 0:2], ALU.pow)
        for dh in range(2):
            o = dsc[:, dh * 128:(dh + 1) * 128]
            k.stt(mix[:, 256 + dh * 128:256 + (dh + 1) * 128], o, rcp[:, 2 + dh:3 + dh], subln_t[:], ALU.mult, ALU.mult)

    def rwkv_tile(j, sh, mix):
        r = sh[:, 0:2, :]
        kx = sh[:, 2:4, :]
        v = sh[:, 4:6, :]
        k.act(View(lorab.t[0:64, :], lorab.tk), View(sh.t[0:64, 6, :], sh.tk), AF.Tanh)
        k.cp("act", View(lorab.t[64:128, :], lorab.tk), View(sh.t[64:128, 6, :], sh.tk))
        k.act(sgd[:, 0, :], sh[:, 7, :], AF.Tanh, scale=0.5)
        k.act(View(sgd.t[0:32, 1, :], sgd.tk), View(sh.t[0:32, 8, :], sh.tk), AF.Tanh, scale=0.5)
        k.ts("pool", sgd[:, 0, :], sgd[:, 0, :], 0.5, ALU.mult, 0.5, ALU.add)
        k.ts("pool", View(sgd.t[0:32, 1, :], sgd.tk), View(sgd.t[0:32, 1, :], sgd.tk), 0.5, ALU.mult, 0.5, ALU.add)
        psZ = ps_rot.next()
        items = []
        for c2 in range(2):
            items.append((psZ[:, c2 * 128:(c2 + 1) * 128], w2p[:, 0, c2 * 128:(c2 + 1) * 128], lorab[:], True, True))
            items.append((psZ[:, 256 + c2 * 128:256 + (c2 + 1) * 128], w2p[:, 1, c2 * 128:(c2 + 1) * 128], lorab[:], True, True))
        k.pe(items)
        for c2 in range(2):
            k.act(sw[:, c2, :], psZ[:, c2 * 128:(c2 + 1) * 128], AF.Tanh, bias=hb[:, c2:c2 + 1], scale=0.5)
            k.act(aa[:, c2, :], psZ[:, 256 + c2 * 128:256 + (c2 + 1) * 128], AF.Tanh, bias=hb[:, 2 + c2:3 + c2], scale=0.5)
        k.ts("dve", sw[:], sw[:], 0.5, ALU.mult, 0.5, ALU.add)
        k.ts("pool", aa[:], aa[:], 0.5, ALU.mult, 0.5, ALU.add)
        psG = ps_rot.next()
        k.pe([(psG[:, 0:256], sgd[:, 0, :], g2b[:, 0, :], True, False),
              (psG[:, 0:256], sgd[:, 1, :], g2b[:, 1, :], False, True)])
        k.cp("act", gtm[:], psG[:, 0:256])
        for c2 in range(2):
            k.op("dve", lambda e, c2=c2: e.tensor_tensor_scan(out=cum.t[:, c2, :], data0=cs.t[:, 256:384], data1=sw.t[:, c2, :],
                                                           initial=0.0, op0=ALU.mult, op1=ALU.add), [cs.tk, sw.tk], [cum.tk])
        k.tt("pool", Ewx[:], cum[:], sw[:], ALU.subtract)
        k.act(Ew[:], cum[:], AF.Exp, scale=-C0)
        k.act(Ewx[:], Ewx[:], AF.Exp, scale=-C0)
        k.act(Einv[:], cum[:], AF.Exp, scale=C0)
        k.ts("dve", bC[:], View(cum.t[:, :, 63:128:64], cum.tk), -C0, ALU.mult)
        k.act(WC[:], bC[:], AF.Exp)
        for c2 in range(2):
            for ch in range(2):
                k.act(Eend[:, c2, ch * 64:(ch + 1) * 64], cum[:, c2, ch * 64:(ch + 1) * 64], AF.Exp, scale=C0, bias=bC[:, c2, ch:ch + 1])
        for c2 in range(2):
            k.act(kk[:, c2, :], sh[:, 2 + c2, :], AF.Copy, scale=pvt[:, 13 + c2:14 + c2])
        k.tt("pool", kq[:], kk[:], kk[:], ALU.mult)
        psN = ps_rot.next()
        k.pe([(psN[:, c2 * 128:(c2 + 1) * 128], cs[:, 0:128], kq[:, c2, :], True, True) for c2 in range(2)])
        k.act(rn[:], View(psN.t[:, 0:256].rearrange("p (a b) -> p a b", a=2), psN.tk), AF.Sqrt)
        k.ts("dve", rn[:], rn[:], 1e-12, ALU.max)
        k.recip(rn[:], rn[:])
        k.tt("pool", kk[:], kk[:], rn[:], ALU.mult)
        for c2 in range(2):
            k.ts("dve", k2[:, c2, :], aa[:, c2, :], pvt[:, 15 + c2:16 + c2], ALU.mult, omka[:, c2:c2 + 1], ALU.add)
        k.tt("pool", k2[:], kx, k2[:], ALU.mult)
        k.tt("pool", bvec[:], kk[:], aa[:], ALU.mult)

        def c4(buf):
            return buf[:, :, :]

        k.stt(FT[:, 2, :, :], kk[:], -1.0, Ewx[:], ALU.mult, ALU.mult)
        k.tt("dve", RT[:], r, Ew[:], ALU.mult)
        k.tt("pool", BK[:, :, 0, :], bvec[:], Einv[:], ALU.mult)
        k.tt("dve", BK[:, :, 1, :], k2[:], Einv[:], ALU.mult)
        k.tt("pool", FT[:, 0, :, :], bvec[:], Eend[:], ALU.mult)
        k.tt("dve", FT[:, 1, :, :], k2[:], Eend[:], ALU.mult)
        k.cp("act", FT[:, 3, :, :], v)
        for e in range(2):
            pb = 64 * e
            k.cp("act", View(ARbd.t[pb:pb + 64, :, e, 0, :], ARbd.tk), View(FT.t[pb:pb + 64, 2, :, :], FT.tk))
            k.cp("act", View(ARbd.t[pb:pb + 64, :, e, 1, :], ARbd.tk), View(RT.t[pb:pb + 64, :, :], RT.tk))
            k.cp("act", View(Bbd.t[pb:pb + 64, :, e, :], Bbd.tk), View(BK.t[pb:pb + 64, :, 0, :], BK.tk))
        k.tt("pool", kq[:], r, k2[:], ALU.mult)
        for c2 in range(2):
            k.act(rk[:, c2, :], kq[:, c2, :], AF.Copy, scale=pvt[:, 17 + c2:18 + c2])
        psT = ps_rot.next()
        psT16 = psT.same(BF16)
        k.pe([("T", psT16[:, (kind * 2 + c2) * 128:(kind * 2 + c2 + 1) * 128], FT[:, kind, c2, :], identb[:])
              for kind in range(4) for c2 in range(2)])
        psTv = psT.t.ap().bitcast(BF16)[:, 0:1024].rearrange("p (a b c) -> p a b c", a=4, b=2)
        k.cp("act", TM[:], View(psTv, psT.tk))
        for ch in range(2):
            pt = 64 * ch
            k.cp("act", View(TMbd.t[pt:pt + 64, :, :, ch, :], TMbd.tk), View(psTv[pt:pt + 64, 0:2, :, :], psT.tk))
        UL = ULr.next()
        mk1 = View(mkb.t[:, 0:512].rearrange("p (a b c) -> p a b c", a=2, b=2), mkb.tk)
        for c2 in range(2):
            arv = View(ARbd.t[:, c2, :, :, :].rearrange("p a b c -> p (a b c)"), ARbd.tk)
            for kind in range(2):
                psg = ps_rot.next()
                k.pe([(psg[:, :], BK[:, c2, kind, :], arv, True, True)])
                k.tt("dve", S4[:, 2 * c2:2 * c2 + 2, 2 * kind:2 * kind + 2, :],
                     View(psg.t[:, :].rearrange("p (a b c) -> p a b c", a=2, b=2), psg.tk), mk1, ALU.mult)
        psLo = ps_rot.next()
        k.pe([(psLo[:, c2 * 256:(c2 + 1) * 256], FT[:, 2, c2, :], View(Bbd.t[:, c2, :, :].rearrange("p a b -> p (a b)"), Bbd.tk), True, True)
              for c2 in range(2)])
        k.tt("dve", UL[:, :, 1, :], View(psLo.t[:, :].rearrange("p (h q) -> p h q", h=4), psLo.tk),
             View(mkb.t[:, 512:1024].rearrange("p (h q) -> p h q", h=4), mkb.tk), ALU.mult)
        k.cp("act", UL[:, :, 0, :], S4[:, :, 0, :])
        Y = Yr.next()
        psY = ps_rot.next()
        k.pe([(psY[:, hh * 64:(hh + 1) * 64], S4[:, hh, 2, :], TM[:, 3, c2, 64 * e:64 * e + 64], True, True)
              for hh, (c2, e) in enumerate(heads)])
        k.cp("act", Y[:, :, 0:64], View(TM.t[:, 2, :, :].rearrange("p a (e q) -> p (a e) q", e=2), TM.tk))
        k.cp("dve", Y[:, :, 64:128], View(psY.t[:, 0:256].rearrange("p (h q) -> p h q", h=4), psY.tk))
        for lvl in range(6):
            psY = ps_rot.next()
            k.pe([(psY[:, hh * 128:(hh + 1) * 128], UL[:, hh, 0, :], Y[:, hh, :], True, True) for hh in range(4)])
            if lvl < 5:
                psU = [ps_rot.next(), ps_rot.next()]
                items = []
                for hh in range(4):
                    pu = psU[hh // 2]
                    o0 = (hh % 2) * 256
                    items.append((pu[:, o0:o0 + 128], UL[:, hh, 1, :], UL[:, hh, 0, :], True, True))
                    if lvl < 4:
                        items.append((pu[:, o0 + 128:o0 + 256], UL[:, hh, 0, :], UL[:, hh, 1, :], True, True))
                k.pe(items)
            Yn = Yr.next()
            k.tt("dve", Yn[:], View(psY.t[:, :].rearrange("p (h q) -> p h q", h=4), psY.tk), Y[:], ALU.add)
            Y = Yn
            if lvl < 5:
                UL = ULr.next()
                for half in range(2):
                    k.cp("act", UL[:, 2 * half:2 * half + 2, :, :], View(psU[half].t[:, :].rearrange("p (h a q) -> p h a q", h=2, a=2), psU[half].tk))
        psP = ps_rot.next()
        items = []
        for hh, (c2, e) in enumerate(heads):
            pbk = 64 * e
            m1 = Y[:, hh, 0:64]
            items.append((View(psP.t[pbk:pbk + 64, c2 * 256:c2 * 256 + 128], psP.tk), m1,
                          View(TMbd.t[:, 0, c2, :, pbk:pbk + 64], TMbd.tk), True, True))
            items.append((View(psP.t[pbk:pbk + 64, c2 * 256 + 128:c2 * 256 + 256], psP.tk), m1, S4[:, hh, 1, :], True, True))
        k.pe(items)
        ppv = psP.t[:, :].rearrange("p (a x) -> p a x", a=2)
        for e in range(2):
            pbk = 64 * e
            k.cp("dve", View(PTbd.t[pbk:pbk + 64, :, :, pbk:pbk + 64], PTbd.tk),
                 View(ppv[pbk:pbk + 64, :, 0:128].rearrange("p a (c q) -> p a c q", c=2), psP.tk))
        k.tt("dve", GTs[:], View(ppv[:, :, 128:256].rearrange("p a (c q) -> p a c q", c=2), psP.tk),
             View(RT.t[:, :, :].rearrange("p a (c q) -> p a c q", c=2), RT.tk), ALU.add)
        psBn = ps_rot.next()
        k.pe([(psBn[:, 2 * c2:2 * c2 + 2], rk[:, c2, :], bones[:], True, True) for c2 in range(2)])
        k.cp("act", bsum[:], psBn[:, 0:4])
        psYo = ps_rot.next()
        for ch in range(2):
            pt = 64 * ch
            psH = ps_rot.next()
            items = []
            for c2 in range(2):
                items.append((View(psYo.t[pt:pt + 64, c2 * 128:(c2 + 1) * 128], psYo.tk), GTs[:, c2, ch, :],
                              View(Hbd.t[:, c2, :, :].rearrange("p a b -> p (a b)"), Hbd.tk), True, False))
                for e in range(2):
                    hh = 2 * c2 + e
                    yo = View(psYo.t[pt:pt + 64, hh * 64:(hh + 1) * 64], psYo.tk)
                    items.append((yo, S4[:, hh, 1, pt:pt + 64], Y[:, hh, 64:128], False, False))
                    items.append((yo, S4[:, hh, 3, pt:pt + 64], TM[:, 3, c2, 64 * e:64 * e + 64], False, e == 1))
            for c2 in range(2):
                items.append((psH[:, c2 * 64:(c2 + 1) * 64], PTbd[:, c2, ch, :], Hb[:, c2, :], True, False))
                for e in range(2):
                    hh = 2 * c2 + e
                    pbk = 64 * e
                    ho = View(psH.t[pbk:pbk + 64, c2 * 64:(c2 + 1) * 64], psH.tk)
                    items.append((ho, TMbd[:, 0, c2, ch, pbk:pbk + 64], Y[:, hh, 64:128], False, False))
                    items.append((ho, TMbd[:, 1, c2, ch, pbk:pbk + 64], TM[:, 3, c2, pbk:pbk + 64], False, True))
            k.pe(items)
            for c2 in range(2):
                k.stt(H32[:, c2, :], H32[:, c2, :], WC[:, c2, ch:ch + 1], psH[:, c2 * 64:(c2 + 1) * 64], ALU.mult, ALU.add)
            k.cp("act", Hb[:], H32[:])
            for e in range(2):
                pbk = 64 * e
                k.cp("act", View(Hbd.t[pbk:pbk + 64, :, e, :], Hbd.tk), View(H32.t[pbk:pbk + 64, :, :], H32.tk))
        if j == 0:
            return
        k.cp("act", yv[:], View(psYo.t[:, 0:256].rearrange("p (h q) -> p h q", h=4), psYo.tk))
        k.red(gst[:, 0:4], yv[:])
        k.tt("pool", ytmp[:], yv[:], yv[:], ALU.mult)
        k.red(gst[:, 4:8], ytmp[:])
        k.ts("dve", gst[:, 0:8], gst[:, 0:8], 1.0 / 64, ALU.mult)
        k.tt("dve", gst[:, 8:12], gst[:, 0:4], gst[:, 0:4], ALU.mult)
        k.tt("dve", gst[:, 8:12], gst[:, 4:8], gst[:, 8:12], ALU.subtract)
        k.ts("pool", gst[:, 8:12], gst[:, 8:12], 1.0, ALU.mult, 64e-5, ALU.add)
        k.tt("pool", gst[:, 12:16], gst[:, 8:12], mhalf[:, 0:4], ALU.pow)
        for hh in range(4):
            k.ts("dve", ytmp[:, hh, :], yv[:, hh, :], gst[:, hh:hh + 1], ALU.subtract, gst[:, 12 + hh:13 + hh], ALU.mult)
        yt2 = View(ytmp.t[:, :, :].rearrange("p h q -> p (h q)"), ytmp.tk)
        k.tt("pool", yt2, yt2, lnw_t[:, 0, :], ALU.mult)
        k.tt("pool", yt2, yt2, lnw_t[:, 1, :], ALU.add)
        for hh, (c2, e) in enumerate(heads):
            k.stt(ytmp[:, hh, :], TM[:, 3, c2, 64 * e:64 * e + 64], bsum[:, hh:hh + 1], ytmp[:, hh, :], ALU.mult, ALU.add)
        k.tt("dve", mix[:, 0:256], yt2, gtm[:], ALU.mult)

    def outproj_tile(j, mix):
        psT = ps_rot.next()
        psT16 = psT.same(BF16)
        k.pe([("T", psT16[:, c * 128:(c + 1) * 128], mix[:, c * 128:(c + 1) * 128], identb[:]) for c in range(4)])
        mixT = mixT_r.next()
        k.cp("act", mixT[:], View(psT.t.ap().bitcast(BF16)[:, 0:512].rearrange("p (a b) -> p a b", a=4), psT.tk))
        po, spo = po_r.next()
        for br in range(2):
            for half in range(2):
                ps = ps_rot.next()
                k.pe([(ps[:, :], mixT[:, 2 * br + kc, :], wob[:, 2 * br + kc, half * 512:(half + 1) * 512], kc == 0, kc == 1) for kc in range(2)])
                k.cp("act" if half == 0 else "dve", po[:, br * 1024 + half * 512:br * 1024 + (half + 1) * 512], ps[:, :])
        i = j - 1
        k.dma("sp", View(rs_in[i // 16].t[(i % 16) * 128:(i % 16 + 1) * 128, :], rs_tk[i // 16][i % 16]), po[:], spo)
        if dbg:
            dm, sdm = dmix_r.next()
            k.cp("pool", dm[:], mix[:])
            k.dma("sp", d_mix[j * 128:(j + 1) * 128, :], dm[:], sdm)

    cc_sem = es.enter_context(nc.semaphore("cc_sem"))
    k.engs["cc"] = EngS("cc", None, cc_sem)
    n_cc = 0
    def issue_rs(j):
        nonlocal n_cc
        if do_rs and j >= 1 and j % 16 == 0:
            q = j // 16 - 1
            E = k.engs["pool"]
            k._wait(E, rs_tk[q], [rs_out[q].tk])
            nc.gpsimd.collective_compute("ReduceScatter", ALU.add, replica_groups=[[0, 1, 2, 3], [4, 5, 6, 7]],
                                         ins=[rs_in[q].t.ap().opt()], outs=[rs_out[q].t.ap().opt()]).then_inc(cc_sem)
            n_cc += 1
            for t_ in rs_tk[q]:
                t_.rd["cc"] = n_cc
            rs_out[q].tk.lw = ("cc", n_cc)
            rs_out[q].tk.rd = {}

    tiles = [{"j": j} for j in range(NT1)]

    def P(t):
        t["sh"], t["QT"] = p1_tile(t["j"])

    def R(t):
        t["mix"] = mix_r.next()
        rwkv_tile(t["j"], t["sh"], t["mix"])

    def Y(t):
        attn_tile(t["j"], t["QT"], t["mix"])
        outproj_tile(t["j"], t["mix"])

    lenP, lenR = 40, 230
    if NT1 > 0:
        n0 = k.nops
        P(tiles[0])
        lenP = k.nops - n0
    for j in range(NT1):
        fns, ests = [], []
        fns.append(lambda t=tiles[j]: R(t))
        ests.append(lenR)
        if j >= 2:
            jj = j - 1
            fns.append(lambda t=tiles[jj]: Y(t))
            ests.append(4 * (3 * ((jj + 4) // 4) + 2) + 40)
        else:
            fns.append(lambda: None)
            ests.append(1)
        if j + 1 < NT1:
            fns.append(lambda t=tiles[j + 1]: P(t))
            ests.append(lenP)
        done = k.weave.run(fns, ests)
        lenR = max(1, done[0])
        if j >= 2:
            issue_rs(j - 1)
    if NT1 >= 2:
        Y(tiles[NT1 - 1])
        issue_rs(NT1 - 1)

    k.barrier()
    es1.close()
    if do_p2:
        src = rs_dbg if rs_dbg is not None else rs_out
        esA = ExitStack()
        k.es = es
        h2 = k.sb([128, 16, D], F32, "h2")
        k.es = esA
        wgb = k.sb([128, 8, 2048], BF16, "wgb")
        woutb = k.sb([128, 8, D], BF16, "woutb")
        stgA = Rot([(k.sb([128, D], F32, f"stgA{i}"), k.dsem()) for i in range(4)])
        for kc in range(8):
            for hf in range(2):
                k.ldcast(wgb[:, kc, hf * D:(hf + 1) * D], wg[kc * 128:(kc + 1) * 128, hf * D:(hf + 1) * D], stgA, "act" if hf == 0 else "dve")
        for kc in range(8):
            k.ldcast(woutb[:, kc, :], wout[kc * 128:(kc + 1) * 128, :], stgA, "act" if kc % 2 == 0 else "dve")
        nw2 = k.sb([128, D], F32, "nw2")
        k.dma("sp", nw2[:], nwb[:, :], k.dsem())
        x2_r = Rot([(k.sb([128, 4, D], F32, f"x2t{i}"), k.dsem()) for i in range(1)])
        rs_r = Rot([(k.sb([128, 4, 2048], BF16, f"rst{i}"), k.dsem()) for i in range(1)])
        st2 = k.sb([128, 8], F32, "st2")
        xs2 = k.sb([128, D], BF16, "xs2")
        hT2 = k.sb([128, 8, 512], BF16, "hT2")
        gate = k.sb([128, 2048], BF16, "gate")
        mrg = k.sb([128, D], BF16, "mrg")
        mrg2 = k.sb([128, D], BF16, "mrg2")
        mT = k.sb([128, 8, 128], BF16, "mT")
        gateB = k.sb([128, 2048], BF16, "gateB")
        mrgB = k.sb([128, D], BF16, "mrgB")
        mrg2B = k.sb([128, D], BF16, "mrg2B")
        mTB = k.sb([128, 8, 128], BF16, "mTB")

        def norm_T(xin, nwt, xs_, st_, dstT, col0):
            k.act(xs_[:], xin, AF.Square, accum=st_[:, 0:1])
            k.act(st_[:, 1:2], st_[:, 0:1], AF.Sqrt, bias=1e-5, scale=1.0 / D)
            k.recip(st_[:, 2:3], st_[:, 1:2])
            k.stt(xs_[:], xin, st_[:, 2:3], nwt, ALU.mult, ALU.mult)
            ps = ps_rot.next()
            ps16 = ps.same(BF16)
            k.pe([("T", ps16[:, kc * 128:(kc + 1) * 128], xs_[:, kc * 128:(kc + 1) * 128], identb[:]) for kc in range(8)])
            k.cp("act", View(dstT.t[:, :, col0:col0 + 128], dstT.tk),
                 View(ps.t.ap().bitcast(BF16)[:, 0:1024].rearrange("p (a b) -> p a b", a=8), ps.tk))

        for kq in range(4):
            x2t, sx2 = x2_r.next()
            k.dma("sp", x2t[:], View(x2.t[kq].rearrange("(s p) d -> p s d", p=128), x2.tk), sx2)
            rst, srs = rs_r.next()
            k.dma("sp", rst[:], View(src[kq].t.ap().rearrange("(s p) d -> p s d", p=128), src[kq].tk), srs)
            for s_ in range(4):
                norm_T(x2t[:, s_, :], nw2[:], xs2, st2, hT2, s_ * 128)
            def sub2a(s_, gate, mrg, mrg2, mT, kq=kq, x2t=x2t, rst=rst):
                for cb in range(4):
                    ps = ps_rot.next()
                    k.pe([(ps[:, :], hT2[:, kc, s_ * 128:(s_ + 1) * 128], wgb[:, kc, cb * 512:(cb + 1) * 512], kc == 0, kc == 7) for kc in range(8)])
                    k.act(gate[:, cb * 512:(cb + 1) * 512], ps[:, :], AF.Sigmoid)
                k.tt("pool", mrg[:], gate[:, 0:D], rst[:, s_, 0:D], ALU.mult)
                k.tt("dve", mrg2[:], gate[:, D:2 * D], rst[:, s_, D:2 * D], ALU.mult)
                k.tt("pool", mrg[:], mrg[:], mrg2[:], ALU.add)
                ps = ps_rot.next()
                ps16 = ps.same(BF16)
                k.pe([("T", ps16[:, kc * 128:(kc + 1) * 128], mrg[:, kc * 128:(kc + 1) * 128], identb[:]) for kc in range(8)])
                k.cp("act", mT[:], View(ps.t.ap().bitcast(BF16)[:, 0:1024].rearrange("p (a b) -> p a b", a=8), ps.tk))
                for cb in range(2):
                    ps = ps_rot.next()
                    k.pe([(ps[:, :], mT[:, kc, :], woutb[:, kc, cb * 512:(cb + 1) * 512], kc == 0, kc == 7) for kc in range(8)])
                    k.tt("dve", h2[:, kq * 4 + s_, cb * 512:(cb + 1) * 512], ps[:, :], x2t[:, s_, cb * 512:(cb + 1) * 512], ALU.add)

            for s0 in (0, 2):
                k.weave.run([lambda: sub2a(s0, gate, mrg, mrg2, mT), lambda: sub2a(s0 + 1, gateB, mrgB, mrg2B, mTB)], [1, 1])
        k.barrier()
        esA.close()
        esB = ExitStack()
        k.es = esB
        nmt = k.sb([128, 2, D], F32, "nmt")
        k.dma("sp", nmt[:], nm2[:, :, :], k.dsem())
        hnT = k.sb([128, 8, 2048], BF16, "hnT")
        xs3 = k.sb([128, D], BF16, "xs3")
        st3 = k.sb([128, 8], F32, "st3")
        esC = ExitStack()
        k.es = esC
        wq_r = Rot([(k.sb([128, 8, D], BF16, f"w1q{i}"), k.sb([128, 8, D], BF16, f"w2q{i}")) for i in range(2)])
        aT_r = rot(2, [128, 8, 512], BF16, "aT")
        rl_r = rot(2, [128, 512], BF16, "rl")
        stgB = Rot([(k.sb([128, D], F32, f"stgB{i}"), k.dsem()) for i in range(4)])

        def load_q(p):
            w1q, w2q = wq_r.next()
            for kc in range(8):
                k.ldcast(w1q[:, kc, :], mw1[kc * 128:(kc + 1) * 128, p * D:(p + 1) * D], stgB, "pool" if p > 0 else ("act", "dve")[kc % 2])
            for fc in range(8):
                k.ldcast(w2q[:, fc, :], mw2[p * D + fc * 128:p * D + (fc + 1) * 128, :], stgB, "pool" if p > 0 else ("act", "dve")[fc % 2])
            return w1q, w2q

        nxt = load_q(0)
        for t16 in range(16):
            norm_T(h2[:, t16, :], nmt[:, 0, :], xs3, st3, hnT, t16 * 128)
        for p in range(4):
            w1q, w2q = nxt
            if p < 3:
                nxt = load_q(p + 1)
            def up(kq, aT, w1q=w1q):
                for fc in range(8):
                    ps = ps_rot.next()
                    k.pe([(ps[:, :], w1q[:, kc, fc * 128:(fc + 1) * 128], hnT[:, kc, kq * 512:(kq + 1) * 512], kc == 0, kc == 7) for kc in range(8)])
                    rl = rl_r.next()
                    k.act(rl[:], ps[:, :], AF.Relu)
                    k.tt("pool", aT[:, fc, :], rl[:], rl[:], ALU.mult)

            def down(kq, aT, w2q=w2q):
                for s_ in range(4):
                    for cb in range(2):
                        ps = ps_rot.next()
                        k.pe([(ps[:, :], aT[:, fc, s_ * 128:(s_ + 1) * 128], w2q[:, fc, cb * 512:(cb + 1) * 512], fc == 0, fc == 7) for fc in range(8)])
                        hv = h2[:, kq * 4 + s_, cb * 512:(cb + 1) * 512]
                        k.tt("dve", hv, ps[:, :], hv, ALU.add)

            aTs = [aT_r.next() for _ in range(4)]
            up(0, aTs[0])
            for kq in range(4):
                if kq < 3:
                    k.weave.run([lambda: down(kq, aTs[kq]), lambda: up(kq + 1, aTs[kq + 1])], [1, 1])
                else:
                    down(kq, aTs[kq])
        k.barrier()
        esC.close()
        k.es = esB
        yo_r = Rot([(k.sb([128, D], F32, f"yo{i}"), k.dsem()) for i in range(2)])
        for t16 in range(16):
            yo, syo = yo_r.next()
            hv = h2[:, t16, :]
            k.act(yo[:], hv, AF.Square, accum=st3[:, 0:1])
            k.act(st3[:, 1:2], st3[:, 0:1], AF.Sqrt, bias=1e-5, scale=1.0 / D)
            k.recip(st3[:, 2:3], st3[:, 1:2])
            k.stt(yo[:], hv, st3[:, 2:3], nmt[:, 1, :], ALU.mult, ALU.mult)
            k.dma("sp", yout[t16 // 4, (t16 % 4) * 128:(t16 % 4 + 1) * 128, :], yo[:], syo)
        k.barrier()
        esB.close()
    k.barrier()
    print("total ops", k.nops, {n: e.count for n, e in k.engs.items() if e.count})
    return nc


def make_consts():
    c = np.zeros((128, 1024), np.float32)
    p = np.arange(128)[:, None]
    q = np.arange(128)[None, :]
    c[:, 0:128] = (p == q)
    c[:, 128:256] = (p <= q)
    c[:, 256:384] = (p // 64 == q // 64)
    pm = np.zeros((128, 128), np.float32)
    for hb in (0, 64):
        for d in range(8):
            pm[hb + d + 8, hb + d] = -1.0
            pm[hb + d, hb + d + 8] = 1.0
    c[:, 384:512] = pm
    c[:, 512:576] = (p % 64 == np.arange(64)[None, :])
    s = (np.arange(128) % 64)[:, None]
    t = np.arange(64)[None, :]
    c[:, 576:640] = (s < t)
    c[:, 640:704] = (s <= t)
    c[:, 704:768] = (s < t)
    c[:, 768:832] = (s <= t)
    c[:, 832:896] = (s > t)
    c[:, 896:1024] = 1.0
    c[:, 896] = 0.0
    c[:, 960] = 0.0
    return c


def make_consts2():
    c = np.zeros((128, 1024), np.float32)
    p = np.arange(128)[:, None]
    q = np.arange(128)[None, :]
    same = (p // 64 == q // 64)
    strict = same & ((p % 64) < (q % 64))
    incl = same & ((p % 64) <= (q % 64))
    lo = same & ((p % 64) > (q % 64))
    for e in range(2):
        c[:, e * 256:e * 256 + 128] = strict
        c[:, e * 256 + 128:e * 256 + 256] = incl
    for h in range(4):
        c[:, 512 + h * 128:512 + (h + 1) * 128] = lo
    return c


def make_rope():
    half = 8
    inv = (np.float32(500000.0) ** (-(np.arange(0, 16, 2, dtype=np.float32)) / np.float32(16))).astype(np.float32)
    pos = np.maximum(np.arange(LP) - 112, 0).astype(np.float32)
    ang = pos[:, None] * inv[None, :]
    cos = np.cos(ang).astype(np.float32)
    sin = np.sin(ang).astype(np.float32)
    ct = np.ones((128, LP), np.float32)
    stb = np.zeros((128, LP), np.float32)
    for hb in (0, 64):
        for d in range(16):
            ct[hb + d] = cos[:, d % 8]
            stb[hb + d] = sin[:, d % 8]
    r = np.stack([ct.reshape(128, NT, 128), stb.reshape(128, NT, 128)], axis=2)
    return np.ascontiguousarray(r)


def core_inputs(inp, c, consts, rope):
    b, g = c // 4, c % 4
    f = lambda a: np.ascontiguousarray(np.asarray(a, dtype=np.float32))
    w_in = inp["w_in"][0]
    sl = slice(256 * g, 256 * g + 256)
    cols = np.concatenate([np.arange(256 * g, 256 * g + 256), 1024 + np.arange(256 * g, 256 * g + 256),
                           2048 + np.arange(256 * g, 256 * g + 256), np.arange(3072, 3360),
                           3360 + np.arange(256 * g, 256 * g + 256), 4384 + np.arange(256 * g, 256 * g + 256),
                           5408 + np.arange(256 * g, 256 * g + 256)])
    mu = inp["rwkv_mu"][0]
    pv = np.zeros((128, 32), np.float32)
    mucols = [mu[0 + 256 * g:0 + 256 * g + 128], mu[256 * g + 128:256 * g + 256],
              mu[1024 + 256 * g:1024 + 256 * g + 128], mu[1024 + 256 * g + 128:1024 + 256 * g + 256],
              mu[2048 + 256 * g:2048 + 256 * g + 128], mu[2048 + 256 * g + 128:2048 + 256 * g + 256],
              mu[3072:3200], mu[3200:3328]]
    for i, m in enumerate(mucols):
        pv[:, i] = m
    pv[0:32, 8] = mu[3328:3360]
    for i, nm in enumerate(["rwkv_w0", "rwkv_a0", "rwkv_k_k", "rwkv_k_a"]):
        vv = inp[nm][0][sl]
        pv[:, 9 + 2 * i] = vv[0:128]
        pv[:, 10 + 2 * i] = vv[128:256]
    rkv = inp["rwkv_r_k"][0].reshape(-1)[sl]
    pv[:, 17] = rkv[0:128]
    pv[:, 18] = rkv[128:256]
    q = g
    x2 = np.stack([inp["x"][b, 2048 * kk + 512 * q:2048 * kk + 512 * q + 512] for kk in range(4)], 0)
    d = {
        "xb": f(inp["x"][b]),
        "meta": f(inp["meta_tokens"]),
        "x2": f(x2),
        "w1c": f(w_in[:, cols]),
        "wg": f(w_in[:, 6432:8480]),
        "pv": pv,
        "nwb": f(np.broadcast_to(inp["norm_mix_w"][0][None, :], (128, D))),
        "lnwb": f(np.broadcast_to(np.stack([inp["rwkv_ln_w"][0][sl], inp["rwkv_ln_b"][0][sl]], 0)[None], (128, 2, 256))),
        "sublnb": f(np.broadcast_to(inp["diff_subln_w"][0][None, :], (128, 128))),
        "lamv": f(np.broadcast_to(np.stack([inp["diff_lq1"][0], inp["diff_lk1"][0], inp["diff_lq2"][0], inp["diff_lk2"][0]], 0)[None], (128, 4, 64))),
        "w2a2": f(np.concatenate([inp["rwkv_w2"][0][:, sl], inp["rwkv_a2"][0][:, sl]], 0)),
        "g2": f(inp["rwkv_g2"][0][:, sl]),
        "wor": f(inp["rwkv_w_o"][0][sl, :]),
        "wod": f(inp["diff_w_o"][0][sl, :]),
        "wout": f(inp["w_out"][0]),
        "mw1": f(inp["mlp_w1"][0]),
        "mw2": f(inp["mlp_w2"][0]),
        "nm2": f(np.broadcast_to(np.stack([inp["norm_mlp_w"][0], inp["final_norm_w"]], 0)[None], (128, 2, D))),
        "rope": rope,
        "cst": consts,
        "cst2": make_consts2(),
    }
    return d


_CACHE = {}


def kernel(**inputs):
    inp = {kk: np.asarray(v) for kk, v in inputs.items()}
    if "nc" not in _CACHE:
        _CACHE["nc"] = build({})
    nc = _CACHE["nc"]
    consts = make_consts()
    rope = make_rope()
    in_maps = [core_inputs(inp, c, consts, rope) for c in range(8)]
    res = run_bass_kernel_spmd(nc, in_maps, core_ids=list(range(8)))
    out = np.zeros((2, 8192, D), np.float32)
    for c in range(8):
        b, q = c // 4, c % 4
        y = np.asarray(res.results[c]["yout"])
        for kk in range(4):
            out[b, 2048 * kk + 512 * q:2048 * kk + 512 * q + 512] = y[kk]
    return out
```
